# Optimizing a Trainium2 kernel written in Bass

```python
import jax, jax.numpy as jnp
from jax import lax
import numpy as np

D_MODEL = 1024
BATCH = 32
SEQ = 2048
DEPTH = 2
DEC_BATCH = 4
DEC_SEQ = 8192
PAST_LEN = 128

W_A = 384
W_B = 384
W_C = 384
W_D = 384
N_BRANCH = 4
CONV_A_WIDTH = 31
GMLP_CHUNK = 128
GMLP_HEADS = 4
GMLP_HEAD_DIM = W_B // GMLP_HEADS
FNET_GROUPS = 4
FNET_GROUP_DIM = W_C // FNET_GROUPS
POOL_WINDOWS = (2, 4, 8, 16)
POOL_GROUP_DIM = W_D // len(POOL_WINDOWS)
D_FF = 2816
FFN_CONV_WIDTH = 3
EPS = 1e-6
OFF_A = 0
OFF_B = OFF_A + 2 * W_A
OFF_C = OFF_B + 2 * W_B
OFF_D = OFF_C + W_C
D_IN = OFF_D + W_D

kernel_name = "hybrid_bidir_gated_branch_encoder"


def rms_norm(x, g):
    xf = x.astype(jnp.float32)
    y = xf * lax.rsqrt(jnp.mean(xf * xf, axis=-1, keepdims=True) + EPS)
    return (y * g.astype(jnp.float32)).astype(x.dtype)


def layer_norm(x, g, b):
    xf = x.astype(jnp.float32)
    mu = jnp.mean(xf, axis=-1, keepdims=True)
    var = jnp.mean(jnp.square(xf - mu), axis=-1, keepdims=True)
    y = (xf - mu) * lax.rsqrt(var + EPS)
    return (y * g.astype(jnp.float32) + b.astype(jnp.float32)).astype(x.dtype)


def depthwise_conv(x, w, b):
    k = w.shape[0]
    pad = k // 2
    y = lax.conv_general_dilated(
        x, w[:, None, :].astype(x.dtype), window_strides=(1,), padding=[(pad, pad)],
        dimension_numbers=("NWC", "WIO", "NWC"), feature_group_count=x.shape[-1])
    return y + b.astype(x.dtype)


def mixer_a(z, conv_w, conv_b, ln_g, ln_b, w_proj):
    a, gate = z[..., :W_A], z[..., W_A:]
    c = depthwise_conv(a * jax.nn.sigmoid(gate), conv_w, conv_b)
    c = jax.nn.silu(layer_norm(c, ln_g, ln_b))
    return c @ w_proj


def mixer_b(z, ln_g, ln_b, w_s, b_s, w_proj):
    bsz, s, _ = z.shape
    z = jax.nn.gelu(z)
    u, v = z[..., :W_B], z[..., W_B:]
    v = layer_norm(v, ln_g, ln_b)
    v = v.reshape(bsz, s // GMLP_CHUNK, GMLP_CHUNK, GMLP_HEADS, GMLP_HEAD_DIM)
    sv = jnp.einsum("hpq,bnqhc->bnphc", w_s.astype(v.dtype), v) + b_s.T[:, :, None].astype(v.dtype)
    sv = sv.reshape(bsz, s, W_B)
    return (u * sv) @ w_proj


def mixer_c(z, w_proj):
    bsz, s, _ = z.shape
    zg = z.reshape(bsz, s, FNET_GROUPS, FNET_GROUP_DIM).astype(jnp.float32)
    f = jnp.fft.fftn(zg, axes=(1, 3), norm="ortho").real
    f = f.reshape(bsz, s, W_C).astype(z.dtype)
    return f @ w_proj


def mixer_d(z, w_group, scale, w_proj):
    bsz, s, _ = z.shape
    zf = z.astype(jnp.float32)
    csum = jnp.concatenate([jnp.zeros((bsz, 1, W_D), jnp.float32), jnp.cumsum(zf, axis=1)], axis=1)
    t = np.arange(s)
    pooled = []
    for k, w in enumerate(POOL_WINDOWS):
        lo, hi = -(w // 2), w // 2 - 1
        start = np.clip(t + lo, 0, s)
        end = np.clip(t + hi + 1, 0, s)
        cnt = jnp.asarray((end - start).astype(np.float32))[None, :, None]
        ck = csum[..., k * POOL_GROUP_DIM:(k + 1) * POOL_GROUP_DIM]
        seg = jnp.take(ck, jnp.asarray(end), axis=1) - jnp.take(ck, jnp.asarray(start), axis=1)
        pooled.append(seg / cnt)
    pooled = jnp.stack(pooled, axis=2)
    diff = (pooled - zf.reshape(bsz, s, len(POOL_WINDOWS), POOL_GROUP_DIM)).astype(z.dtype)
    y = jnp.einsum("bsgc,gcd->bsgd", diff, w_group.astype(z.dtype)).reshape(bsz, s, W_D)
    return (y * scale.astype(z.dtype)) @ w_proj


def conv_ffn(h, w_up, conv_w, conv_b, w_down):
    up = depthwise_conv(h @ w_up, conv_w, conv_b)
    g, v = up[..., :D_FF], up[..., D_FF:]
    return (jax.nn.gelu(g) * v) @ w_down


def trunk(x, norm1_g, w_in, a_conv_w, a_conv_b, a_ln_g, a_ln_b, a_proj,
          b_ln_g, b_ln_b, b_ws, b_bs, b_proj, c_proj, d_wg, d_scale, d_proj,
          w_gate, b_gate, w_out, norm2_g, f_up, f_conv_w, f_conv_b, f_down, final_g):
    for l in range(DEPTH):
        h = rms_norm(x, norm1_g[l])
        z = h @ w_in[l]
        branches = (
            mixer_a(z[..., OFF_A:OFF_B], a_conv_w[l], a_conv_b[l], a_ln_g[l], a_ln_b[l], a_proj[l]),
            mixer_b(z[..., OFF_B:OFF_C], b_ln_g[l], b_ln_b[l], b_ws[l], b_bs[l], b_proj[l]),
            mixer_c(z[..., OFF_C:OFF_D], c_proj[l]),
            mixer_d(z[..., OFF_D:D_IN], d_wg[l], d_scale[l], d_proj[l]),
        )
        merged = jnp.zeros_like(x)
        for i in range(N_BRANCH):
            merged = merged + jax.nn.sigmoid(h @ w_gate[l, i] + b_gate[l, i]) * branches[i]
        x = x + merged @ w_out[l]
        x = x + conv_ffn(rms_norm(x, norm2_g[l]), f_up[l], f_conv_w[l], f_conv_b[l], f_down[l])
    return rms_norm(x, final_g)


def setup_inputs(seed: int = 0) -> dict:
    key = jax.random.key(seed)
    ks = jax.random.split(key, 32)
    L, D = DEPTH, D_MODEL

    def nrm(k, shape, scale):
        return jax.random.normal(k, shape, jnp.float32) * scale

    return {
        "x_prompt": nrm(ks[0], (BATCH, SEQ, D), 1.0),
        "x_sample": nrm(ks[1], (DEC_BATCH, DEC_SEQ, D), 1.0),
        "norm1_g": 1.0 + nrm(ks[2], (L, D), 0.05),
        "w_in": nrm(ks[3], (L, D, D_IN), D ** -0.5),
        "a_conv_w": nrm(ks[4], (L, CONV_A_WIDTH, W_A), CONV_A_WIDTH ** -0.5),
        "a_conv_b": nrm(ks[5], (L, W_A), 0.02),
        "a_ln_g": 1.0 + nrm(ks[6], (L, W_A), 0.05),
        "a_ln_b": nrm(ks[7], (L, W_A), 0.02),
        "a_proj": nrm(ks[8], (L, W_A, D), W_A ** -0.5),
        "b_ln_g": 1.0 + nrm(ks[9], (L, W_B), 0.05),
        "b_ln_b": nrm(ks[10], (L, W_B), 0.02),
        "b_ws": nrm(ks[11], (L, GMLP_HEADS, GMLP_CHUNK, GMLP_CHUNK), GMLP_CHUNK ** -0.5),
        "b_bs": 1.0 + nrm(ks[12], (L, GMLP_HEADS, GMLP_CHUNK), 0.1),
        "b_proj": nrm(ks[13], (L, W_B, D), W_B ** -0.5),
        "c_proj": nrm(ks[14], (L, W_C, D), W_C ** -0.5),
        "d_wg": nrm(ks[15], (L, len(POOL_WINDOWS), POOL_GROUP_DIM, POOL_GROUP_DIM), POOL_GROUP_DIM ** -0.5),
        "d_scale": 1.0 + nrm(ks[16], (L, W_D), 0.1),
        "d_proj": nrm(ks[17], (L, W_D, D), W_D ** -0.5),
        "w_gate": nrm(ks[18], (L, N_BRANCH, D, D), D ** -0.5),
        "b_gate": nrm(ks[19], (L, N_BRANCH, D), 0.1),
        "w_out": nrm(ks[20], (L, D, D), D ** -0.5),
        "norm2_g": 1.0 + nrm(ks[21], (L, D), 0.05),
        "f_up": nrm(ks[22], (L, D, 2 * D_FF), D ** -0.5),
        "f_conv_w": nrm(ks[23], (L, FFN_CONV_WIDTH, 2 * D_FF), FFN_CONV_WIDTH ** -0.5),
        "f_conv_b": nrm(ks[24], (L, 2 * D_FF), 0.02),
        "f_down": nrm(ks[25], (L, D_FF, D), D_FF ** -0.5),
        "final_g": 1.0 + nrm(ks[26], (D,), 0.05),
    }


def reference(x_prompt, x_sample, norm1_g, w_in, a_conv_w, a_conv_b, a_ln_g, a_ln_b, a_proj,
              b_ln_g, b_ln_b, b_ws, b_bs, b_proj, c_proj, d_wg, d_scale, d_proj,
              w_gate, b_gate, w_out, norm2_g, f_up, f_conv_w, f_conv_b, f_down, final_g):
    y_prompt = trunk(x_prompt, norm1_g, w_in, a_conv_w, a_conv_b, a_ln_g, a_ln_b, a_proj,
                     b_ln_g, b_ln_b, b_ws, b_bs, b_proj, c_proj, d_wg, d_scale, d_proj,
                     w_gate, b_gate, w_out, norm2_g, f_up, f_conv_w, f_conv_b, f_down, final_g)
    y_sample = trunk(x_sample, norm1_g, w_in, a_conv_w, a_conv_b, a_ln_g, a_ln_b, a_proj,
                     b_ln_g, b_ln_b, b_ws, b_bs, b_proj, c_proj, d_wg, d_scale, d_proj,
                     w_gate, b_gate, w_out, norm2_g, f_up, f_conv_w, f_conv_b, f_down, final_g)
    return (y_prompt, y_sample)
```

```python
import numpy as np
import ml_dtypes
from contextlib import ExitStack
import concourse.bass as bass
import concourse.mybir as mybir
from concourse.bass_utils import run_bass_kernel_spmd

F32 = mybir.dt.float32
BF = mybir.dt.bfloat16
AF = mybir.ActivationFunctionType
ALU = mybir.AluOpType

D = 1024
DIN = 2304
DFF = 2816
NMT = DFF // 128
T = 512
HL = 16
EPS = 1e-6
OFF_A, OFF_B, OFF_C, OFF_D = 0, 768, 1536, 1920
POOLW = (2, 4, 8, 16)
ENG = ("pe", "act", "dve", "pool", "sp")

PP_CW = 0
PP_CB = PP_CW + 93
PP_LG = PP_CB + 3
PP_LB = PP_LG + 3
PP_BG = PP_LB + 3
PP_DS = PP_BG + 32
PP_FW = PP_DS + 4
PP_FB = PP_FW + 132
NPP = PP_FB + 44


class Buf:
    __slots__ = ("name", "w", "r")

    def __init__(self, name):
        self.name = name
        self.w = None
        self.r = {}


class Sched:
    def __init__(self, nc, es):
        self.nc = nc
        self.h = {}
        for e in ENG:
            self.h[e] = es.enter_context(nc.semaphore("s_" + e))
        self.cnt = {e: 0 for e in ENG}
        self.dcnt = {}
        self.known = {e: {} for e in ENG}
        self.ops = {e: [] for e in ENG}
        self.es = es

    def dsem(self, name):
        if name not in self.h:
            self.h[name] = self.es.enter_context(self.nc.semaphore("d_" + name))
            self.dcnt[name] = 0
        return name

    def _collect(self, eng, reads, writes):
        waits = {}

        def need(dep, kind):
            if dep is None:
                return
            key, val = dep
            if key == eng:
                if eng == "pe" or eng == "sp":
                    return
                if kind == "war":
                    return
            if val > waits.get(key, 0):
                waits[key] = val

        for b in reads:
            need(b.w, "raw")
        for b in writes:
            need(b.w, "waw")
            for r in b.r.values():
                need(r, "war")
        kn = self.known[eng]
        final = []
        for k, v in waits.items():
            if kn.get(k, 0) < v:
                kn[k] = v
                final.append((k, v))
        return final

    def op(self, eng, fn, reads=(), writes=()):
        waits = self._collect(eng, reads, writes)
        self.cnt[eng] += 1
        me = (eng, self.cnt[eng])
        for b in reads:
            b.r[eng] = me
        for b in writes:
            b.w = me
            b.r = {}
        self.ops[eng].append((waits, fn, (eng, 1)))

    def dma(self, q, sem, fn, reads=(), writes=()):
        self.dsem(sem)
        waits = self._collect(q, reads, writes)
        self.dcnt[sem] += 16
        me = (sem, self.dcnt[sem])
        for b in reads:
            b.r[sem] = me
        for b in writes:
            b.w = me
            b.r = {}
        self.ops[q].append((waits, fn, (sem, 16)))

    def dma_group(self, q, sem, fns, reads=(), writes=()):
        self.dsem(sem)
        waits = self._collect(q, reads, writes)
        for idx, fn in enumerate(fns):
            self.dcnt[sem] += 16
            self.ops[q].append((waits if idx == 0 else [], fn, (sem, 16)))
        me = (sem, self.dcnt[sem])
        for b in reads:
            b.r[sem] = me
        for b in writes:
            b.w = me
            b.r = {}

    def emit(self):
        nc = self.nc
        fin = [(k, v) for k, v in self.dcnt.items() if self.known["sp"].get(k, 0) < v]
        for k, v in fin:
            self.known["sp"][k] = v
        self.ops["sp"].append((fin, None, None))
        with nc.Block() as block:
            decos = {"pe": block.tensor, "act": block.scalar, "dve": block.vector,
                     "pool": block.gpsimd, "sp": block.sync}
            for eng in ENG:
                ops = self.ops[eng]
                if not ops:
                    continue

                def body(e, ops=ops):
                    for waits, fn, inc in ops:
                        for k, v in waits:
                            e.wait_ge(self.h[k], v)
                        if fn is not None:
                            inst = fn(e)
                            inst.then_inc(self.h[inc[0]], inc[1])

                decos[eng](body)
        self.ops = {e: [] for e in ENG}
        for e in ENG:
            for e2 in ENG:
                self.known[e][e2] = self.cnt[e2]


class Ps:
    def __init__(self, es, nc, n, tag):
        self.t = [es.enter_context(nc.psum_tensor(f"ps_{tag}{i}", [128, 512], F32)) for i in range(n)]
        self.b = [Buf(f"ps{i}") for i in range(n)]
        self.i = 0

    def get(self):
        i = self.i
        self.i = (i + 1) % len(self.t)
        return self.t[i], self.b[i]


class FeQ:
    def __init__(self, tasks):
        self.q = tasks
        self.pos = [0] * len(tasks)

    def step(self, i, n=1):
        if i >= len(self.q):
            return
        for _ in range(n):
            if self.pos[i] < len(self.q[i]):
                self.q[i][self.pos[i]]()
                self.pos[i] += 1

    def flush(self, i):
        self.step(i, 99)


def build(cfg):
    SEQS = cfg["seqs"]
    NTOK = cfg["ntok"]
    L = cfg["L"]
    NTT = cfg["ntiles"]
    dbg = cfg.get("dbg")
    nc = bass.Bass("TRN2", target_bir_lowering=False)

    def din(name, shape, dt=F32):
        return nc.dram_tensor(name, list(shape), dt, kind="ExternalInput").ap()

    def dscr(name, shape, dt):
        return nc.dram_tensor(name, list(shape), dt, kind="Internal").ap()

    x_in = din("x", [NTOK, D])
    y_out = nc.dram_tensor("y", [NTOK, D], F32, kind="ExternalOutput").ap()
    W = {}
    wshapes = {
        "w_in": [L, D, DIN], "w_gate": [L, 4, D, D], "a_proj": [L, 384, D], "b_proj": [L, 384, D],
        "cpk": [L, 512, D], "d_proj": [L, 384, D], "w_out": [L, D, D], "f_up": [L, D, 2 * DFF],
        "f_down": [L, DFF, D], "wst": [L, 128, 4 * 128], "dwg": [L, 96, 4 * 96], "bsr": [L, 1, 512],
    }
    WB = {}
    for k, shp in wshapes.items():
        W[k] = din(k, shp)
        WB[k] = dscr(k + "_bf", shp, BF)
    pp_in = din("pp", [L, 128, NPP])
    g1_in = din("g1", [L, 1, D])
    g2_in = din("g2", [L, 1, D])
    gf_in = din("gf", [1, D])
    blg_in = din("blg", [L, 1, 384])
    blb_in = din("blb", [L, 1, 384])
    cs96_in = din("cs96", [96, 98], BF)
    ident_in = din("ident", [128, 128], BF)
    ones_in = din("ones", [128, 128])
    hmask_in = din("hmask", [32, NTT])
    invc_in = din("invc", [4, NTOK])
    dft_in = {}
    for (off, S_, dn, tb) in SEQS:
        if dn not in dft_in:
            dft_in[dn] = din("dft" + dn, [2, S_, S_], BF)
    x1_d = dscr("x1_scr", [NTOK, D], F32)
    x2_d = dscr("x2_scr", [NTOK, D], F32)
    F_d = dscr("F_scr", [4, 128, NTOK], BF)
    dbg_out = {}
    if dbg:
        for nm, shp in dbg.items():
            dbg_out[nm] = nc.dram_tensor("dbg_" + nm, list(shp), F32, kind="ExternalOutput").ap()

    with ExitStack() as ges:
        S = Sched(nc, ges)

        wcastb = Buf("wcast")
        for k, shp in wshapes.items():
            src = W[k]
            dst = WB[k]
            if len(shp) == 4:
                src = src.rearrange("l i r c -> (l i r) c")
                dst = dst.rearrange("l i r c -> (l i r) c")
            else:
                src = src.rearrange("l r c -> (l r) c")
                dst = dst.rearrange("l r c -> (l r) c")
            R = src.shape[0]
            r0 = 0
            semn = "wc0" if k == "w_in" else "wc"
            while r0 < R:
                rr = min(128, R - r0)
                S.dma("pool", semn, lambda e, s=src[r0:r0 + rr, :], d=dst[r0:r0 + rr, :]:
                      e.dma_start(out=d, in_=s), writes=[wcastb] if k == "w_in" and r0 + rr >= R else [])
                r0 += rr

        class Front:
            def __init__(self, es, tag, g_src, pp):
                sbt = lambda n, s, d: es.enter_context(nc.sbuf_tensor(f"{tag}_{n}", s, d))
                self.tag = tag
                self.xs = [sbt(f"xs{i}", [128, D], F32) for i in range(2)]
                self.xsb = [Buf("xs") for _ in range(2)]
                self.ss = [sbt(f"ss{i}", [128, 1], F32) for i in range(2)]
                self.rs = [sbt(f"rs{i}", [128, 1], F32) for i in range(2)]
                self.ssb = [Buf("ss") for _ in range(2)]
                self.hb = [sbt(f"hb{i}", [128, D], BF) for i in range(2)]
                self.hbb = [Buf("hb") for _ in range(2)]
                self.pp = pp
                self.negh = sbt("negh", [128, 1], F32)
                self.neghb = Buf("negh")
                S.op("pool", lambda e: e.memset(self.negh[:], -0.5), writes=[self.neghb])
                self.junk = sbt("junk", [128, D], BF)
                self.junkb = Buf("junk")
                self.gbc = sbt("gbc", [128, D], F32)
                self.gbcb = Buf("gbc")
                self.ident = sbt("ident", [128, 128], BF)
                self.identb = Buf("ident")
                self.hm = sbt("hm", [32, NTT], F32)
                self.hmb = Buf("hm")
                self.i = 0
                S.dma_group("sp", "par", [
                    lambda e: e.dma_start(out=self.gbc[:], in_=g_src.broadcast_to([128, D])),
                    lambda e: e.dma_start(out=self.ident[:], in_=ident_in[:, :]),
                    lambda e: e.dma_start(out=self.hm[:], in_=hmask_in[:, :])],
                    writes=[self.gbcb, self.identb, self.hmb])

            def new(self, srcs, dst, dstb, npart=128, zero=False, mask_col=None):
                s = self.i % 2
                self.i += 1
                return dict(s=s, srcs=srcs, dst=dst, dstb=dstb, P=npart, zero=zero, mask=mask_col)

            def load(self, c):
                s, P = c["s"], c["P"]
                xs, xsb = self.xs[s], self.xsb[s]
                if c["zero"]:
                    S.op("pool", lambda e: e.memset(xs[0:P, :], 0.0), writes=[xsb])
                if c["srcs"]:
                    S.dma_group("sp", f"fxs{s}",
                                [(lambda e, r0=r0, nr=nr, ap=ap: e.dma_start(out=xs[r0:r0 + nr, :], in_=ap))
                                 for (r0, nr, ap) in c["srcs"]], writes=[xsb])

            def norm_a(self, c):
                s, P = c["s"], c["P"]
                xs, xsb, ss, rs, ssb, hb, hbb = (self.xs[s], self.xsb[s], self.ss[s], self.rs[s], self.ssb[s],
                                                self.hb[s], self.hbb[s])
                S.op("act", lambda e: e.activation(out=self.junk[0:P, :], in_=xs[0:P, :], func=AF.Square,
                                                   accum_out=ss[0:P, :]),
                     reads=[xsb], writes=[self.junkb, ssb])
                S.op("dve", lambda e: e.tensor_scalar(out=rs[0:P, :], in0=ss[0:P, :], scalar1=1.0 / D, scalar2=EPS,
                                                      op0=ALU.mult, op1=ALU.add), reads=[ssb], writes=[ssb])

            def norm_b(self, c):
                s, P = c["s"], c["P"]
                xs, xsb, ss, rs, ssb, hb, hbb = (self.xs[s], self.xsb[s], self.ss[s], self.rs[s], self.ssb[s],
                                                self.hb[s], self.hbb[s])
                S.op("pool", lambda e: e.tensor_tensor(out=rs[0:P, :], in0=rs[0:P, :], in1=self.negh[0:P, :], op=ALU.pow),
                     reads=[ssb, self.neghb], writes=[ssb])
                if c["mask"] is not None:
                    mc = c["mask"]
                    S.op("dve", lambda e: e.tensor_tensor(out=rs[0:P, :], in0=rs[0:P, :],
                                                          in1=self.hm[0:P, mc:mc + 1], op=ALU.mult),
                         reads=[ssb, self.hmb], writes=[ssb])
                S.op("dve", lambda e: e.scalar_tensor_tensor(out=hb[0:P, :], in0=xs[0:P, :], scalar=rs[0:P, 0:1],
                                                             in1=self.gbc[0:P, :], op0=ALU.mult, op1=ALU.mult),
                     reads=[xsb, ssb, self.gbcb], writes=[hbb])

            def trans(self, c):
                s, P = c["s"], c["P"]
                hb, hbb = self.hb[s], self.hbb[s]
                dst, dstb = c["dst"], c["dstb"]
                ps, tpb = self.pp.get()
                tp = ps[:, :].bitcast(BF).rearrange("p (k t) -> p k t", k=8)

                def tr(e):
                    ins = None
                    for k in range(8):
                        ins = e.transpose(out=tp[:, k, 0:P], in_=hb[0:P, k * 128:(k + 1) * 128],
                                          identity=self.ident[0:P, 0:P])
                    return ins

                S.op("pe", tr, reads=[hbb, self.identb], writes=[tpb])
                S.op("act", lambda e: e.copy(out=dst, in_=tp[:, :, 0:P]), reads=[tpb], writes=[dstb])

            def tasks(self, chunks):
                n = len(chunks)

                def first():
                    for c in chunks[0:2]:
                        self.load(c)
                    self.norm_a(chunks[0])

                out = [first]
                for st in range(n + 1):
                    def t(st=st):
                        if 1 <= st <= n:
                            self.trans(chunks[st - 1])
                        if st < n:
                            self.norm_b(chunks[st])
                        if st + 2 < n:
                            self.load(chunks[st + 2])
                        if st + 1 < n:
                            self.norm_a(chunks[st + 1])
                    out.append(t)
                return out

        def load_piece(ring, ringb, slot, parts, sem):
            S.dma_group("sp", sem, [(lambda e, d_=d_, s_=s_: e.dma_start(out=d_, in_=s_)) for d_, s_ in parts],
                        writes=[ringb[slot]])

        def wview(name, l, rows0, nrows, c0, c1, i=None):
            w = WB[name]
            if i is not None:
                w2 = w[l, i]
            else:
                w2 = w[l]
            return w2[rows0:rows0 + nrows, c0:c1]

        uid = [0]

        def pass0(l, seq):
            uid[0] += 1
            U = f"u{uid[0]}"
            off, SL, dn, tb = seq
            NT = SL // T
            NCH = SL // 128
            xsrc = x_in if l == 0 else x2_d
            with ExitStack() as es:
                sbt = lambda n, s, d: es.enter_context(nc.sbuf_tensor(f"{U}p0_{n}", s, d))
                pp = Ps(es, nc, 8, U + "p0")
                fr = Front(es, U + "p0", g1_in[l], pp)
                hT = [sbt(f"hT{i}", [128, 8, T], BF) for i in range(2)]
                hTb = [Buf("hT") for _ in range(2)]
                WC = sbt("WC", [128, 8, 384], BF)
                WCb = Buf("WC")
                CS = sbt("CS", [96, 98], BF)
                CSb = Buf("CS")
                zc = sbt("zc", [96, 4, T], BF)
                zcb = [Buf("zc") for _ in range(4)]
                ZCS = sbt("ZCS", [128, NCH, 392], BF)
                ZCSb = [[Buf("zcs")] for _ in range(NCH)]
                Bsb = sbt("Bsb", [128, 2, T], F32)
                Bsbb = [Buf("Bsb") for _ in range(2)]
                ring = [sbt(f"ring{i}", [128, 8, 2, T], BF) for i in range(3)]
                ringb = [Buf("ring") for _ in range(3)]
                Fsb = [sbt(f"Fsb{i}", [128, 4, T], BF) for i in range(2)]
                Fsbb = [[Buf("Fsb") for _ in range(4)] for _ in range(2)]
                S.dma("sp", "par", lambda e: e.dma_start(
                    out=WC[:], in_=WB["w_in"][l][:, OFF_C:OFF_D].rearrange("(k p) c -> p k c", p=128)),
                    reads=[wcastb], writes=[WCb])
                S.dma("sp", "par", lambda e: e.dma_start(out=CS[:], in_=cs96_in[:, :]), writes=[CSb])
                def fe_tasks0(i):
                    s = i % 2
                    t0 = off + i * T
                    return fr.tasks([fr.new([(0, 128, xsrc[t0 + j * 128:t0 + (j + 1) * 128, :])],
                                            hT[s][:, :, j * 128:(j + 1) * 128], hTb[s]) for j in range(4)])

                feq = FeQ([fe_tasks0(i) for i in range(NT)])
                feq.flush(0)
                for i in range(NT):
                    s = i % 2
                    t0 = off + i * T
                    for g in range(4):
                        feq.step(i + 1)
                        ps, pb = pp.get()

                        def mm(e, ps=ps, g=g, s=s):
                            ins = None
                            for k in range(8):
                                ins = e.matmul(ps[0:96, :], lhsT=WC[:, k, g * 96:(g + 1) * 96], rhs=hT[s][:, k, :],
                                               start=(k == 0), stop=(k == 7))
                            return ins

                        S.op("pe", mm, reads=[hTb[s], WCb], writes=[pb])
                        S.op("act", lambda e, ps=ps, g=g: e.copy(out=zc[:, g, :], in_=ps[0:96, :]),
                             reads=[pb], writes=[zcb[g]])
                    for j in range(4):
                        feq.step(i + 1)
                        ch = i * 4 + j
                        psA, pbA = pp.get()

                        def mm2(e, psA=psA, j=j):
                            ins = None
                            for g in range(4):
                                ins = e.matmul(psA[:, g * 98:(g + 1) * 98],
                                               lhsT=zc[:, g, j * 128:(j + 1) * 128], rhs=CS[:, :],
                                               start=True, stop=True)
                            return ins

                        S.op("pe", mm2, reads=zcb + [CSb], writes=[pbA])
                        if j % 2 == 0:
                            S.op("dve", lambda e, psA=psA, ch=ch: e.tensor_copy(
                                out=ZCS[:, ch, :].rearrange("p (cs g c) -> p g cs c", cs=2, g=4),
                                in_=psA[:, 0:392].rearrange("p (g cs c) -> p g cs c", g=4, cs=2)),
                                 reads=[pbA], writes=[ZCSb[ch][0]])
                        else:
                            S.op("act", lambda e, psA=psA, ch=ch: e.copy(
                                out=ZCS[:, ch, :].rearrange("p (cs g c) -> p g cs c", cs=2, g=4),
                                in_=psA[:, 0:392].rearrange("p (g cs c) -> p g cs c", g=4, cs=2)),
                                 reads=[pbA], writes=[ZCSb[ch][0]])
                dft = dft_in[dn]
                NPC = NCH // 8
                pieces = [(i, pc) for i in range(NT) for pc in range(NPC)]

                def issue(idx):
                    i, pc = pieces[idx]
                    slot = idx % 3
                    parts = []
                    for cs in range(2):
                        parts.append((ring[slot][:, :, cs, :],
                                      dft[cs, pc * 1024:(pc + 1) * 1024, i * T:(i + 1) * T].rearrange(
                                          "(ch q) p -> q ch p", q=128)))
                    load_piece(ring, ringb, slot, parts, f"rg{slot}")

                done = set()
                nxt = {"n": 0}

                def pref():
                    while nxt["n"] < len(pieces) and (nxt["n"] < 3 or (nxt["n"] - 3) in done):
                        issue(nxt["n"])
                        nxt["n"] += 1

                Fps = None
                for idx, (i, pc) in enumerate(pieces):
                    pref()
                    slot = idx % 3
                    if pc == 0:
                        Fps = [pp.get() for _ in range(4)]

                    def mm3(e, Fps=Fps, pc=pc, slot=slot):
                        ins = None
                        for c8 in range(8):
                            ch = pc * 8 + c8
                            for cs in range(2):
                                for g in range(2):
                                    msz = 128 if g == 0 else 68
                                    ins = e.matmul(Fps[cs * 2 + g][0][0:msz, :],
                                                   lhsT=ZCS[:, ch, cs * 196 + g * 128:cs * 196 + g * 128 + msz],
                                                   rhs=ring[slot][:, c8, cs, :],
                                                   start=(ch == 0), stop=(ch == NCH - 1))
                        return ins

                    S.op("pe", mm3, reads=[ringb[slot]] + [b_ for cb_ in ZCSb[pc * 8:(pc + 1) * 8] for b_ in cb_],
                         writes=[p[1] for p in Fps])
                    done.add(idx)
                    if pc == NPC - 1:
                        fs = i % 2
                        for g in range(2):
                            msz = 128 if g == 0 else 68
                            S.op("act", lambda e, g=g, Fps=Fps, msz=msz: e.copy(out=Bsb[0:msz, g, :], in_=Fps[2 + g][0][0:msz, :]),
                                 reads=[Fps[2 + g][1]], writes=[Bsbb[g]])
                            S.op("dve", lambda e, g=g, Fps=Fps, fs=fs, msz=msz: e.tensor_tensor(
                                out=Fsb[fs][0:msz, g, :], in0=Fps[g][0][0:msz, :], in1=Bsb[0:msz, g, :], op=ALU.add),
                                reads=[Fps[g][1], Bsbb[g]], writes=[Fsbb[fs][g]])
                            S.op("dve", lambda e, g=g, Fps=Fps, fs=fs, msz=msz: e.tensor_tensor(
                                out=Fsb[fs][0:msz, 2 + g, :], in0=Fps[g][0][0:msz, :], in1=Bsb[0:msz, g, :], op=ALU.subtract),
                                reads=[Fps[g][1], Bsbb[g]], writes=[Fsbb[fs][2 + g]])
                        t0 = off + i * T
                        S.dma("pool", f"fst{fs}", lambda e, fs=fs, t0=t0: e.dma_start(
                            out=F_d[:, :, t0:t0 + T].rearrange("g c t -> c g t"), in_=Fsb[fs][:, :, :]),
                            reads=Fsbb[fs])
                S.emit()

        def pass1(l, seq):
            uid[0] += 1
            U = f"u{uid[0]}"
            off, SL, dn, tb = seq
            NT = SL // T
            xsrc = x_in if l == 0 else x2_d
            with ExitStack() as es:
                sbt = lambda n, s, d: es.enter_context(nc.sbuf_tensor(f"{U}p1_{n}", s, d))
                pp = Ps(es, nc, 8, U + "p1")
                fr = Front(es, U + "p1", g1_in[l], pp)
                HW_ = 32 + T
                hT = [sbt(f"hT{i}", [128, 8, HW_], BF) for i in range(2)]
                hTb = [Buf("hT") for _ in range(2)]
                hThb = [Buf("hTh") for _ in range(2)]
                NSL = 5
                ring = [sbt(f"ring{i}", [128, 4096], BF) for i in range(NSL)]
                ringb = [Buf("ring") for _ in range(NSL)]
                PPt = sbt("pp", [128, NPP], F32)
                PPb = Buf("pp")
                ones = sbt("ones", [128, 128], F32)
                onesb = Buf("ones")
                WsT = sbt("WsT", [128, 4, 128], BF)
                dwg = sbt("dwg", [96, 4, 96], BF)
                bsr = sbt("bsr", [1, 512], BF)
                one1 = sbt("one1", [1, 96], BF)
                smallb = Buf("small")
                Gb = sbt("Gb", [128, 384], F32)
                Bb = sbt("Bb", [128, 384], F32)
                sg = [sbt(f"sg{i}", [128, T], F32) for i in range(2)]
                sgb = [Buf("sg") for _ in range(2)]
                sgh = sbt("sgh", [128, 32], F32)
                sghb = Buf("sgh")
                u = sbt("u", [128, 3, T + 32], BF)
                mg = sbt("mg", [128, 8, T], F32)
                mgb = [Buf("mg") for _ in range(8)]
                dg = sbt("dg", [128, 48, 128], BF)
                dgb = Buf("dg")
                ub_ = [Buf("u") for _ in range(3)]
                acc = sbt("acc", [128, 3, T], F32)
                accb = [Buf("acc") for _ in range(3)]
                lnm2_t = sbt("lnm2", [128, T], F32)
                lnrs_t = sbt("lnrs", [128, T], F32)
                lnm2 = lnm2_t[:, :]
                lnrs = lnrs_t[:, :]
                lnm2b = Buf("lnm2")
                lnrsb = Buf("lnrs")
                lnt = [sbt(f"lnt{i}", [128, T], F32) for i in range(2)]
                lntb = [Buf("lnt") for _ in range(2)]
                cbf = sbt("cbf", [128, 3, T], BF)
                cbfb = [Buf("cbf") for _ in range(3)]
                vg = [sbt(f"vg{i}", [128, 384], F32) for i in range(2)]
                vgb = [Buf("vg") for _ in range(2)]
                bst = [sbt(f"bst{i}", [128, 6], F32) for i in range(2)]
                bmv = [sbt(f"bmv{i}", [128, 2], F32) for i in range(2)]
                bstb = [Buf("bst") for _ in range(2)]
                vn = sbt("vn", [128, 384], F32)
                vnb_ = Buf("vn")
                vnb = sbt("vnb", [128, 4, 384], BF)
                vnbb = [Buf("vnb") for _ in range(4)]
                ug = sbt("ug", [96, 4, T], BF)
                ugb = [Buf("ug") for _ in range(4)]
                usv = sbt("usv", [96, 4, T], BF)
                usvb = [Buf("usv") for _ in range(4)]
                fsb = sbt("fsb", [128, 4, T], BF)
                fsbb = Buf("fsb")
                zd2 = [sbt(f"zd{i}", [96, T + 32], F32) for i in range(2)]
                zdb2 = [Buf("zd") for _ in range(2)]
                sa = sbt("sa", [96, T + 32], F32)
                sbb_ = sbt("sb", [96, T + 32], F32)
                sab = Buf("sa")
                sbb = Buf("sb")
                invc = [sbt(f"invc{i}", [96, T], F32) for i in range(2)]
                invcb = [Buf("invc") for _ in range(2)]
                diff = sbt("diff", [96, 4, T], BF)
                diffb = [Buf("diff") for _ in range(4)]
                yd = sbt("yd", [96, 4, T], BF)
                ydb = [Buf("yd") for _ in range(4)]
                mb = sbt("mb", [128, 8, T], BF)
                mbb = [Buf("mb") for _ in range(8)]
                tmp = [sbt(f"tmp{i}", [128, T], F32) for i in range(2)]
                tmpb = [Buf("tmp") for _ in range(2)]
                gs = [sbt(f"gs{i}", [128, T], F32) for i in range(2)]
                gsb = [Buf("gs") for _ in range(2)]
                xr = [sbt(f"xr{i}", [128, D], F32) for i in range(2)]
                xrb = [Buf("xr") for _ in range(2)]
                sqs = [lnt[0], lnt[1], tmp[1]]
                sqb = [lntb[0], lntb[1], tmpb[1]]
                cnt = {"tmp": 0, "gs": 0, "xr": 0, "sg": 0, "lnt": 0}

                par = [(PPt[:], pp_in[l]), (ones[:], ones_in[:, :]),
                       (WsT[:], WB["wst"][l].rearrange("q (h p) -> q h p", h=4)),
                       (dwg[:], WB["dwg"][l].rearrange("c (g d) -> c g d", g=4)),
                       (bsr[:], WB["bsr"][l]),
                       (Gb[:], blg_in[l].broadcast_to([128, 384])),
                       (Bb[:], blb_in[l].broadcast_to([128, 384]))]
                one1b = Buf("one1b")
                S.dma_group("sp", "par", [(lambda e, d_=d_, s_=s_: e.dma_start(out=d_, in_=s_)) for d_, s_ in par],
                            writes=[smallb, PPb, onesb])
                S.op("pool", lambda e: e.memset(one1[:], 1.0), writes=[one1b])
                for m_ in range(3):
                    for k_ in range(16):
                        S.op("dve", lambda e, m_=m_, k_=k_: e.tensor_scalar(
                            out=dg[:, m_ * 16 + k_, :], in0=fr.ident[:, :],
                            scalar1=PPt[:, PP_CW + m_ * 31 + k_:PP_CW + m_ * 31 + k_ + 1], scalar2=None,
                            op0=ALU.mult), reads=[fr.identb, PPb] + ([dgb] if (m_ + k_) > 0 else []), writes=[dgb])

                def piece_list():
                    wi = WB["w_in"][l]

                    def cols(c0, n):
                        return [(lambda slot, c0=c0, n=n: ring[slot][:, 0:8 * n].rearrange("p (k c) -> p k c", k=8),
                                 wi[:, c0:c0 + n].rearrange("(k p) c -> p k c", p=128))]

                    pcs = []
                    pcs.append(("WBv", cols(OFF_B + 384, 384)))
                    pcs.append(("WD", cols(OFF_D, 384)))
                    pcs.append(("WAa", cols(OFF_A, 384)))
                    pcs.append(("WAg", cols(OFF_A + 384, 384)))
                    pcs.append(("WBu", cols(OFF_B, 384)))
                    for br, pj, kk, kp in ((2, "cpk", 4, 128), (3, "d_proj", 4, 96), (1, "b_proj", 4, 96),
                                           (0, "a_proj", 3, 128)):
                        for hf in range(2):
                            pcs.append((f"Wg{br}{hf}", [(
                                lambda slot: ring[slot][:, 0:4096].rearrange("p (k c) -> p k c", k=8),
                                WB["w_gate"][l, br][:, hf * 512:(hf + 1) * 512].rearrange("(k p) c -> p k c", p=128))]))
                        pcs.append((f"Pj{br}", [(
                            lambda slot, kk=kk, kp=kp: ring[slot][0:kp, 0:kk * 1024].rearrange("p (k c) -> p k c", k=kk),
                            WB[pj][l].rearrange("(k p) c -> p k c", p=kp))]))
                    for hf in range(2):
                        pcs.append((f"Wo{hf}", [(
                            lambda slot: ring[slot][:, 0:4096].rearrange("p (k c) -> p k c", k=8),
                            WB["w_out"][l][:, hf * 512:(hf + 1) * 512].rearrange("(k p) c -> p k c", p=128))]))
                    return pcs

                allp = []
                for i in range(NT):
                    for nm, parts in piece_list():
                        allp.append((i, nm, parts))
                pstate = {"next": 0}
                pslot = {}
                pdone = set()
                pidx_of = {}
                for idx_, (i_, nm_, _) in enumerate(allp):
                    pidx_of[(i_, nm_)] = idx_

                def prefetch():
                    while pstate["next"] < len(allp) and (pstate["next"] < NSL or (pstate["next"] - NSL) in pdone):
                        idx = pstate["next"]
                        slot = idx % NSL
                        i_, nm, parts = allp[idx]
                        load_piece(ring, ringb, slot, [(dfn(slot), s_) for dfn, s_ in parts], f"rg{slot}")
                        pslot[(i_, nm)] = slot
                        pstate["next"] += 1

                def use(i, nm):
                    prefetch()
                    return pslot[(i, nm)]

                def rel(i, *nms):
                    for nm in nms:
                        pdone.add(pidx_of[(i, nm)])
                    prefetch()

                def fe_tasks1(i, src_d):
                    s = i % 2
                    t0 = off + i * T
                    tile_id = tb + i
                    srcs = []
                    if i > 0:
                        srcs.append((0, 16, src_d[t0 - 16:t0, :]))
                    if i < NT - 1:
                        srcs.append((16, 16, src_d[t0 + T:t0 + T + 16, :]))
                    chunks = [fr.new(srcs, hT[s][:, :, 0:32], hThb[s], npart=32, zero=(len(srcs) < 2), mask_col=tile_id)]
                    for j in range(4):
                        chunks.append(fr.new([(0, 128, src_d[t0 + j * 128:t0 + (j + 1) * 128, :])],
                                             hT[s][:, :, 32 + j * 128:32 + (j + 1) * 128], hTb[s]))
                    return fr.tasks(chunks)

                def load_F(i):
                    if i >= NT:
                        return
                    t0_ = off + i * T
                    S.dma("pool", "fl", lambda e: e.dma_start(
                        out=fsb[:, :, :], in_=F_d[:, :, t0_:t0_ + T].rearrange("g c t -> c g t")), writes=[fsbb])

                def xr_load1(j, t0_):
                    q_ = j % 2
                    r0_ = t0_ + j * 128
                    S.dma("pool", f"xr{q_}", lambda e: e.dma_start(out=xr[q_][:], in_=xsrc[r0_:r0_ + 128, :]),
                          writes=[xrb[q_]])

                feq = FeQ([fe_tasks1(i, xsrc) for i in range(NT)])
                feq.flush(0)
                load_F(0)
                for i in range(NT):
                    s = i % 2
                    t0 = off + i * T
                    tile_id = tb + i
                    hc = hT[s]

                    slot = use(i, "WBv")
                    wv = ring[slot][:, 0:8 * 384].rearrange("p (k c) -> p k c", k=8)

                    def bv_a(j, slot=slot, wv=wv, hc=hc, s=s):
                        ps, pb = pp.get()
                        q = j % 2

                        def mmv(e):
                            ins = None
                            for k in range(8):
                                ins = e.matmul(ps[:, 0:384], lhsT=hc[:, k, 32 + j * 128:32 + (j + 1) * 128],
                                               rhs=wv[:, k, :], start=(k == 0), stop=(k == 7))
                            return ins

                        S.op("pe", mmv, reads=[hTb[s], ringb[slot]], writes=[pb])
                        S.op("act", lambda e: e.activation(out=vg[q][:], in_=ps[:, 0:384], func=AF.Gelu_apprx_tanh),
                             reads=[pb], writes=[vgb[q]])
                        S.op("dve", lambda e: e.bn_stats(out=bst[q][:], in_=vg[q][:]), reads=[vgb[q]], writes=[bstb[q]])
                        S.op("dve", lambda e: e.bn_aggr(out=bmv[q][:], in_=bst[q][:]), reads=[bstb[q]], writes=[bstb[q]])
                        S.op("dve", lambda e: e.tensor_scalar(out=bmv[q][:, 1:2], in0=bmv[q][:, 1:2], scalar1=EPS, scalar2=None,
                                                              op0=ALU.add), reads=[bstb[q]], writes=[bstb[q]])

                    def bv_b(j):
                        q = j % 2
                        S.op("pool", lambda e: e.tensor_tensor(out=bmv[q][:, 1:2], in0=bmv[q][:, 1:2], in1=fr.negh[:, :], op=ALU.pow),
                             reads=[bstb[q], fr.neghb], writes=[bstb[q]])
                        S.op("dve", lambda e: e.tensor_scalar(out=vn[:], in0=vg[q][:], scalar1=bmv[q][:, 0:1], scalar2=bmv[q][:, 1:2],
                                                              op0=ALU.subtract, op1=ALU.mult),
                             reads=[vgb[q], bstb[q]], writes=[vnb_])
                        S.op("pool", lambda e: e.tensor_tensor(out=vn[:], in0=vn[:], in1=Gb[:], op=ALU.mult),
                             reads=[vnb_, smallb], writes=[vnb_])
                        S.op("pool", lambda e: e.tensor_tensor(out=vnb[:, j, :], in0=vn[:], in1=Bb[:], op=ALU.add),
                             reads=[vnb_, smallb], writes=[vnbb[j]])

                    bv_a(0)
                    bv_a(1)
                    bv_b(0)
                    bv_a(2)
                    bv_b(1)
                    bv_a(3)
                    bv_b(2)
                    bv_b(3)

                    feq.step(i + 1)
                    rel(i, "WBv")
                    slot = use(i, "WD")
                    slotD = slot
                    wd = ring[slot][:, 0:8 * 384].rearrange("p (k c) -> p k c", k=8)
                    slotA = use(i, "WAa")
                    slotG = use(i, "WAg")
                    wa = ring[slotA][:, 0:8 * 384].rearrange("p (k c) -> p k c", k=8)
                    wg_ = ring[slotG][:, 0:8 * 384].rearrange("p (k c) -> p k c", k=8)

                    def d_group(g, wd=wd, hc=hc, s=s, t0=t0, slotD=slotD):
                        S.dma("pool", f"ic{g % 2}", lambda e, t0=t0, g=g: e.dma_start(
                            out=invc[g % 2][:, :], in_=invc_in[g:g + 1, t0:t0 + T].broadcast_to([96, T])),
                            writes=[invcb[g % 2]])
                        zd, zdb = zd2[g % 2], zdb2[g % 2]
                        ps, pb = pp.get()
                        psh, pbh = pp.get()

                        def mmd(e, ps=ps, psh=psh, g=g, wd=wd, hc=hc):
                            ins = None
                            for k in range(8):
                                ins = e.matmul(ps[0:96, :], lhsT=wd[:, k, g * 96:(g + 1) * 96], rhs=hc[:, k, 32:32 + T],
                                               start=(k == 0), stop=(k == 7))
                            for k in range(8):
                                ins = e.matmul(psh[0:96, 0:32], lhsT=wd[:, k, g * 96:(g + 1) * 96], rhs=hc[:, k, 0:32],
                                               start=(k == 0), stop=(k == 7))
                            return ins

                        S.op("pe", mmd, reads=[hTb[s], hThb[s], ringb[slotD]], writes=[pb, pbh])
                        S.op("act", lambda e, ps=ps, zd=zd: e.copy(out=zd[:, 16:16 + T], in_=ps[0:96, :]), reads=[pb], writes=[zdb])
                        S.op("act", lambda e, psh=psh, zd=zd: e.copy(out=zd[:, 0:16], in_=psh[0:96, 0:16]), reads=[pbh, zdb], writes=[zdb])
                        S.op("act", lambda e, psh=psh, zd=zd: e.copy(out=zd[:, 16 + T:32 + T], in_=psh[0:96, 16:32]),
                             reads=[pbh, zdb], writes=[zdb])
                        E = T + 32
                        S.op("dve", lambda e, zd=zd: e.tensor_tensor(out=sa[:, 1:E], in0=zd[:, 0:E - 1], in1=zd[:, 1:E], op=ALU.add),
                             reads=[zdb], writes=[sab])
                        cur, curb, oth, othb = sa, sab, sbb_, sbb
                        lo, hi = 1, E
                        sh = 1
                        for lev in range(g):
                            nlo, nhi = lo + sh, hi - sh
                            S.op("dve", lambda e, cur=cur, oth=oth, nlo=nlo, nhi=nhi, sh=sh: e.tensor_tensor(
                                out=oth[:, nlo:nhi], in0=cur[:, nlo - sh:nhi - sh], in1=cur[:, nlo + sh:nhi + sh], op=ALU.add),
                                reads=[curb], writes=[othb])
                            cur, curb, oth, othb = oth, othb, cur, curb
                            lo, hi = nlo, nhi
                            sh *= 2
                        S.op("dve", lambda e, cur=cur, oth=oth, g=g: e.tensor_tensor(
                            out=oth[:, 16:16 + T], in0=cur[:, 16:16 + T], in1=invc[g % 2][:, :], op=ALU.mult),
                            reads=[curb, invcb[g % 2]], writes=[othb])
                        S.op("dve", lambda e, oth=oth, g=g, zd=zd: e.tensor_tensor(
                            out=diff[:, g, :], in0=oth[:, 16:16 + T], in1=zd[:, 16:16 + T], op=ALU.subtract),
                            reads=[othb, zdb], writes=[diffb[g]])


                    def a_tile(m, wa=wa, wg_=wg_, hc=hc, s=s, slotA=slotA, slotG=slotG):
                        psa, pba = pp.get()
                        psg, pbg = pp.get()
                        psh, pbh = pp.get()

                        def mma(e, psa=psa, psg=psg, psh=psh, m=m, wa=wa, wg_=wg_, hc=hc):
                            ins = None
                            for k in range(8):
                                ins = e.matmul(psg[:, :], lhsT=wg_[:, k, m * 128:(m + 1) * 128], rhs=hc[:, k, 32:32 + T],
                                               start=(k == 0), stop=(k == 7))
                            for k in range(8):
                                ins = e.matmul(psa[:, :], lhsT=wa[:, k, m * 128:(m + 1) * 128], rhs=hc[:, k, 32:32 + T],
                                               start=(k == 0), stop=(k == 7))
                            for k in range(8):
                                ins = e.matmul(psh[:, 32:64], lhsT=wg_[:, k, m * 128:(m + 1) * 128], rhs=hc[:, k, 0:32],
                                               start=(k == 0), stop=(k == 7))
                            for k in range(8):
                                ins = e.matmul(psh[:, 0:32], lhsT=wa[:, k, m * 128:(m + 1) * 128], rhs=hc[:, k, 0:32],
                                               start=(k == 0), stop=(k == 7))
                            return ins

                        S.op("pe", mma, reads=[hTb[s], hThb[s], ringb[slotA], ringb[slotG]], writes=[pba, pbg, pbh])
                        q = cnt["sg"] % 2
                        cnt["sg"] += 1
                        S.op("act", lambda e, psg=psg, q=q: e.activation(out=sg[q][:], in_=psg[:, :], func=AF.Sigmoid),
                             reads=[pbg], writes=[sgb[q]])
                        S.op("act", lambda e, psh=psh: e.activation(out=sgh[:], in_=psh[:, 32:64], func=AF.Sigmoid),
                             reads=[pbh], writes=[sghb])
                        S.op("dve", lambda e, psa=psa, q=q, m=m: e.tensor_tensor(out=u[:, m, 16:16 + T], in0=psa[:, :], in1=sg[q][:],
                                                                               op=ALU.mult),
                             reads=[pba, sgb[q]], writes=[ub_[m]])
                        S.op("dve", lambda e, psh=psh, m=m: e.tensor_tensor(out=u[:, m, 0:16], in0=psh[:, 0:16], in1=sgh[:, 0:16],
                                                                          op=ALU.mult), reads=[pbh, sghb, ub_[m]], writes=[ub_[m]])
                        S.op("dve", lambda e, psh=psh, m=m: e.tensor_tensor(out=u[:, m, 16 + T:32 + T], in0=psh[:, 16:32],
                                                                          in1=sgh[:, 16:32], op=ALU.mult),
                             reads=[pbh, sghb, ub_[m]], writes=[ub_[m]])

                    d_group(0)
                    a_tile(0)
                    d_group(1)
                    feq.step(i + 1)
                    a_tile(1)
                    d_group(2)
                    a_tile(2)
                    d_group(3)
                    rel(i, "WD")
                    feq.step(i + 1)
                    rel(i, "WAa", "WAg")
                    for m in range(3):
                        psc, pbc = pp.get()

                        def mmc(e, psc=psc, m=m):
                            ins = None
                            for kk in range(16):
                                ins = e.matmul(psc[:, :], lhsT=dg[:, m * 16 + kk, :], rhs=u[:, m, 1 + kk:1 + kk + T],
                                               start=(kk == 0), stop=(kk == 15))
                            return ins

                        S.op("pe", mmc, reads=[ub_[m], dgb], writes=[pbc])
                        S.op("act", lambda e, psc=psc, m=m: e.activation(out=acc[:, m, :], in_=psc[:, :], func=AF.Identity,
                                                                       bias=PPt[:, PP_CB + m:PP_CB + m + 1]),
                             reads=[pbc, PPb], writes=[accb[m]])
                    drip = []
                    for kk in range(16, 31):
                        for m in range(3):
                            wcol = PPt[:, PP_CW + m * 31 + kk:PP_CW + m * 31 + kk + 1]
                            drip.append(lambda m=m, wcol=wcol, kk=kk: S.op("dve", lambda e: e.scalar_tensor_tensor(
                                out=acc[:, m, :], in0=u[:, m, 1 + kk:1 + kk + T], scalar=wcol, in1=acc[:, m, :],
                                op0=ALU.mult, op1=ALU.add), reads=[ub_[m], PPb, accb[m]], writes=[accb[m]]))

                    def ln_block():
                        for m in range(3):
                            S.op("act", lambda e, m=m: e.activation(out=sqs[m][:], in_=acc[:, m, :], func=AF.Square),
                                 reads=[accb[m]], writes=[sqb[m]])
                        ps1, pb1 = pp.get()
                        ps2, pb2 = pp.get()

                        def mmst(e, ps1=ps1, ps2=ps2):
                            ins = None
                            for m in range(3):
                                ins = e.matmul(ps1[:, :], lhsT=ones[:, :], rhs=acc[:, m, :], start=(m == 0), stop=(m == 2))
                            for m in range(3):
                                ins = e.matmul(ps2[:, :], lhsT=ones[:, :], rhs=sqs[m][:], start=(m == 0), stop=(m == 2))
                            return ins

                        S.op("pe", mmst, reads=accb + sqb + [onesb], writes=[pb1, pb2])
                        S.op("act", lambda e, ps1=ps1: e.activation(out=lnm2, in_=ps1[:, :], func=AF.Square),
                             reads=[pb1], writes=[lnm2b])
                        S.op("act", lambda e, ps1=ps1: e.copy(out=tmp[0][:], in_=ps1[:, :]), reads=[pb1], writes=[tmpb[0]])
                        S.op("dve", lambda e, ps2=ps2: e.scalar_tensor_tensor(out=lnrs, in0=ps2[:, :], scalar=EPS, in1=lnm2,
                                                                           op0=ALU.add, op1=ALU.subtract),
                             reads=[pb2, lnm2b], writes=[lnrsb])
                        S.op("act", lambda e: e.activation(out=lnrs, in_=lnrs, func=AF.Sqrt), reads=[lnrsb], writes=[lnrsb])
                        S.op("dve", lambda e: e.reciprocal(out=lnrs, in_=lnrs), reads=[lnrsb], writes=[lnrsb])
                        for m in range(3):
                            q = cnt["lnt"] % 2
                            cnt["lnt"] += 1
                            S.op("dve", lambda e, m=m, q=q: e.tensor_tensor(out=lnt[q][:], in0=acc[:, m, :], in1=tmp[0][:],
                                                                         op=ALU.subtract),
                                 reads=[accb[m], tmpb[0]], writes=[lntb[q]])
                            S.op("dve", lambda e, q=q: e.tensor_tensor(out=lnt[q][:], in0=lnt[q][:], in1=lnrs, op=ALU.mult),
                                 reads=[lntb[q], lnrsb], writes=[lntb[q]])
                            S.op("act", lambda e, m=m, q=q: e.activation(out=cbf[:, m, :], in_=lnt[q][:], func=AF.Silu,
                                                                       scale=PPt[:, PP_LG + m:PP_LG + m + 1],
                                                                       bias=PPt[:, PP_LB + m:PP_LB + m + 1]),
                                 reads=[lntb[q], PPb], writes=[cbfb[m]])


                    slot = use(i, "WBu")
                    wu = ring[slot][:, 0:8 * 384].rearrange("p (k c) -> p k c", k=8)
                    for hd in range(4):
                        ps, pb = pp.get()

                        def mmu(e, ps=ps, hd=hd, wu=wu, hc=hc):
                            ins = None
                            for k in range(8):
                                ins = e.matmul(ps[0:96, :], lhsT=wu[:, k, hd * 96:(hd + 1) * 96], rhs=hc[:, k, 32:32 + T],
                                               start=(k == 0), stop=(k == 7))
                            return ins

                        S.op("pe", mmu, reads=[hTb[s], ringb[slot]], writes=[pb])
                        S.op("act", lambda e, ps=ps, hd=hd: e.activation(out=ug[:, hd, :], in_=ps[0:96, :],
                                                                       func=AF.Gelu_apprx_tanh),
                             reads=[pb], writes=[ugb[hd]])
                        for _ in range(3):
                            if drip:
                                drip.pop(0)()
                    feq.step(i + 1)
                    rel(i, "WBu")
                    for hd in range(4):
                        ps, pb = pp.get()

                        def mms(e, ps=ps, hd=hd):
                            ins = None
                            for j in range(4):
                                ins = e.matmul(ps[0:96, j * 128:(j + 1) * 128], lhsT=vnb[:, j, hd * 96:(hd + 1) * 96],
                                               rhs=WsT[:, hd, :], start=True, stop=False)
                                ins = e.matmul(ps[0:96, j * 128:(j + 1) * 128], lhsT=one1[0:1, :],
                                               rhs=bsr[0:1, hd * 128:(hd + 1) * 128], start=False, stop=True)
                            return ins

                        S.op("pe", mms, reads=vnbb + [smallb, one1b], writes=[pb])
                        S.op("dve", lambda e, ps=ps, hd=hd: e.tensor_tensor(out=usv[:, hd, :], in0=ps[0:96, :], in1=ug[:, hd, :],
                                                                          op=ALU.mult),
                             reads=[pb, ugb[hd]], writes=[usvb[hd]])
                        for _ in range(3):
                            if drip:
                                drip.pop(0)()
                    for g in range(4):
                        ps, pb = pp.get()
                        S.op("pe", lambda e, ps=ps, g=g: e.matmul(ps[0:96, :], lhsT=dwg[:, g, :], rhs=diff[:, g, :],
                                                                   start=True, stop=True),
                             reads=[diffb[g], smallb], writes=[pb])
                        S.op("act", lambda e, ps=ps, g=g: e.activation(out=yd[:, g, :], in_=ps[0:96, :], func=AF.Copy,
                                                                     scale=PPt[0:96, PP_DS + g:PP_DS + g + 1]),
                             reads=[pb, PPb], writes=[ydb[g]])

                    first = True
                    for br, nm, kk, kp, srcs_ in ((2, "c", 4, 128, None), (3, "d", 4, 96, None), (1, "b", 4, 96, None),
                                                  (0, "a", 3, 128, None)):
                        feq.step(i + 1)
                        if br == 0:
                            xr_load1(0, t0)
                            xr_load1(1, t0)
                        sl0 = use(i, f"Wg{br}0")
                        sl1 = use(i, f"Wg{br}1")
                        slp = use(i, f"Pj{br}")
                        wgl = [ring[sl0][:, 0:4096].rearrange("p (k c) -> p k c", k=8),
                               ring[sl1][:, 0:4096].rearrange("p (k c) -> p k c", k=8)]
                        wpj = ring[slp][0:kp, 0:kk * 1024].rearrange("p (k c) -> p k c", k=kk)
                        ksz = [kp] * kk
                        if br == 2:
                            ksz = [128, 68, 128, 68]
                            rhs_of = lambda k_: fsb[0:(128 if k_ % 2 == 0 else 68), k_, :]
                            rbufs = [fsbb]
                        elif br == 3:
                            rhs_of = lambda k_: yd[:, k_, :]
                            rbufs = ydb
                        elif br == 1:
                            rhs_of = lambda k_: usv[:, k_, :]
                            rbufs = usvb
                        else:
                            rhs_of = lambda k_: cbf[:, k_, :]
                            rbufs = cbfb
                        for dt in range(8):
                            psg, pbg = pp.get()
                            psp, pbp = pp.get()
                            wgh = wgl[dt // 4]
                            slg = sl0 if dt < 4 else sl1

                            def mmg(e, psg=psg, psp=psp, dt=dt, wgh=wgh, wpj=wpj, rhs_of=rhs_of, kk=kk, hc=hc, ksz=ksz):
                                ins = None
                                for k in range(8):
                                    ins = e.matmul(psg[:, :], lhsT=wgh[:, k, (dt % 4) * 128:(dt % 4 + 1) * 128],
                                                   rhs=hc[:, k, 32:32 + T], start=(k == 0), stop=(k == 7))
                                for k in range(kk):
                                    ins = e.matmul(psp[:, :], lhsT=wpj[0:ksz[k], k, dt * 128:(dt + 1) * 128], rhs=rhs_of(k),
                                                   start=(k == 0), stop=(k == kk - 1))
                                return ins

                            S.op("pe", mmg, reads=[hTb[s], ringb[slg], ringb[slp]] + list(rbufs), writes=[pbg, pbp])
                            q = cnt["gs"] % 2
                            cnt["gs"] += 1
                            S.op("act", lambda e, psg=psg, q=q, br=br, dt=dt: e.activation(
                                out=gs[q][:], in_=psg[:, :], func=AF.Sigmoid,
                                bias=PPt[:, PP_BG + br * 8 + dt:PP_BG + br * 8 + dt + 1]),
                                reads=[pbg, PPb], writes=[gsb[q]])
                            if first:
                                S.op("dve", lambda e, psp=psp, q=q, dt=dt: e.tensor_tensor(out=mg[:, dt, :], in0=psp[:, :], in1=gs[q][:],
                                                                                          op=ALU.mult),
                                     reads=[pbp, gsb[q]], writes=[mgb[dt]])
                            else:
                                q2 = cnt["tmp"] % 2
                                cnt["tmp"] += 1
                                S.op("dve", lambda e, psp=psp, q=q, q2=q2: e.tensor_tensor(out=tmp[q2][:], in0=psp[:, :], in1=gs[q][:],
                                                                                        op=ALU.mult),
                                     reads=[pbp, gsb[q]], writes=[tmpb[q2]])
                                if br != 0:
                                    S.op("pool", lambda e, q2=q2, dt=dt: e.tensor_tensor(out=mg[:, dt, :], in0=mg[:, dt, :], in1=tmp[q2][:],
                                                                                        op=ALU.add),
                                         reads=[tmpb[q2], mgb[dt]], writes=[mgb[dt]])
                                else:
                                    S.op("pool", lambda e, q2=q2, dt=dt: e.tensor_tensor(out=mb[:, dt, :], in0=mg[:, dt, :], in1=tmp[q2][:],
                                                                                        op=ALU.add),
                                         reads=[tmpb[q2], mgb[dt]], writes=[mbb[dt]])
                            if br in (2, 3):
                                for _ in range(2 if br == 2 else 1):
                                    if drip:
                                        drip.pop(0)()
                            if br == 1 and dt == 1:
                                ln_block()
                        if br == 2:
                            load_F(i + 1)
                        rel(i, f"Wg{br}0", f"Wg{br}1", f"Pj{br}")
                        if br == 3:
                            while drip:
                                drip.pop(0)()
                        first = False

                    slo = [use(i, "Wo0"), use(i, "Wo1")]
                    wol = [ring[slo[0]][:, 0:4096].rearrange("p (k c) -> p k c", k=8),
                           ring[slo[1]][:, 0:4096].rearrange("p (k c) -> p k c", k=8)]
                    for j in range(4):
                        q = j % 2
                        r0 = t0 + j * 128
                        for hf in range(2):
                            ps, pb = pp.get()

                            def mmo(e, ps=ps, j=j, hf=hf, wol=wol):
                                ins = None
                                for k in range(8):
                                    ins = e.matmul(ps[:, :], lhsT=mb[:, k, j * 128:(j + 1) * 128], rhs=wol[hf][:, k, :],
                                                   start=(k == 0), stop=(k == 7))
                                return ins

                            S.op("pe", mmo, reads=mbb + [ringb[slo[hf]]], writes=[pb])
                            S.op("dve", lambda e, ps=ps, q=q, hf=hf: e.tensor_tensor(
                                out=xr[q][:, hf * 512:(hf + 1) * 512], in0=ps[:, :], in1=xr[q][:, hf * 512:(hf + 1) * 512],
                                op=ALU.add), reads=[pb, xrb[q]], writes=[xrb[q]])
                        S.dma("pool", f"xst{q}", lambda e, q=q, r0=r0: e.dma_start(out=x1_d[r0:r0 + 128, :], in_=xr[q][:]),
                              reads=[xrb[q]])
                        if j + 2 < 4:
                            xr_load1(j + 2, t0)
                    rel(i, "Wo0", "Wo1")
                    feq.flush(i + 1)
                S.emit()

        def pass2(l, seq):
            uid[0] += 1
            U = f"u{uid[0]}"
            off, SL, dn, tb = seq
            NT = SL // T
            last = (l == L - 1)
            xdst = y_out if last else x2_d
            with ExitStack() as es:
                sbt = lambda n, s, d: es.enter_context(nc.sbuf_tensor(f"{U}p2_{n}", s, d))
                pp = Ps(es, nc, 8, U + "p2")
                fr = Front(es, U + "p2", g2_in[l], pp)
                HW_ = 32 + T
                hT = [sbt(f"hT{i}", [128, 8, HW_], BF) for i in range(2)]
                hTb = [Buf("hT") for _ in range(2)]
                hThb = [Buf("hTh") for _ in range(2)]
                NSL = 4
                ring = [sbt(f"ring{i}", [128, 8, 2, 256], BF) for i in range(NSL)]
                ringb = [Buf("ring") for _ in range(NSL)]
                PPt = sbt("pp", [128, NPP], F32)
                PPb = Buf("pp")
                FD = sbt("FD", [128, NMT, D], BF)
                FDb = Buf("FD")
                gfb = sbt("gfb", [128, D], F32)
                gfbb = Buf("gfb")
                act_bf = [sbt(f"actbf{i}", [128, NMT, T], BF) for i in range(2)]
                actb = [[Buf("act") for _ in range(NMT)] for _ in range(2)]
                cg = [sbt(f"cg{i}", [128, T], F32) for i in range(3)]
                cv = [sbt(f"cv{i}", [128, T], F32) for i in range(3)]
                cgb = [Buf("cg") for _ in range(3)]
                cvb = [Buf("cv") for _ in range(3)]
                xr = [sbt(f"xr{i}", [128, D], F32) for i in range(2)]
                xrb = [Buf("xr") for _ in range(2)]
                fss = sbt("fss", [128, 1], F32)
                fssb = Buf("fss")
                fjunk = fr.junk
                fjunkb = fr.junkb
                cnt = {"xr": 0, "ub": 0}
                S.dma("sp", "par", lambda e: e.dma_start(out=PPt[:], in_=pp_in[l]), writes=[PPb])
                S.dma("sp", "par", lambda e: e.dma_start(out=gfb[:], in_=gf_in.broadcast_to([128, D])), writes=[gfbb])
                for c3 in range(0, NMT, 6):
                    n3 = min(6, NMT - c3)
                    S.dma("sp", "par", lambda e, c3=c3, n3=n3: e.dma_start(
                        out=FD[:, c3:c3 + n3, :],
                        in_=WB["f_down"][l][c3 * 128:(c3 + n3) * 128, :].rearrange("(k p) c -> p k c", p=128)),
                        writes=[FDb])
                FDb.w = ("par", S.dcnt["par"])
                PPb.w = FDb.w
                gfbb.w = FDb.w
                NPC = NMT // 2
                allp = [(i, r) for i in range(NT) for r in range(NPC)]
                pstate = {"next": 0}
                wu_ = WB["f_up"][l]

                pdone = set()

                def prefetch():
                    while pstate["next"] < len(allp) and (pstate["next"] < NSL or (pstate["next"] - NSL) in pdone):
                        idx = pstate["next"]
                        slot = idx % NSL
                        i_, r = allp[idx]
                        parts = []
                        for gv in range(2):
                            c0 = gv * DFF + r * 256
                            parts.append((ring[slot][:, :, gv, :], wu_[:, c0:c0 + 256].rearrange("(k p) c -> p k c", p=128)))
                        load_piece(ring, ringb, slot, parts, f"rg{slot}")
                        pstate["next"] += 1

                def fe_tasks2(i):
                    s = i % 2
                    t0 = off + i * T
                    tile_id = tb + i
                    srcs = []
                    if i > 0:
                        srcs.append((0, 16, x1_d[t0 - 16:t0, :]))
                    if i < NT - 1:
                        srcs.append((16, 16, x1_d[t0 + T:t0 + T + 16, :]))
                    chunks = [fr.new(srcs, hT[s][:, :, 0:32], hThb[s], npart=32, zero=(len(srcs) < 2), mask_col=tile_id)]
                    for j in range(4):
                        chunks.append(fr.new([(0, 128, x1_d[t0 + j * 128:t0 + (j + 1) * 128, :])],
                                             hT[s][:, :, 32 + j * 128:32 + (j + 1) * 128], hTb[s]))
                    return fr.tasks(chunks)

                feq = FeQ([fe_tasks2(i) for i in range(NT)])
                feq.flush(0)
                pist = {"pi": 0, "pend": None}

                def up_piece(i, r):
                    if True:
                        s = i % 2
                        hc = hT[s]
                        ab = act_bf[i % 2]
                        abb = actb[i % 2]
                        prefetch()
                        slot = pist["pi"] % NSL
                        pist["pi"] += 1
                        feq.step(i + 1)
                        for mm_ in range(2):
                            mt = 2 * r + mm_
                            psg, pbg = pp.get()
                            psv, pbv = pp.get()
                            psh, pbh = pp.get()

                            def mmup(e, psg=psg, psv=psv, psh=psh, mm_=mm_, slot=slot, hc=hc):
                                ins = None
                                for gv, pso in ((0, psg), (1, psv)):
                                    for k in range(8):
                                        ins = e.matmul(pso[:, :], lhsT=ring[slot][:, k, gv, mm_ * 128:(mm_ + 1) * 128],
                                                       rhs=hc[:, k, 32:32 + T], start=(k == 0), stop=(k == 7))
                                for gv in range(2):
                                    for k in range(8):
                                        ins = e.matmul(psh[:, gv * 32:gv * 32 + 32],
                                                       lhsT=ring[slot][:, k, gv, mm_ * 128:(mm_ + 1) * 128],
                                                       rhs=hc[:, k, 0:32], start=(k == 0), stop=(k == 7))
                                return ins

                            S.op("pe", mmup, reads=[hTb[s], hThb[s], ringb[slot]], writes=[pbg, pbv, pbh])
                            q = cnt["ub"] % 3
                            cnt["ub"] += 1
                            for gv, pso, pbo, cc, ccb in ((0, psg, pbg, cg[q], cgb[q]), (1, psv, pbv, cv[q], cvb[q])):
                                ch_ = gv * NMT + mt
                                w0 = PPt[:, PP_FW + ch_ * 3 + 0:PP_FW + ch_ * 3 + 1]
                                w1 = PPt[:, PP_FW + ch_ * 3 + 1:PP_FW + ch_ * 3 + 2]
                                w2 = PPt[:, PP_FW + ch_ * 3 + 2:PP_FW + ch_ * 3 + 3]
                                fb = PPt[:, PP_FB + ch_:PP_FB + ch_ + 1]
                                S.op("act", lambda e, pso=pso, cc=cc, w1=w1, fb=fb: e.activation(
                                    out=cc[:], in_=pso[:, :], func=AF.Identity, scale=w1, bias=fb),
                                    reads=[pbo, PPb], writes=[ccb])
                                S.op("act", lambda e, psh=psh, cc=cc, w0=w0, gv=gv: e.activation(
                                    out=cc[:, 0:1], in_=psh[:, gv * 32 + 15:gv * 32 + 16], func=AF.Identity,
                                    scale=w0, bias=cc[:, 0:1]), reads=[pbh, PPb, ccb], writes=[ccb])
                                S.op("act", lambda e, psh=psh, cc=cc, w2=w2, gv=gv: e.activation(
                                    out=cc[:, T - 1:T], in_=psh[:, gv * 32 + 16:gv * 32 + 17], func=AF.Identity,
                                    scale=w2, bias=cc[:, T - 1:T]), reads=[pbh, PPb, ccb], writes=[ccb])
                                S.op("dve", lambda e, pso=pso, cc=cc, w0=w0: e.scalar_tensor_tensor(
                                    out=cc[:, 1:T], in0=pso[:, 0:T - 1], scalar=w0, in1=cc[:, 1:T], op0=ALU.mult, op1=ALU.add),
                                    reads=[pbo, PPb, ccb], writes=[ccb])
                                S.op("dve", lambda e, pso=pso, cc=cc, w2=w2: e.scalar_tensor_tensor(
                                    out=cc[:, 0:T - 1], in0=pso[:, 1:T], scalar=w2, in1=cc[:, 0:T - 1], op0=ALU.mult, op1=ALU.add),
                                    reads=[pbo, PPb, ccb], writes=[ccb])

                            def fin(q=q, mt=mt, ab=ab, abb=abb):
                                S.op("act", lambda e: e.activation(out=cg[q][:], in_=cg[q][:], func=AF.Gelu_apprx_tanh),
                                     reads=[cgb[q]], writes=[cgb[q]])
                                S.op("pool", lambda e: e.tensor_tensor(out=ab[:, mt, :], in0=cg[q][:], in1=cv[q][:],
                                                                       op=ALU.mult),
                                     reads=[cgb[q], cvb[q]], writes=[abb[mt]])

                            if pist["pend"] is not None:
                                pist["pend"]()
                            pist["pend"] = fin
                        pdone.add(pist["pi"] - 1)
                def down_tile(i):
                    t0 = off + i * T
                    ab = act_bf[i % 2]
                    abb = actb[i % 2]
                    def xr_load2(j):
                        q_ = j % 2
                        r0_ = t0 + j * 128
                        S.dma("pool", f"xr{q_}", lambda e: e.dma_start(out=xr[q_][:], in_=x1_d[r0_:r0_ + 128, :]),
                              writes=[xrb[q_]])

                    xr_load2(0)
                    xr_load2(1)
                    for j in range(4):
                        q = j % 2
                        r0 = t0 + j * 128
                        for hf in range(2):
                            ps, pb = pp.get()

                            def mmdn(e, ps=ps, j=j, hf=hf, ab=ab):
                                ins = None
                                for mt in range(NMT):
                                    ins = e.matmul(ps[:, :], lhsT=ab[:, mt, j * 128:(j + 1) * 128],
                                                   rhs=FD[:, mt, hf * 512:(hf + 1) * 512], start=(mt == 0), stop=(mt == NMT - 1))
                                return ins

                            S.op("pe", mmdn, reads=abb + [FDb], writes=[pb])
                            S.op("dve", lambda e, ps=ps, q=q, hf=hf: e.tensor_tensor(
                                out=xr[q][:, hf * 512:(hf + 1) * 512], in0=ps[:, :], in1=xr[q][:, hf * 512:(hf + 1) * 512],
                                op=ALU.add), reads=[pb, xrb[q]], writes=[xrb[q]])
                        if last:
                            S.op("act", lambda e, q=q: e.activation(out=fjunk[:, :], in_=xr[q][:], func=AF.Square, accum_out=fss[:]),
                                 reads=[xrb[q]], writes=[fjunkb, fssb])
                            S.op("dve", lambda e: e.tensor_scalar(out=fss[:], in0=fss[:], scalar1=1.0 / D, scalar2=EPS,
                                                                  op0=ALU.mult, op1=ALU.add), reads=[fssb], writes=[fssb])
                            S.op("pool", lambda e: e.tensor_tensor(out=fss[:], in0=fss[:], in1=fr.negh[:, :], op=ALU.pow),
                                 reads=[fssb, fr.neghb], writes=[fssb])
                            S.op("dve", lambda e, q=q: e.scalar_tensor_tensor(out=xr[q][:], in0=xr[q][:], scalar=fss[:, 0:1], in1=gfb[:],
                                                                           op0=ALU.mult, op1=ALU.mult),
                                 reads=[xrb[q], fssb, gfbb], writes=[xrb[q]])
                        S.dma("pool", f"xst{q}", lambda e, q=q, r0=r0: e.dma_start(out=xdst[r0:r0 + 128, :], in_=xr[q][:]),
                              reads=[xrb[q]])
                        if j + 2 < 4:
                            xr_load2(j + 2)

                R0 = 2
                for i in range(NT):
                    for r in range(R0 if i > 0 else 0, NPC):
                        up_piece(i, r)
                    feq.flush(i + 1)
                    if i + 1 < NT:
                        for r in range(R0):
                            up_piece(i + 1, r)
                    else:
                        pist["pend"]()
                        pist["pend"] = None
                    down_tile(i)
                S.emit()

        stages = cfg.get("stages", ("p0", "p1", "p2"))
        for l in range(L):
            for seq in SEQS:
                if "p0" in stages:
                    pass0(l, seq)
                if "p1" in stages:
                    pass1(l, seq)
                if "p2" in stages:
                    pass2(l, seq)
        if dbg:
            for nm, ap in dbg_out.items():
                src = {"x1": x1_d, "x2": x2_d}[nm]
                S.dma("pool", "dbg", lambda e, ap=ap, src=src: e.dma_start(out=ap, in_=src))
            S.emit()
    return nc


def _bf(a):
    return np.ascontiguousarray(a.astype(ml_dtypes.bfloat16))


def dft_tables(Stot, Ssub):
    out = np.zeros((2, Stot, Stot), dtype=ml_dtypes.bfloat16)
    k = np.arange(Ssub, dtype=np.int64)
    ang = 2.0 * np.pi * np.arange(Ssub, dtype=np.float64) / Ssub
    ct = (np.cos(ang) / np.sqrt(Ssub)).astype(np.float32)
    st = (-np.sin(ang) / np.sqrt(Ssub)).astype(np.float32)
    idx = (np.outer(k, k) % Ssub)
    cb = ct[idx].astype(ml_dtypes.bfloat16)
    sb_ = st[idx].astype(ml_dtypes.bfloat16)
    for b in range(Stot // Ssub):
        out[0, b * Ssub:(b + 1) * Ssub, b * Ssub:(b + 1) * Ssub] = cb
        out[1, b * Ssub:(b + 1) * Ssub, b * Ssub:(b + 1) * Ssub] = sb_
    return out


def invcnt_table(seq_lens):
    cols = []
    for S_ in seq_lens:
        t = np.arange(S_)
        rows = []
        for w in POOLW:
            lo, hi = -(w // 2), w // 2 - 1
            start = np.clip(t + lo, 0, S_)
            end = np.clip(t + hi + 1, 0, S_)
            rows.append(1.0 / (end - start).astype(np.float32))
        cols.append(np.stack(rows, 0))
    return np.ascontiguousarray(np.concatenate(cols, axis=1).astype(np.float32))


def host_weights(inp, L):
    m = {}
    for k in ("w_in", "w_gate", "a_proj", "b_proj", "d_proj", "w_out", "f_up", "f_down"):
        m[k] = np.ascontiguousarray(inp[k], dtype=np.float32)
    cp = np.asarray(inp["c_proj"], dtype=np.float32)
    cpk = np.zeros((L, 2, 256, D), np.float32)
    for g in range(4):
        for c in range(49):
            idx = g * 49 + c
            cpk[:, 0, idx] = cp[:, g * 96 + c]
            if 1 <= c <= 47:
                cpk[:, 1, idx] = cp[:, g * 96 + 96 - c]
    m["cpk"] = np.ascontiguousarray(cpk.reshape(L, 512, D))
    m["wst"] = np.ascontiguousarray(np.transpose(inp["b_ws"], (0, 3, 1, 2)).reshape(L, 128, 512), dtype=np.float32)
    m["dwg"] = np.ascontiguousarray(np.transpose(inp["d_wg"], (0, 2, 1, 3)).reshape(L, 96, 384), dtype=np.float32)
    m["bsr"] = np.ascontiguousarray(inp["b_bs"].reshape(L, 1, 512), dtype=np.float32)
    pp = np.zeros((L, 128, NPP), np.float32)
    cw = np.transpose(inp["a_conv_w"], (0, 2, 1)).reshape(L, 3, 128, 31)
    pp[:, :, PP_CW:PP_CW + 93] = np.transpose(cw, (0, 2, 1, 3)).reshape(L, 128, 93)
    pp[:, :, PP_CB:PP_CB + 3] = np.transpose(inp["a_conv_b"].reshape(L, 3, 128), (0, 2, 1))
    pp[:, :, PP_LG:PP_LG + 3] = np.transpose(inp["a_ln_g"].reshape(L, 3, 128), (0, 2, 1))
    pp[:, :, PP_LB:PP_LB + 3] = np.transpose(inp["a_ln_b"].reshape(L, 3, 128), (0, 2, 1))
    pp[:, :, PP_BG:PP_BG + 32] = np.transpose(inp["b_gate"].reshape(L, 4, 8, 128), (0, 3, 1, 2)).reshape(L, 128, 32)
    pp[:, 0:96, PP_DS:PP_DS + 4] = np.transpose(inp["d_scale"].reshape(L, 4, 96), (0, 2, 1))
    fw = np.transpose(inp["f_conv_w"], (0, 2, 1)).reshape(L, 44, 128, 3)
    pp[:, :, PP_FW:PP_FW + 132] = np.transpose(fw, (0, 2, 1, 3)).reshape(L, 128, 132)
    pp[:, :, PP_FB:PP_FB + 44] = np.transpose(inp["f_conv_b"].reshape(L, 44, 128), (0, 2, 1))
    m["pp"] = pp
    m["g1"] = np.ascontiguousarray(inp["norm1_g"].reshape(L, 1, D), dtype=np.float32)
    m["g2"] = np.ascontiguousarray(inp["norm2_g"].reshape(L, 1, D), dtype=np.float32)
    m["gf"] = np.ascontiguousarray(inp["final_g"].reshape(1, D), dtype=np.float32)
    m["blg"] = np.ascontiguousarray(inp["b_ln_g"].reshape(L, 1, 384), dtype=np.float32)
    m["blb"] = np.ascontiguousarray(inp["b_ln_b"].reshape(L, 1, 384), dtype=np.float32)
    c = np.arange(96)
    ang = 2.0 * np.pi * np.outer(c, c) / 96.0
    cs96 = np.concatenate([np.cos(ang)[:, 0:49], np.sin(ang)[:, 0:49]], axis=1) / np.sqrt(96.0)
    m["cs96"] = _bf(cs96.astype(np.float32))
    m["ident"] = _bf(np.eye(128, dtype=np.float32))
    m["ones"] = np.full((128, 128), 1.0 / 384.0, np.float32)
    return m


_CACHE = {}


def kernel(**inputs):
    L = 2
    xp = np.asarray(inputs["x_prompt"], dtype=np.float32)
    xs = np.asarray(inputs["x_sample"], dtype=np.float32)
    SA, SB = 8192, 2048
    NTOK = SA + 2 * SB
    seqs = [(0, SA, "A", 0), (SA, SB, "B", SA // T), (SA + SB, SB, "B", SA // T + SB // T)]
    NTT = NTOK // T
    cfg = {"seqs": seqs, "ntok": NTOK, "L": L, "ntiles": NTT}
    if "nc" not in _CACHE:
        _CACHE["nc"] = build(cfg)
    nc = _CACHE["nc"]
    base = host_weights(inputs, L)
    dftA_full = dft_tables(SA, SA)
    dftA_blk = dft_tables(SA, SB)
    dftB = dft_tables(SB, SB)
    in_maps = []
    assign = []
    for c in range(8):
        if c < 4:
            xa = xs[c]
            pids = [2 * c, 2 * c + 1]
            hm = np.ones((32, NTT), np.float32)
            invc = invcnt_table([SA, SB, SB])
            dA = dftA_full
            apid = None
        else:
            p0 = 8 + 6 * (c - 4)
            apid = [p0, p0 + 1, p0 + 2, p0 + 3]
            xa = xp[apid].reshape(SA, D)
            pids = [p0 + 4, p0 + 5]
            hm = np.ones((32, NTT), np.float32)
            for ti in range(SA // T):
                t0 = ti * T
                if t0 % SB == 0:
                    hm[0:16, ti] = 0.0
                if (t0 + T) % SB == 0:
                    hm[16:32, ti] = 0.0
            invc = invcnt_table([SB] * 6)
            dA = dftA_blk
        xcat = np.ascontiguousarray(np.concatenate([xa, xp[pids[0]], xp[pids[1]]], axis=0))
        m = dict(base)
        m["x"] = xcat
        m["hmask"] = hm
        m["invc"] = invc
        m["dftA"] = dA
        m["dftB"] = dftB
        in_maps.append(m)
        assign.append((c, apid, pids))
    res = run_bass_kernel_spmd(nc, in_maps, core_ids=list(range(8)))
    yp = np.empty_like(xp)
    ys = np.empty_like(xs)
    for (c, apid, pids), r in zip(assign, res.results):
        yv = np.asarray(r["y"], dtype=np.float32)
        if apid is None:
            ys[c] = yv[0:SA]
        else:
            yp[apid] = yv[0:SA].reshape(4, SB, D)
        yp[pids[0]] = yv[SA:SA + SB]
        yp[pids[1]] = yv[SA + SB:SA + 2 * SB]
    return (yp, ys)
```

```python
import numpy as np
import ml_dtypes
from contextlib import ExitStack
import concourse.bass as bass
import concourse.mybir as mybir
from concourse.bass_utils import run_bass_kernel_spmd

F32 = mybir.dt.float32
BF = mybir.dt.bfloat16
AF = mybir.ActivationFunctionType
ALU = mybir.AluOpType

D = 1024
DIN = 2304
DFF = 2816
NMT = DFF // 128
T = 512
HL = 16
EPS = 1e-6
OFF_A, OFF_B, OFF_C, OFF_D = 0, 768, 1536, 1920
POOLW = (2, 4, 8, 16)
ENG = ("pe", "act", "dve", "pool", "sp")

PP_CW = 0
PP_CB = PP_CW + 93
PP_LG = PP_CB + 3
PP_LB = PP_LG + 3
PP_BG = PP_LB + 3
PP_DS = PP_BG + 32
PP_FW = PP_DS + 4
PP_FB = PP_FW + 132
NPP = PP_FB + 44


class Buf:
    __slots__ = ("name", "w", "r")

    def __init__(self, name):
        self.name = name
        self.w = None
        self.r = {}


class Sched:
    def __init__(self, nc, es):
        self.nc = nc
        self.h = {}
        for e in ENG:
            self.h[e] = es.enter_context(nc.semaphore("s_" + e))
        self.cnt = {e: 0 for e in ENG}
        self.dcnt = {}
        self.known = {e: {} for e in ENG}
        self.ops = {e: [] for e in ENG}
        self.es = es

    def dsem(self, name):
        if name not in self.h:
            self.h[name] = self.es.enter_context(self.nc.semaphore("d_" + name))
            self.dcnt[name] = 0
        return name

    def _collect(self, eng, reads, writes):
        waits = {}

        def need(dep, kind):
            if dep is None:
                return
            key, val = dep
            if key == eng:
                if eng == "pe" or eng == "sp":
                    return
                if kind == "war":
                    return
            if val > waits.get(key, 0):
                waits[key] = val

        for b in reads:
            need(b.w, "raw")
        for b in writes:
            need(b.w, "waw")
            for r in b.r.values():
                need(r, "war")
        kn = self.known[eng]
        final = []
        for k, v in waits.items():
            if kn.get(k, 0) < v:
                kn[k] = v
                final.append((k, v))
        return final

    def op(self, eng, fn, reads=(), writes=()):
        waits = self._collect(eng, reads, writes)
        self.cnt[eng] += 1
        me = (eng, self.cnt[eng])
        for b in reads:
            b.r[eng] = me
        for b in writes:
            b.w = me
            b.r = {}
        self.ops[eng].append((waits, fn, (eng, 1)))

    def dma(self, q, sem, fn, reads=(), writes=()):
        self.dsem(sem)
        waits = self._collect(q, reads, writes)
        self.dcnt[sem] += 16
        me = (sem, self.dcnt[sem])
        for b in reads:
            b.r[sem] = me
        for b in writes:
            b.w = me
            b.r = {}
        self.ops[q].append((waits, fn, (sem, 16)))

    def dma_group(self, q, sem, fns, reads=(), writes=()):
        self.dsem(sem)
        waits = self._collect(q, reads, writes)
        for idx, fn in enumerate(fns):
            self.dcnt[sem] += 16
            self.ops[q].append((waits if idx == 0 else [], fn, (sem, 16)))
        me = (sem, self.dcnt[sem])
        for b in reads:
            b.r[sem] = me
        for b in writes:
            b.w = me
            b.r = {}

    def emit(self):
        nc = self.nc
        fin = [(k, v) for k, v in self.dcnt.items() if self.known["sp"].get(k, 0) < v]
        for k, v in fin:
            self.known["sp"][k] = v
        self.ops["sp"].append((fin, None, None))
        with nc.Block() as block:
            decos = {"pe": block.tensor, "act": block.scalar, "dve": block.vector,
                     "pool": block.gpsimd, "sp": block.sync}
            for eng in ENG:
                ops = self.ops[eng]
                if not ops:
                    continue

                def body(e, ops=ops):
                    for waits, fn, inc in ops:
                        for k, v in waits:
                            e.wait_ge(self.h[k], v)
                        if fn is not None:
                            inst = fn(e)
                            inst.then_inc(self.h[inc[0]], inc[1])

                decos[eng](body)
        self.ops = {e: [] for e in ENG}
        for e in ENG:
            for e2 in ENG:
                self.known[e][e2] = self.cnt[e2]


class Ps:
    def __init__(self, es, nc, n, tag):
        self.t = [es.enter_context(nc.psum_tensor(f"ps_{tag}{i}", [128, 512], F32)) for i in range(n)]
        self.b = [Buf(f"ps{i}") for i in range(n)]
        self.i = 0

    def get(self):
        i = self.i
        self.i = (i + 1) % len(self.t)
        return self.t[i], self.b[i]


class FeQ:
    def __init__(self, tasks):
        self.q = tasks
        self.pos = [0] * len(tasks)

    def step(self, i, n=1):
        if i >= len(self.q):
            return
        for _ in range(n):
            if self.pos[i] < len(self.q[i]):
                self.q[i][self.pos[i]]()
                self.pos[i] += 1

    def flush(self, i):
        self.step(i, 99)


def build(cfg):
    SEQS = cfg["seqs"]
    NTOK = cfg["ntok"]
    L = cfg["L"]
    NTT = cfg["ntiles"]
    dbg = cfg.get("dbg")
    nc = bass.Bass("TRN2", target_bir_lowering=False)

    def din(name, shape, dt=F32):
        return nc.dram_tensor(name, list(shape), dt, kind="ExternalInput").ap()

    def dscr(name, shape, dt):
        return nc.dram_tensor(name, list(shape), dt, kind="Internal").ap()

    x_in = din("x", [NTOK, D])
    y_out = nc.dram_tensor("y", [NTOK, D], F32, kind="ExternalOutput").ap()
    W = {}
    wshapes = {
        "w_in": [L, D, DIN], "w_gate": [L, 4, D, D], "a_proj": [L, 384, D], "b_proj": [L, 384, D],
        "cpk": [L, 512, D], "d_proj": [L, 384, D], "w_out": [L, D, D], "f_up": [L, D, 2 * DFF],
        "f_down": [L, DFF, D], "wst": [L, 128, 4 * 128], "dwg": [L, 96, 4 * 96], "bsr": [L, 1, 512],
    }
    WB = {}
    for k, shp in wshapes.items():
        W[k] = din(k, shp)
        WB[k] = dscr(k + "_bf", shp, BF)
    pp_in = din("pp", [L, 128, NPP])
    g1_in = din("g1", [L, 1, D])
    g2_in = din("g2", [L, 1, D])
    gf_in = din("gf", [1, D])
    blg_in = din("blg", [L, 1, 384])
    blb_in = din("blb", [L, 1, 384])
    cs96_in = din("cs96", [96, 98], BF)
    ident_in = din("ident", [128, 128], BF)
    ones_in = din("ones", [128, 128])
    hmask_in = din("hmask", [32, NTT])
    invc_in = din("invc", [4, NTOK])
    dft_in = {}
    for (off, S_, dn, tb) in SEQS:
        if dn not in dft_in:
            dft_in[dn] = din("dft" + dn, [2, S_, S_], BF)
    x1_d = dscr("x1_scr", [NTOK, D], F32)
    x2_d = dscr("x2_scr", [NTOK, D], F32)
    F_d = dscr("F_scr", [4, 128, NTOK], BF)
    dbg_out = {}
    if dbg:
        for nm, shp in dbg.items():
            dbg_out[nm] = nc.dram_tensor("dbg_" + nm, list(shp), F32, kind="ExternalOutput").ap()

    with ExitStack() as ges:
        S = Sched(nc, ges)

        wcastb = Buf("wcast")
        for k, shp in wshapes.items():
            src = W[k]
            dst = WB[k]
            if len(shp) == 4:
                src = src.rearrange("l i r c -> (l i r) c")
                dst = dst.rearrange("l i r c -> (l i r) c")
            else:
                src = src.rearrange("l r c -> (l r) c")
                dst = dst.rearrange("l r c -> (l r) c")
            R = src.shape[0]
            r0 = 0
            semn = "wc0" if k == "w_in" else "wc"
            while r0 < R:
                rr = min(128, R - r0)
                S.dma("pool", semn, lambda e, s=src[r0:r0 + rr, :], d=dst[r0:r0 + rr, :]:
                      e.dma_start(out=d, in_=s), writes=[wcastb] if k == "w_in" and r0 + rr >= R else [])
                r0 += rr

        class Front:
            def __init__(self, es, tag, g_src, pp):
                sbt = lambda n, s, d: es.enter_context(nc.sbuf_tensor(f"{tag}_{n}", s, d))
                self.tag = tag
                self.xs = [sbt(f"xs{i}", [128, D], F32) for i in range(2)]
                self.xsb = [Buf("xs") for _ in range(2)]
                self.ss = [sbt(f"ss{i}", [128, 1], F32) for i in range(2)]
                self.rs = [sbt(f"rs{i}", [128, 1], F32) for i in range(2)]
                self.ssb = [Buf("ss") for _ in range(2)]
                self.hb = [sbt(f"hb{i}", [128, D], BF) for i in range(2)]
                self.hbb = [Buf("hb") for _ in range(2)]
                self.pp = pp
                self.negh = sbt("negh", [128, 1], F32)
                self.neghb = Buf("negh")
                S.op("pool", lambda e: e.memset(self.negh[:], -0.5), writes=[self.neghb])
                self.junk = sbt("junk", [128, D], BF)
                self.junkb = Buf("junk")
                self.gbc = sbt("gbc", [128, D], F32)
                self.gbcb = Buf("gbc")
                self.ident = sbt("ident", [128, 128], BF)
                self.identb = Buf("ident")
                self.hm = sbt("hm", [32, NTT], F32)
                self.hmb = Buf("hm")
                self.i = 0
                S.dma_group("sp", "par", [
                    lambda e: e.dma_start(out=self.gbc[:], in_=g_src.broadcast_to([128, D])),
                    lambda e: e.dma_start(out=self.ident[:], in_=ident_in[:, :]),
                    lambda e: e.dma_start(out=self.hm[:], in_=hmask_in[:, :])],
                    writes=[self.gbcb, self.identb, self.hmb])

            def new(self, srcs, dst, dstb, npart=128, zero=False, mask_col=None):
                s = self.i % 2
                self.i += 1
                return dict(s=s, srcs=srcs, dst=dst, dstb=dstb, P=npart, zero=zero, mask=mask_col)

            def load(self, c):
                s, P = c["s"], c["P"]
                xs, xsb = self.xs[s], self.xsb[s]
                if c["zero"]:
                    S.op("pool", lambda e: e.memset(xs[0:P, :], 0.0), writes=[xsb])
                if c["srcs"]:
                    S.dma_group("sp", f"fxs{s}",
                                [(lambda e, r0=r0, nr=nr, ap=ap: e.dma_start(out=xs[r0:r0 + nr, :], in_=ap))
                                 for (r0, nr, ap) in c["srcs"]], writes=[xsb])

            def norm_a(self, c):
                s, P = c["s"], c["P"]
                xs, xsb, ss, rs, ssb, hb, hbb = (self.xs[s], self.xsb[s], self.ss[s], self.rs[s], self.ssb[s],
                                                self.hb[s], self.hbb[s])
                S.op("act", lambda e: e.activation(out=self.junk[0:P, :], in_=xs[0:P, :], func=AF.Square,
                                                   accum_out=ss[0:P, :]),
                     reads=[xsb], writes=[self.junkb, ssb])
                S.op("dve", lambda e: e.tensor_scalar(out=rs[0:P, :], in0=ss[0:P, :], scalar1=1.0 / D, scalar2=EPS,
                                                      op0=ALU.mult, op1=ALU.add), reads=[ssb], writes=[ssb])

            def norm_b(self, c):
                s, P = c["s"], c["P"]
                xs, xsb, ss, rs, ssb, hb, hbb = (self.xs[s], self.xsb[s], self.ss[s], self.rs[s], self.ssb[s],
                                                self.hb[s], self.hbb[s])
                S.op("pool", lambda e: e.tensor_tensor(out=rs[0:P, :], in0=rs[0:P, :], in1=self.negh[0:P, :], op=ALU.pow),
                     reads=[ssb, self.neghb], writes=[ssb])
                if c["mask"] is not None:
                    mc = c["mask"]
                    S.op("dve", lambda e: e.tensor_tensor(out=rs[0:P, :], in0=rs[0:P, :],
                                                          in1=self.hm[0:P, mc:mc + 1], op=ALU.mult),
                         reads=[ssb, self.hmb], writes=[ssb])
                S.op("dve", lambda e: e.scalar_tensor_tensor(out=hb[0:P, :], in0=xs[0:P, :], scalar=rs[0:P, 0:1],
                                                             in1=self.gbc[0:P, :], op0=ALU.mult, op1=ALU.mult),
                     reads=[xsb, ssb, self.gbcb], writes=[hbb])

            def trans(self, c):
                s, P = c["s"], c["P"]
                hb, hbb = self.hb[s], self.hbb[s]
                dst, dstb = c["dst"], c["dstb"]
                ps, tpb = self.pp.get()
                tp = ps[:, :].bitcast(BF).rearrange("p (k t) -> p k t", k=8)

                def tr(e):
                    ins = None
                    for k in range(8):
                        ins = e.transpose(out=tp[:, k, 0:P], in_=hb[0:P, k * 128:(k + 1) * 128],
                                          identity=self.ident[0:P, 0:P])
                    return ins

                S.op("pe", tr, reads=[hbb, self.identb], writes=[tpb])
                S.op("act", lambda e: e.copy(out=dst, in_=tp[:, :, 0:P]), reads=[tpb], writes=[dstb])

            def tasks(self, chunks):
                n = len(chunks)

                def first():
                    for c in chunks[0:2]:
                        self.load(c)
                    self.norm_a(chunks[0])

                out = [first]
                for st in range(n + 1):
                    def t(st=st):
                        if 1 <= st <= n:
                            self.trans(chunks[st - 1])
                        if st < n:
                            self.norm_b(chunks[st])
                        if st + 2 < n:
                            self.load(chunks[st + 2])
                        if st + 1 < n:
                            self.norm_a(chunks[st + 1])
                    out.append(t)
                return out

        def load_piece(ring, ringb, slot, parts, sem):
            S.dma_group("sp", sem, [(lambda e, d_=d_, s_=s_: e.dma_start(out=d_, in_=s_)) for d_, s_ in parts],
                        writes=[ringb[slot]])

        def wview(name, l, rows0, nrows, c0, c1, i=None):
            w = WB[name]
            if i is not None:
                w2 = w[l, i]
            else:
                w2 = w[l]
            return w2[rows0:rows0 + nrows, c0:c1]

        uid = [0]

        def pass0(l, seq):
            uid[0] += 1
            U = f"u{uid[0]}"
            off, SL, dn, tb = seq
            NT = SL // T
            NCH = SL // 128
            xsrc = x_in if l == 0 else x2_d
            with ExitStack() as es:
                sbt = lambda n, s, d: es.enter_context(nc.sbuf_tensor(f"{U}p0_{n}", s, d))
                pp = Ps(es, nc, 8, U + "p0")
                fr = Front(es, U + "p0", g1_in[l], pp)
                hT = [sbt(f"hT{i}", [128, 8, T], BF) for i in range(2)]
                hTb = [Buf("hT") for _ in range(2)]
                WC = sbt("WC", [128, 8, 384], BF)
                WCb = Buf("WC")
                CS = sbt("CS", [96, 98], BF)
                CSb = Buf("CS")
                zc = sbt("zc", [96, 4, T], BF)
                zcb = [Buf("zc") for _ in range(4)]
                ZCS = sbt("ZCS", [128, NCH, 392], BF)
                ZCSb = [[Buf("zcs")] for _ in range(NCH)]
                Bsb = sbt("Bsb", [128, 2, T], F32)
                Bsbb = [Buf("Bsb") for _ in range(2)]
                ring = [sbt(f"ring{i}", [128, 8, 2, T], BF) for i in range(3)]
                ringb = [Buf("ring") for _ in range(3)]
                Fsb = [sbt(f"Fsb{i}", [128, 4, T], BF) for i in range(2)]
                Fsbb = [[Buf("Fsb") for _ in range(4)] for _ in range(2)]
                S.dma("sp", "par", lambda e: e.dma_start(
                    out=WC[:], in_=WB["w_in"][l][:, OFF_C:OFF_D].rearrange("(k p) c -> p k c", p=128)),
                    reads=[wcastb], writes=[WCb])
                S.dma("sp", "par", lambda e: e.dma_start(out=CS[:], in_=cs96_in[:, :]), writes=[CSb])
                def fe_tasks0(i):
                    s = i % 2
                    t0 = off + i * T
                    return fr.tasks([fr.new([(0, 128, xsrc[t0 + j * 128:t0 + (j + 1) * 128, :])],
                                            hT[s][:, :, j * 128:(j + 1) * 128], hTb[s]) for j in range(4)])

                feq = FeQ([fe_tasks0(i) for i in range(NT)])
                feq.flush(0)
                for i in range(NT):
                    s = i % 2
                    t0 = off + i * T
                    for g in range(4):
                        feq.step(i + 1)
                        ps, pb = pp.get()

                        def mm(e, ps=ps, g=g, s=s):
                            ins = None
                            for k in range(8):
                                ins = e.matmul(ps[0:96, :], lhsT=WC[:, k, g * 96:(g + 1) * 96], rhs=hT[s][:, k, :],
                                               start=(k == 0), stop=(k == 7))
                            return ins

                        S.op("pe", mm, reads=[hTb[s], WCb], writes=[pb])
                        S.op("act", lambda e, ps=ps, g=g: e.copy(out=zc[:, g, :], in_=ps[0:96, :]),
                             reads=[pb], writes=[zcb[g]])
                    for j in range(4):
                        feq.step(i + 1)
                        ch = i * 4 + j
                        psA, pbA = pp.get()

                        def mm2(e, psA=psA, j=j):
                            ins = None
                            for g in range(4):
                                ins = e.matmul(psA[:, g * 98:(g + 1) * 98],
                                               lhsT=zc[:, g, j * 128:(j + 1) * 128], rhs=CS[:, :],
                                               start=True, stop=True)
                            return ins

                        S.op("pe", mm2, reads=zcb + [CSb], writes=[pbA])
                        if j % 2 == 0:
                            S.op("dve", lambda e, psA=psA, ch=ch: e.tensor_copy(
                                out=ZCS[:, ch, :].rearrange("p (cs g c) -> p g cs c", cs=2, g=4),
                                in_=psA[:, 0:392].rearrange("p (g cs c) -> p g cs c", g=4, cs=2)),
                                 reads=[pbA], writes=[ZCSb[ch][0]])
                        else:
                            S.op("act", lambda e, psA=psA, ch=ch: e.copy(
                                out=ZCS[:, ch, :].rearrange("p (cs g c) -> p g cs c", cs=2, g=4),
                                in_=psA[:, 0:392].rearrange("p (g cs c) -> p g cs c", g=4, cs=2)),
                                 reads=[pbA], writes=[ZCSb[ch][0]])
                dft = dft_in[dn]
                NPC = NCH // 8
                pieces = [(i, pc) for i in range(NT) for pc in range(NPC)]

                def issue(idx):
                    i, pc = pieces[idx]
                    slot = idx % 3
                    parts = []
                    for cs in range(2):
                        parts.append((ring[slot][:, :, cs, :],
                                      dft[cs, pc * 1024:(pc + 1) * 1024, i * T:(i + 1) * T].rearrange(
                                          "(ch q) p -> q ch p", q=128)))
                    load_piece(ring, ringb, slot, parts, f"rg{slot}")

                done = set()
                nxt = {"n": 0}

                def pref():
                    while nxt["n"] < len(pieces) and (nxt["n"] < 3 or (nxt["n"] - 3) in done):
                        issue(nxt["n"])
                        nxt["n"] += 1

                Fps = None
                for idx, (i, pc) in enumerate(pieces):
                    pref()
                    slot = idx % 3
                    if pc == 0:
                        Fps = [pp.get() for _ in range(4)]

                    def mm3(e, Fps=Fps, pc=pc, slot=slot):
                        ins = None
                        for c8 in range(8):
                            ch = pc * 8 + c8
                            for cs in range(2):
                                for g in range(2):
                                    msz = 128 if g == 0 else 68
                                    ins = e.matmul(Fps[cs * 2 + g][0][0:msz, :],
                                                   lhsT=ZCS[:, ch, cs * 196 + g * 128:cs * 196 + g * 128 + msz],
                                                   rhs=ring[slot][:, c8, cs, :],
                                                   start=(ch == 0), stop=(ch == NCH - 1))
                        return ins

                    S.op("pe", mm3, reads=[ringb[slot]] + [b_ for cb_ in ZCSb[pc * 8:(pc + 1) * 8] for b_ in cb_],
                         writes=[p[1] for p in Fps])
                    done.add(idx)
                    if pc == NPC - 1:
                        fs = i % 2
                        for g in range(2):
                            msz = 128 if g == 0 else 68
                            S.op("act", lambda e, g=g, Fps=Fps, msz=msz: e.copy(out=Bsb[0:msz, g, :], in_=Fps[2 + g][0][0:msz, :]),
                                 reads=[Fps[2 + g][1]], writes=[Bsbb[g]])
                            S.op("dve", lambda e, g=g, Fps=Fps, fs=fs, msz=msz: e.tensor_tensor(
                                out=Fsb[fs][0:msz, g, :], in0=Fps[g][0][0:msz, :], in1=Bsb[0:msz, g, :], op=ALU.add),
                                reads=[Fps[g][1], Bsbb[g]], writes=[Fsbb[fs][g]])
                            S.op("dve", lambda e, g=g, Fps=Fps, fs=fs, msz=msz: e.tensor_tensor(
                                out=Fsb[fs][0:msz, 2 + g, :], in0=Fps[g][0][0:msz, :], in1=Bsb[0:msz, g, :], op=ALU.subtract),
                                reads=[Fps[g][1], Bsbb[g]], writes=[Fsbb[fs][2 + g]])
                        t0 = off + i * T
                        S.dma("pool", f"fst{fs}", lambda e, fs=fs, t0=t0: e.dma_start(
                            out=F_d[:, :, t0:t0 + T].rearrange("g c t -> c g t"), in_=Fsb[fs][:, :, :]),
                            reads=Fsbb[fs])
                S.emit()

        def pass1(l, seq):
            uid[0] += 1
            U = f"u{uid[0]}"
            off, SL, dn, tb = seq
            NT = SL // T
            xsrc = x_in if l == 0 else x2_d
            with ExitStack() as es:
                sbt = lambda n, s, d: es.enter_context(nc.sbuf_tensor(f"{U}p1_{n}", s, d))
                pp = Ps(es, nc, 8, U + "p1")
                fr = Front(es, U + "p1", g1_in[l], pp)
                HW_ = 32 + T
                hT = [sbt(f"hT{i}", [128, 8, HW_], BF) for i in range(2)]
                hTb = [Buf("hT") for _ in range(2)]
                hThb = [Buf("hTh") for _ in range(2)]
                NSL = 5
                ring = [sbt(f"ring{i}", [128, 4096], BF) for i in range(NSL)]
                ringb = [Buf("ring") for _ in range(NSL)]
                PPt = sbt("pp", [128, NPP], F32)
                PPb = Buf("pp")
                ones = sbt("ones", [128, 128], F32)
                onesb = Buf("ones")
                WsT = sbt("WsT", [128, 4, 128], BF)
                dwg = sbt("dwg", [96, 4, 96], BF)
                bsr = sbt("bsr", [1, 512], BF)
                one1 = sbt("one1", [1, 96], BF)
                smallb = Buf("small")
                Gb = sbt("Gb", [128, 384], F32)
                Bb = sbt("Bb", [128, 384], F32)
                sg = [sbt(f"sg{i}", [128, T], F32) for i in range(2)]
                sgb = [Buf("sg") for _ in range(2)]
                sgh = sbt("sgh", [128, 32], F32)
                sghb = Buf("sgh")
                u = sbt("u", [128, 3, T + 32], BF)
                mg = sbt("mg", [128, 8, T], F32)
                mgb = [Buf("mg") for _ in range(8)]
                dg = sbt("dg", [128, 48, 128], BF)
                dgb = Buf("dg")
                ub_ = [Buf("u") for _ in range(3)]
                acc = sbt("acc", [128, 3, T], F32)
                accb = [Buf("acc") for _ in range(3)]
                lnm2_t = sbt("lnm2", [128, T], F32)
                lnrs_t = sbt("lnrs", [128, T], F32)
                lnm2 = lnm2_t[:, :]
                lnrs = lnrs_t[:, :]
                lnm2b = Buf("lnm2")
                lnrsb = Buf("lnrs")
                lnt = [sbt(f"lnt{i}", [128, T], F32) for i in range(2)]
                lntb = [Buf("lnt") for _ in range(2)]
                cbf = sbt("cbf", [128, 3, T], BF)
                cbfb = [Buf("cbf") for _ in range(3)]
                vg = [sbt(f"vg{i}", [128, 384], F32) for i in range(2)]
                vgb = [Buf("vg") for _ in range(2)]
                bst = [sbt(f"bst{i}", [128, 6], F32) for i in range(2)]
                bmv = [sbt(f"bmv{i}", [128, 2], F32) for i in range(2)]
                bstb = [Buf("bst") for _ in range(2)]
                vn = sbt("vn", [128, 384], F32)
                vnb_ = Buf("vn")
                vnb = sbt("vnb", [128, 4, 384], BF)
                vnbb = [Buf("vnb") for _ in range(4)]
                ug = sbt("ug", [96, 4, T], BF)
                ugb = [Buf("ug") for _ in range(4)]
                usv = sbt("usv", [96, 4, T], BF)
                usvb = [Buf("usv") for _ in range(4)]
                fsb = sbt("fsb", [128, 4, T], BF)
                fsbb = Buf("fsb")
                zd2 = [sbt(f"zd{i}", [96, T + 32], F32) for i in range(2)]
                zdb2 = [Buf("zd") for _ in range(2)]
                sa = sbt("sa", [96, T + 32], F32)
                sbb_ = sbt("sb", [96, T + 32], F32)
                sab = Buf("sa")
                sbb = Buf("sb")
                invc = [sbt(f"invc{i}", [96, T], F32) for i in range(4)]
                invcb = [Buf("invc") for _ in range(4)]
                diff = sbt("diff", [96, 4, T], BF)
                diffb = [Buf("diff") for _ in range(4)]
                yd = sbt("yd", [96, 4, T], BF)
                ydb = [Buf("yd") for _ in range(4)]
                mb = sbt("mb", [128, 8, T], BF)
                mbb = [Buf("mb") for _ in range(8)]
                tmp = [sbt(f"tmp{i}", [128, T], F32) for i in range(2)]
                tmpb = [Buf("tmp") for _ in range(2)]
                gs = [sbt(f"gs{i}", [128, T], F32) for i in range(2)]
                gsb = [Buf("gs") for _ in range(2)]
                xr = [sbt(f"xr{i}", [128, D], F32) for i in range(2)]
                xrb = [Buf("xr") for _ in range(2)]
                sqs = [lnt[0], lnt[1], tmp[1]]
                sqb = [lntb[0], lntb[1], tmpb[1]]
                cnt = {"tmp": 0, "gs": 0, "xr": 0, "sg": 0, "lnt": 0}

                par = [(PPt[:], pp_in[l]), (ones[:], ones_in[:, :]),
                       (WsT[:], WB["wst"][l].rearrange("q (h p) -> q h p", h=4)),
                       (dwg[:], WB["dwg"][l].rearrange("c (g d) -> c g d", g=4)),
                       (bsr[:], WB["bsr"][l]),
                       (Gb[:], blg_in[l].broadcast_to([128, 384])),
                       (Bb[:], blb_in[l].broadcast_to([128, 384]))]
                one1b = Buf("one1b")
                S.dma_group("sp", "par", [(lambda e, d_=d_, s_=s_: e.dma_start(out=d_, in_=s_)) for d_, s_ in par],
                            writes=[smallb, PPb, onesb])
                S.op("pool", lambda e: e.memset(one1[:], 1.0), writes=[one1b])
                for m_ in range(3):
                    for k_ in range(16):
                        S.op("dve", lambda e, m_=m_, k_=k_: e.tensor_scalar(
                            out=dg[:, m_ * 16 + k_, :], in0=fr.ident[:, :],
                            scalar1=PPt[:, PP_CW + m_ * 31 + k_:PP_CW + m_ * 31 + k_ + 1], scalar2=None,
                            op0=ALU.mult), reads=[fr.identb, PPb] + ([dgb] if (m_ + k_) > 0 else []), writes=[dgb])

                def piece_list():
                    wi = WB["w_in"][l]

                    def cols(c0, n):
                        return [(lambda slot, c0=c0, n=n: ring[slot][:, 0:8 * n].rearrange("p (k c) -> p k c", k=8),
                                 wi[:, c0:c0 + n].rearrange("(k p) c -> p k c", p=128))]

                    pcs = []
                    pcs.append(("WBv", cols(OFF_B + 384, 384)))
                    pcs.append(("WD", cols(OFF_D, 384)))
                    pcs.append(("WAa", cols(OFF_A, 384)))
                    pcs.append(("WAg", cols(OFF_A + 384, 384)))
                    pcs.append(("WBu", cols(OFF_B, 384)))
                    for br, pj, kk, kp in ((2, "cpk", 4, 128), (3, "d_proj", 4, 96), (1, "b_proj", 4, 96),
                                           (0, "a_proj", 3, 128)):
                        for hf in range(2):
                            pcs.append((f"Wg{br}{hf}", [(
                                lambda slot: ring[slot][:, 0:4096].rearrange("p (k c) -> p k c", k=8),
                                WB["w_gate"][l, br][:, hf * 512:(hf + 1) * 512].rearrange("(k p) c -> p k c", p=128))]))
                        pcs.append((f"Pj{br}", [(
                            lambda slot, kk=kk, kp=kp: ring[slot][0:kp, 0:kk * 1024].rearrange("p (k c) -> p k c", k=kk),
                            WB[pj][l].rearrange("(k p) c -> p k c", p=kp))]))
                    for hf in range(2):
                        pcs.append((f"Wo{hf}", [(
                            lambda slot: ring[slot][:, 0:4096].rearrange("p (k c) -> p k c", k=8),
                            WB["w_out"][l][:, hf * 512:(hf + 1) * 512].rearrange("(k p) c -> p k c", p=128))]))
                    return pcs

                allp = []
                for i in range(NT):
                    for nm, parts in piece_list():
                        allp.append((i, nm, parts))
                pstate = {"next": 0}
                pslot = {}
                pdone = set()
                pidx_of = {}
                for idx_, (i_, nm_, _) in enumerate(allp):
                    pidx_of[(i_, nm_)] = idx_

                def prefetch():
                    while pstate["next"] < len(allp) and (pstate["next"] < NSL or (pstate["next"] - NSL) in pdone):
                        idx = pstate["next"]
                        slot = idx % NSL
                        i_, nm, parts = allp[idx]
                        load_piece(ring, ringb, slot, [(dfn(slot), s_) for dfn, s_ in parts], f"rg{slot}")
                        pslot[(i_, nm)] = slot
                        pstate["next"] += 1

                def use(i, nm):
                    prefetch()
                    return pslot[(i, nm)]

                def rel(i, *nms):
                    for nm in nms:
                        pdone.add(pidx_of[(i, nm)])
                    prefetch()

                def fe_tasks1(i, src_d):
                    s = i % 2
                    t0 = off + i * T
                    tile_id = tb + i
                    srcs = []
                    if i > 0:
                        srcs.append((0, 16, src_d[t0 - 16:t0, :]))
                    if i < NT - 1:
                        srcs.append((16, 16, src_d[t0 + T:t0 + T + 16, :]))
                    chunks = [fr.new(srcs, hT[s][:, :, 0:32], hThb[s], npart=32, zero=(len(srcs) < 2), mask_col=tile_id)]
                    for j in range(4):
                        chunks.append(fr.new([(0, 128, src_d[t0 + j * 128:t0 + (j + 1) * 128, :])],
                                             hT[s][:, :, 32 + j * 128:32 + (j + 1) * 128], hTb[s]))
                    return fr.tasks(chunks)

                def load_F(i):
                    if i >= NT:
                        return
                    t0_ = off + i * T
                    S.dma("pool", "fl", lambda e: e.dma_start(
                        out=fsb[:, :, :], in_=F_d[:, :, t0_:t0_ + T].rearrange("g c t -> c g t")), writes=[fsbb])

                def xr_load1(j, t0_):
                    q_ = j % 2
                    r0_ = t0_ + j * 128
                    S.dma("pool", f"xr{q_}", lambda e: e.dma_start(out=xr[q_][:], in_=xsrc[r0_:r0_ + 128, :]),
                          writes=[xrb[q_]])

                feq = FeQ([fe_tasks1(i, xsrc) for i in range(NT)])
                feq.flush(0)
                load_F(0)
                for i in range(NT):
                    s = i % 2
                    t0 = off + i * T
                    tile_id = tb + i
                    hc = hT[s]

                    for g_ in range(4):
                        S.dma("pool", f"ic{g_}", lambda e, t0=t0, g_=g_: e.dma_start(
                            out=invc[g_][:, :], in_=invc_in[g_:g_ + 1, t0:t0 + T].broadcast_to([96, T])),
                            writes=[invcb[g_]])
                    slot = use(i, "WBv")
                    wv = ring[slot][:, 0:8 * 384].rearrange("p (k c) -> p k c", k=8)

                    def bv_a(j, slot=slot, wv=wv, hc=hc, s=s):
                        ps, pb = pp.get()
                        q = j % 2

                        def mmv(e):
                            ins = None
                            for k in range(8):
                                ins = e.matmul(ps[:, 0:384], lhsT=hc[:, k, 32 + j * 128:32 + (j + 1) * 128],
                                               rhs=wv[:, k, :], start=(k == 0), stop=(k == 7))
                            return ins

                        S.op("pe", mmv, reads=[hTb[s], ringb[slot]], writes=[pb])
                        S.op("act", lambda e: e.activation(out=vg[q][:], in_=ps[:, 0:384], func=AF.Gelu_apprx_tanh),
                             reads=[pb], writes=[vgb[q]])
                        S.op("dve", lambda e: e.bn_stats(out=bst[q][:], in_=vg[q][:]), reads=[vgb[q]], writes=[bstb[q]])
                        S.op("dve", lambda e: e.bn_aggr(out=bmv[q][:], in_=bst[q][:]), reads=[bstb[q]], writes=[bstb[q]])
                        S.op("dve", lambda e: e.tensor_scalar(out=bmv[q][:, 1:2], in0=bmv[q][:, 1:2], scalar1=EPS, scalar2=None,
                                                              op0=ALU.add), reads=[bstb[q]], writes=[bstb[q]])

                    def bv_b(j):
                        q = j % 2
                        S.op("pool", lambda e: e.tensor_tensor(out=bmv[q][:, 1:2], in0=bmv[q][:, 1:2], in1=fr.negh[:, :], op=ALU.pow),
                             reads=[bstb[q], fr.neghb], writes=[bstb[q]])
                        S.op("dve", lambda e: e.tensor_scalar(out=vn[:], in0=vg[q][:], scalar1=bmv[q][:, 0:1], scalar2=bmv[q][:, 1:2],
                                                              op0=ALU.subtract, op1=ALU.mult),
                             reads=[vgb[q], bstb[q]], writes=[vnb_])
                        S.op("pool", lambda e: e.tensor_tensor(out=vn[:], in0=vn[:], in1=Gb[:], op=ALU.mult),
                             reads=[vnb_, smallb], writes=[vnb_])
                        S.op("pool", lambda e: e.tensor_tensor(out=vnb[:, j, :], in0=vn[:], in1=Bb[:], op=ALU.add),
                             reads=[vnb_, smallb], writes=[vnbb[j]])

                    bv_a(0)
                    bv_a(1)
                    bv_b(0)
                    bv_a(2)
                    bv_b(1)
                    bv_a(3)
                    bv_b(2)
                    bv_b(3)

                    feq.step(i + 1)
                    rel(i, "WBv")
                    slot = use(i, "WD")
                    slotD = slot
                    wd = ring[slot][:, 0:8 * 384].rearrange("p (k c) -> p k c", k=8)
                    slotA = use(i, "WAa")
                    slotG = use(i, "WAg")
                    wa = ring[slotA][:, 0:8 * 384].rearrange("p (k c) -> p k c", k=8)
                    wg_ = ring[slotG][:, 0:8 * 384].rearrange("p (k c) -> p k c", k=8)

                    def d_group(g, wd=wd, hc=hc, s=s, t0=t0, slotD=slotD):
                        zd, zdb = zd2[g % 2], zdb2[g % 2]
                        ps, pb = pp.get()
                        psh, pbh = pp.get()

                        def mmd(e, ps=ps, psh=psh, g=g, wd=wd, hc=hc):
                            ins = None
                            for k in range(8):
                                ins = e.matmul(ps[0:96, :], lhsT=wd[:, k, g * 96:(g + 1) * 96], rhs=hc[:, k, 32:32 + T],
                                               start=(k == 0), stop=(k == 7))
                            for k in range(8):
                                ins = e.matmul(psh[0:96, 0:32], lhsT=wd[:, k, g * 96:(g + 1) * 96], rhs=hc[:, k, 0:32],
                                               start=(k == 0), stop=(k == 7))
                            return ins

                        S.op("pe", mmd, reads=[hTb[s], hThb[s], ringb[slotD]], writes=[pb, pbh])
                        S.op("act", lambda e, ps=ps, zd=zd: e.copy(out=zd[:, 16:16 + T], in_=ps[0:96, :]), reads=[pb], writes=[zdb])
                        S.op("act", lambda e, psh=psh, zd=zd: e.copy(out=zd[:, 0:16], in_=psh[0:96, 0:16]), reads=[pbh, zdb], writes=[zdb])
                        S.op("act", lambda e, psh=psh, zd=zd: e.copy(out=zd[:, 16 + T:32 + T], in_=psh[0:96, 16:32]),
                             reads=[pbh, zdb], writes=[zdb])
                        E = T + 32
                        S.op("dve", lambda e, zd=zd: e.tensor_tensor(out=sa[:, 1:E], in0=zd[:, 0:E - 1], in1=zd[:, 1:E], op=ALU.add),
                             reads=[zdb], writes=[sab])
                        cur, curb, oth, othb = sa, sab, sbb_, sbb
                        lo, hi = 1, E
                        sh = 1
                        for lev in range(g):
                            nlo, nhi = lo + sh, hi - sh
                            S.op("dve", lambda e, cur=cur, oth=oth, nlo=nlo, nhi=nhi, sh=sh: e.tensor_tensor(
                                out=oth[:, nlo:nhi], in0=cur[:, nlo - sh:nhi - sh], in1=cur[:, nlo + sh:nhi + sh], op=ALU.add),
                                reads=[curb], writes=[othb])
                            cur, curb, oth, othb = oth, othb, cur, curb
                            lo, hi = nlo, nhi
                            sh *= 2
                        S.op("dve", lambda e, cur=cur, oth=oth, g=g: e.tensor_tensor(
                            out=oth[:, 16:16 + T], in0=cur[:, 16:16 + T], in1=invc[g][:, :], op=ALU.mult),
                            reads=[curb, invcb[g]], writes=[othb])
                        S.op("dve", lambda e, oth=oth, g=g, zd=zd: e.tensor_tensor(
                            out=diff[:, g, :], in0=oth[:, 16:16 + T], in1=zd[:, 16:16 + T], op=ALU.subtract),
                            reads=[othb, zdb], writes=[diffb[g]])


                    def a_tile(m, wa=wa, wg_=wg_, hc=hc, s=s, slotA=slotA, slotG=slotG):
                        psa, pba = pp.get()
                        psg, pbg = pp.get()
                        psh, pbh = pp.get()

                        def mma(e, psa=psa, psg=psg, psh=psh, m=m, wa=wa, wg_=wg_, hc=hc):
                            ins = None
                            for k in range(8):
                                ins = e.matmul(psg[:, :], lhsT=wg_[:, k, m * 128:(m + 1) * 128], rhs=hc[:, k, 32:32 + T],
                                               start=(k == 0), stop=(k == 7))
                            for k in range(8):
                                ins = e.matmul(psa[:, :], lhsT=wa[:, k, m * 128:(m + 1) * 128], rhs=hc[:, k, 32:32 + T],
                                               start=(k == 0), stop=(k == 7))
                            for k in range(8):
                                ins = e.matmul(psh[:, 32:64], lhsT=wg_[:, k, m * 128:(m + 1) * 128], rhs=hc[:, k, 0:32],
                                               start=(k == 0), stop=(k == 7))
                            for k in range(8):
                                ins = e.matmul(psh[:, 0:32], lhsT=wa[:, k, m * 128:(m + 1) * 128], rhs=hc[:, k, 0:32],
                                               start=(k == 0), stop=(k == 7))
                            return ins

                        S.op("pe", mma, reads=[hTb[s], hThb[s], ringb[slotA], ringb[slotG]], writes=[pba, pbg, pbh])
                        q = cnt["sg"] % 2
                        cnt["sg"] += 1
                        S.op("act", lambda e, psg=psg, q=q: e.activation(out=sg[q][:], in_=psg[:, :], func=AF.Sigmoid),
                             reads=[pbg], writes=[sgb[q]])
                        S.op("act", lambda e, psh=psh: e.activation(out=sgh[:], in_=psh[:, 32:64], func=AF.Sigmoid),
                             reads=[pbh], writes=[sghb])
                        S.op("dve", lambda e, psa=psa, q=q, m=m: e.tensor_tensor(out=u[:, m, 16:16 + T], in0=psa[:, :], in1=sg[q][:],
                                                                               op=ALU.mult),
                             reads=[pba, sgb[q]], writes=[ub_[m]])
                        S.op("dve", lambda e, psh=psh, m=m: e.tensor_tensor(out=u[:, m, 0:16], in0=psh[:, 0:16], in1=sgh[:, 0:16],
                                                                          op=ALU.mult), reads=[pbh, sghb, ub_[m]], writes=[ub_[m]])
                        S.op("dve", lambda e, psh=psh, m=m: e.tensor_tensor(out=u[:, m, 16 + T:32 + T], in0=psh[:, 16:32],
                                                                          in1=sgh[:, 16:32], op=ALU.mult),
                             reads=[pbh, sghb, ub_[m]], writes=[ub_[m]])

                    d_group(0)
                    a_tile(0)
                    d_group(1)
                    feq.step(i + 1)
                    a_tile(1)
                    d_group(2)
                    a_tile(2)
                    d_group(3)
                    rel(i, "WD")
                    feq.step(i + 1)
                    rel(i, "WAa", "WAg")
                    for m in range(3):
                        psc, pbc = pp.get()

                        def mmc(e, psc=psc, m=m):
                            ins = None
                            for kk in range(16):
                                ins = e.matmul(psc[:, :], lhsT=dg[:, m * 16 + kk, :], rhs=u[:, m, 1 + kk:1 + kk + T],
                                               start=(kk == 0), stop=(kk == 15))
                            return ins

                        S.op("pe", mmc, reads=[ub_[m], dgb], writes=[pbc])
                        S.op("act", lambda e, psc=psc, m=m: e.activation(out=acc[:, m, :], in_=psc[:, :], func=AF.Identity,
                                                                       bias=PPt[:, PP_CB + m:PP_CB + m + 1]),
                             reads=[pbc, PPb], writes=[accb[m]])
                    drip = []
                    for kk in range(16, 31):
                        for m in range(3):
                            wcol = PPt[:, PP_CW + m * 31 + kk:PP_CW + m * 31 + kk + 1]
                            drip.append(lambda m=m, wcol=wcol, kk=kk: S.op("dve", lambda e: e.scalar_tensor_tensor(
                                out=acc[:, m, :], in0=u[:, m, 1 + kk:1 + kk + T], scalar=wcol, in1=acc[:, m, :],
                                op0=ALU.mult, op1=ALU.add), reads=[ub_[m], PPb, accb[m]], writes=[accb[m]]))

                    def ln_block():
                        for m in range(3):
                            S.op("act", lambda e, m=m: e.activation(out=sqs[m][:], in_=acc[:, m, :], func=AF.Square),
                                 reads=[accb[m]], writes=[sqb[m]])
                        ps1, pb1 = pp.get()
                        ps2, pb2 = pp.get()

                        def mmst(e, ps1=ps1, ps2=ps2):
                            ins = None
                            for m in range(3):
                                ins = e.matmul(ps1[:, :], lhsT=ones[:, :], rhs=acc[:, m, :], start=(m == 0), stop=(m == 2))
                            for m in range(3):
                                ins = e.matmul(ps2[:, :], lhsT=ones[:, :], rhs=sqs[m][:], start=(m == 0), stop=(m == 2))
                            return ins

                        S.op("pe", mmst, reads=accb + sqb + [onesb], writes=[pb1, pb2])
                        S.op("act", lambda e, ps1=ps1: e.activation(out=lnm2, in_=ps1[:, :], func=AF.Square),
                             reads=[pb1], writes=[lnm2b])
                        S.op("act", lambda e, ps1=ps1: e.copy(out=tmp[0][:], in_=ps1[:, :]), reads=[pb1], writes=[tmpb[0]])
                        S.op("dve", lambda e, ps2=ps2: e.scalar_tensor_tensor(out=lnrs, in0=ps2[:, :], scalar=EPS, in1=lnm2,
                                                                           op0=ALU.add, op1=ALU.subtract),
                             reads=[pb2, lnm2b], writes=[lnrsb])
                        S.op("act", lambda e: e.activation(out=lnrs, in_=lnrs, func=AF.Sqrt), reads=[lnrsb], writes=[lnrsb])
                        S.op("dve", lambda e: e.reciprocal(out=lnrs, in_=lnrs), reads=[lnrsb], writes=[lnrsb])
                        for m in range(3):
                            q = cnt["lnt"] % 2
                            cnt["lnt"] += 1
                            S.op("dve", lambda e, m=m, q=q: e.tensor_tensor(out=lnt[q][:], in0=acc[:, m, :], in1=tmp[0][:],
                                                                         op=ALU.subtract),
                                 reads=[accb[m], tmpb[0]], writes=[lntb[q]])
                            S.op("dve", lambda e, q=q: e.tensor_tensor(out=lnt[q][:], in0=lnt[q][:], in1=lnrs, op=ALU.mult),
                                 reads=[lntb[q], lnrsb], writes=[lntb[q]])
                            S.op("act", lambda e, m=m, q=q: e.activation(out=cbf[:, m, :], in_=lnt[q][:], func=AF.Silu,
                                                                       scale=PPt[:, PP_LG + m:PP_LG + m + 1],
                                                                       bias=PPt[:, PP_LB + m:PP_LB + m + 1]),
                                 reads=[lntb[q], PPb], writes=[cbfb[m]])


                    slot = use(i, "WBu")
                    wu = ring[slot][:, 0:8 * 384].rearrange("p (k c) -> p k c", k=8)
                    for hd in range(4):
                        ps, pb = pp.get()

                        def mmu(e, ps=ps, hd=hd, wu=wu, hc=hc):
                            ins = None
                            for k in range(8):
                                ins = e.matmul(ps[0:96, :], lhsT=wu[:, k, hd * 96:(hd + 1) * 96], rhs=hc[:, k, 32:32 + T],
                                               start=(k == 0), stop=(k == 7))
                            return ins

                        S.op("pe", mmu, reads=[hTb[s], ringb[slot]], writes=[pb])
                        S.op("act", lambda e, ps=ps, hd=hd: e.activation(out=ug[:, hd, :], in_=ps[0:96, :],
                                                                       func=AF.Gelu_apprx_tanh),
                             reads=[pb], writes=[ugb[hd]])
                        for _ in range(3):
                            if drip:
                                drip.pop(0)()
                    feq.step(i + 1)
                    rel(i, "WBu")
                    for hd in range(4):
                        ps, pb = pp.get()

                        def mms(e, ps=ps, hd=hd):
                            ins = None
                            for j in range(4):
                                ins = e.matmul(ps[0:96, j * 128:(j + 1) * 128], lhsT=vnb[:, j, hd * 96:(hd + 1) * 96],
                                               rhs=WsT[:, hd, :], start=True, stop=False)
                                ins = e.matmul(ps[0:96, j * 128:(j + 1) * 128], lhsT=one1[0:1, :],
                                               rhs=bsr[0:1, hd * 128:(hd + 1) * 128], start=False, stop=True)
                            return ins

                        S.op("pe", mms, reads=vnbb + [smallb, one1b], writes=[pb])
                        S.op("dve", lambda e, ps=ps, hd=hd: e.tensor_tensor(out=usv[:, hd, :], in0=ps[0:96, :], in1=ug[:, hd, :],
                                                                          op=ALU.mult),
                             reads=[pb, ugb[hd]], writes=[usvb[hd]])
                        for _ in range(3):
                            if drip:
                                drip.pop(0)()
                    for g in range(4):
                        ps, pb = pp.get()
                        S.op("pe", lambda e, ps=ps, g=g: e.matmul(ps[0:96, :], lhsT=dwg[:, g, :], rhs=diff[:, g, :],
                                                                   start=True, stop=True),
                             reads=[diffb[g], smallb], writes=[pb])
                        S.op("act", lambda e, ps=ps, g=g: e.activation(out=yd[:, g, :], in_=ps[0:96, :], func=AF.Copy,
                                                                     scale=PPt[0:96, PP_DS + g:PP_DS + g + 1]),
                             reads=[pb, PPb], writes=[ydb[g]])

                    first = True
                    for br, nm, kk, kp, srcs_ in ((2, "c", 4, 128, None), (3, "d", 4, 96, None), (1, "b", 4, 96, None),
                                                  (0, "a", 3, 128, None)):
                        feq.step(i + 1)
                        if br == 0:
                            xr_load1(0, t0)
                            xr_load1(1, t0)
                        sl0 = use(i, f"Wg{br}0")
                        sl1 = use(i, f"Wg{br}1")
                        slp = use(i, f"Pj{br}")
                        wgl = [ring[sl0][:, 0:4096].rearrange("p (k c) -> p k c", k=8),
                               ring[sl1][:, 0:4096].rearrange("p (k c) -> p k c", k=8)]
                        wpj = ring[slp][0:kp, 0:kk * 1024].rearrange("p (k c) -> p k c", k=kk)
                        ksz = [kp] * kk
                        if br == 2:
                            ksz = [128, 68, 128, 68]
                            rhs_of = lambda k_: fsb[0:(128 if k_ % 2 == 0 else 68), k_, :]
                            rbufs = [fsbb]
                        elif br == 3:
                            rhs_of = lambda k_: yd[:, k_, :]
                            rbufs = ydb
                        elif br == 1:
                            rhs_of = lambda k_: usv[:, k_, :]
                            rbufs = usvb
                        else:
                            rhs_of = lambda k_: cbf[:, k_, :]
                            rbufs = cbfb
                        for dt in range(8):
                            psg, pbg = pp.get()
                            psp, pbp = pp.get()
                            wgh = wgl[dt // 4]
                            slg = sl0 if dt < 4 else sl1

                            def mmg(e, psg=psg, psp=psp, dt=dt, wgh=wgh, wpj=wpj, rhs_of=rhs_of, kk=kk, hc=hc, ksz=ksz):
                                ins = None
                                for k in range(8):
                                    ins = e.matmul(psg[:, :], lhsT=wgh[:, k, (dt % 4) * 128:(dt % 4 + 1) * 128],
                                                   rhs=hc[:, k, 32:32 + T], start=(k == 0), stop=(k == 7))
                                for k in range(kk):
                                    ins = e.matmul(psp[:, :], lhsT=wpj[0:ksz[k], k, dt * 128:(dt + 1) * 128], rhs=rhs_of(k),
                                                   start=(k == 0), stop=(k == kk - 1))
                                return ins

                            S.op("pe", mmg, reads=[hTb[s], ringb[slg], ringb[slp]] + list(rbufs), writes=[pbg, pbp])
                            q = cnt["gs"] % 2
                            cnt["gs"] += 1
                            S.op("act", lambda e, psg=psg, q=q, br=br, dt=dt: e.activation(
                                out=gs[q][:], in_=psg[:, :], func=AF.Sigmoid,
                                bias=PPt[:, PP_BG + br * 8 + dt:PP_BG + br * 8 + dt + 1]),
                                reads=[pbg, PPb], writes=[gsb[q]])
                            if first:
                                S.op("dve", lambda e, psp=psp, q=q, dt=dt: e.tensor_tensor(out=mg[:, dt, :], in0=psp[:, :], in1=gs[q][:],
                                                                                          op=ALU.mult),
                                     reads=[pbp, gsb[q]], writes=[mgb[dt]])
                            else:
                                q2 = cnt["tmp"] % 2
                                cnt["tmp"] += 1
                                S.op("dve", lambda e, psp=psp, q=q, q2=q2: e.tensor_tensor(out=tmp[q2][:], in0=psp[:, :], in1=gs[q][:],
                                                                                        op=ALU.mult),
                                     reads=[pbp, gsb[q]], writes=[tmpb[q2]])
                                if br != 0:
                                    S.op("pool", lambda e, q2=q2, dt=dt: e.tensor_tensor(out=mg[:, dt, :], in0=mg[:, dt, :], in1=tmp[q2][:],
                                                                                        op=ALU.add),
                                         reads=[tmpb[q2], mgb[dt]], writes=[mgb[dt]])
                                else:
                                    S.op("pool", lambda e, q2=q2, dt=dt: e.tensor_tensor(out=mb[:, dt, :], in0=mg[:, dt, :], in1=tmp[q2][:],
                                                                                        op=ALU.add),
                                         reads=[tmpb[q2], mgb[dt]], writes=[mbb[dt]])
                            if br in (2, 3):
                                for _ in range(2 if br == 2 else 1):
                                    if drip:
                                        drip.pop(0)()
                            if br == 1 and dt == 1:
                                ln_block()
                        if br == 2:
                            load_F(i + 1)
                        rel(i, f"Wg{br}0", f"Wg{br}1", f"Pj{br}")
                        if br == 3:
                            while drip:
                                drip.pop(0)()
                        first = False

                    slo = [use(i, "Wo0"), use(i, "Wo1")]
                    wol = [ring[slo[0]][:, 0:4096].rearrange("p (k c) -> p k c", k=8),
                           ring[slo[1]][:, 0:4096].rearrange("p (k c) -> p k c", k=8)]
                    for j in range(4):
                        q = j % 2
                        r0 = t0 + j * 128
                        for hf in range(2):
                            ps, pb = pp.get()

                            def mmo(e, ps=ps, j=j, hf=hf, wol=wol):
                                ins = None
                                for k in range(8):
                                    ins = e.matmul(ps[:, :], lhsT=mb[:, k, j * 128:(j + 1) * 128], rhs=wol[hf][:, k, :],
                                                   start=(k == 0), stop=(k == 7))
                                return ins

                            S.op("pe", mmo, reads=mbb + [ringb[slo[hf]]], writes=[pb])
                            S.op("dve", lambda e, ps=ps, q=q, hf=hf: e.tensor_tensor(
                                out=xr[q][:, hf * 512:(hf + 1) * 512], in0=ps[:, :], in1=xr[q][:, hf * 512:(hf + 1) * 512],
                                op=ALU.add), reads=[pb, xrb[q]], writes=[xrb[q]])
                        S.dma("pool", f"xst{q}", lambda e, q=q, r0=r0: e.dma_start(out=x1_d[r0:r0 + 128, :], in_=xr[q][:]),
                              reads=[xrb[q]])
                        if j + 2 < 4:
                            xr_load1(j + 2, t0)
                    rel(i, "Wo0", "Wo1")
                    feq.flush(i + 1)
                S.emit()

        def pass2(l, seq):
            uid[0] += 1
            U = f"u{uid[0]}"
            off, SL, dn, tb = seq
            NT = SL // T
            last = (l == L - 1)
            xdst = y_out if last else x2_d
            with ExitStack() as es:
                sbt = lambda n, s, d: es.enter_context(nc.sbuf_tensor(f"{U}p2_{n}", s, d))
                pp = Ps(es, nc, 8, U + "p2")
                fr = Front(es, U + "p2", g2_in[l], pp)
                HW_ = 32 + T
                hT = [sbt(f"hT{i}", [128, 8, HW_], BF) for i in range(2)]
                hTb = [Buf("hT") for _ in range(2)]
                hThb = [Buf("hTh") for _ in range(2)]
                NSL = 4
                ring = [sbt(f"ring{i}", [128, 8, 2, 256], BF) for i in range(NSL)]
                ringb = [Buf("ring") for _ in range(NSL)]
                PPt = sbt("pp", [128, NPP], F32)
                PPb = Buf("pp")
                FD = sbt("FD", [128, NMT, D], BF)
                FDb = Buf("FD")
                gfb = sbt("gfb", [128, D], F32)
                gfbb = Buf("gfb")
                act_bf = [sbt(f"actbf{i}", [128, NMT, T], BF) for i in range(2)]
                actb = [[Buf("act") for _ in range(NMT)] for _ in range(2)]
                cg = [sbt(f"cg{i}", [128, T], F32) for i in range(3)]
                cv = [sbt(f"cv{i}", [128, T], F32) for i in range(3)]
                cgb = [Buf("cg") for _ in range(3)]
                cvb = [Buf("cv") for _ in range(3)]
                xr = [sbt(f"xr{i}", [128, D], F32) for i in range(2)]
                xrb = [Buf("xr") for _ in range(2)]
                fss = sbt("fss", [128, 1], F32)
                fssb = Buf("fss")
                fjunk = fr.junk
                fjunkb = fr.junkb
                cnt = {"xr": 0, "ub": 0}
                S.dma("sp", "par", lambda e: e.dma_start(out=PPt[:], in_=pp_in[l]), writes=[PPb])
                S.dma("sp", "par", lambda e: e.dma_start(out=gfb[:], in_=gf_in.broadcast_to([128, D])), writes=[gfbb])
                for c3 in range(0, NMT, 6):
                    n3 = min(6, NMT - c3)
                    S.dma("sp", "par", lambda e, c3=c3, n3=n3: e.dma_start(
                        out=FD[:, c3:c3 + n3, :],
                        in_=WB["f_down"][l][c3 * 128:(c3 + n3) * 128, :].rearrange("(k p) c -> p k c", p=128)),
                        writes=[FDb])
                FDb.w = ("par", S.dcnt["par"])
                PPb.w = FDb.w
                gfbb.w = FDb.w
                NPC = NMT // 2
                allp = [(i, r) for i in range(NT) for r in range(NPC)]
                pstate = {"next": 0}
                wu_ = WB["f_up"][l]

                pdone = set()

                def prefetch():
                    while pstate["next"] < len(allp) and (pstate["next"] < NSL or (pstate["next"] - NSL) in pdone):
                        idx = pstate["next"]
                        slot = idx % NSL
                        i_, r = allp[idx]
                        parts = []
                        for gv in range(2):
                            c0 = gv * DFF + r * 256
                            parts.append((ring[slot][:, :, gv, :], wu_[:, c0:c0 + 256].rearrange("(k p) c -> p k c", p=128)))
                        load_piece(ring, ringb, slot, parts, f"rg{slot}")
                        pstate["next"] += 1

                def fe_tasks2(i):
                    s = i % 2
                    t0 = off + i * T
                    tile_id = tb + i
                    srcs = []
                    if i > 0:
                        srcs.append((0, 16, x1_d[t0 - 16:t0, :]))
                    if i < NT - 1:
                        srcs.append((16, 16, x1_d[t0 + T:t0 + T + 16, :]))
                    chunks = [fr.new(srcs, hT[s][:, :, 0:32], hThb[s], npart=32, zero=(len(srcs) < 2), mask_col=tile_id)]
                    for j in range(4):
                        chunks.append(fr.new([(0, 128, x1_d[t0 + j * 128:t0 + (j + 1) * 128, :])],
                                             hT[s][:, :, 32 + j * 128:32 + (j + 1) * 128], hTb[s]))
                    return fr.tasks(chunks)

                feq = FeQ([fe_tasks2(i) for i in range(NT)])
                feq.flush(0)
                pist = {"pi": 0, "pend": None}

                def up_piece(i, r):
                    if True:
                        s = i % 2
                        hc = hT[s]
                        ab = act_bf[i % 2]
                        abb = actb[i % 2]
                        prefetch()
                        slot = pist["pi"] % NSL
                        pist["pi"] += 1
                        feq.step(i + 1)
                        for mm_ in range(2):
                            mt = 2 * r + mm_
                            psg, pbg = pp.get()
                            psv, pbv = pp.get()
                            psh, pbh = pp.get()

                            def mmup(e, psg=psg, psv=psv, psh=psh, mm_=mm_, slot=slot, hc=hc):
                                ins = None
                                for gv, pso in ((0, psg), (1, psv)):
                                    for k in range(8):
                                        ins = e.matmul(pso[:, :], lhsT=ring[slot][:, k, gv, mm_ * 128:(mm_ + 1) * 128],
                                                       rhs=hc[:, k, 32:32 + T], start=(k == 0), stop=(k == 7))
                                for gv in range(2):
                                    for k in range(8):
                                        ins = e.matmul(psh[:, gv * 32:gv * 32 + 32],
                                                       lhsT=ring[slot][:, k, gv, mm_ * 128:(mm_ + 1) * 128],
                                                       rhs=hc[:, k, 0:32], start=(k == 0), stop=(k == 7))
                                return ins

                            S.op("pe", mmup, reads=[hTb[s], hThb[s], ringb[slot]], writes=[pbg, pbv, pbh])
                            q = cnt["ub"] % 3
                            cnt["ub"] += 1
                            for gv, pso, pbo, cc, ccb in ((0, psg, pbg, cg[q], cgb[q]), (1, psv, pbv, cv[q], cvb[q])):
                                ch_ = gv * NMT + mt
                                w0 = PPt[:, PP_FW + ch_ * 3 + 0:PP_FW + ch_ * 3 + 1]
                                w1 = PPt[:, PP_FW + ch_ * 3 + 1:PP_FW + ch_ * 3 + 2]
                                w2 = PPt[:, PP_FW + ch_ * 3 + 2:PP_FW + ch_ * 3 + 3]
                                fb = PPt[:, PP_FB + ch_:PP_FB + ch_ + 1]
                                S.op("act", lambda e, pso=pso, cc=cc, w1=w1, fb=fb: e.activation(
                                    out=cc[:], in_=pso[:, :], func=AF.Identity, scale=w1, bias=fb),
                                    reads=[pbo, PPb], writes=[ccb])
                                S.op("act", lambda e, psh=psh, cc=cc, w0=w0, gv=gv: e.activation(
                                    out=cc[:, 0:1], in_=psh[:, gv * 32 + 15:gv * 32 + 16], func=AF.Identity,
                                    scale=w0, bias=cc[:, 0:1]), reads=[pbh, PPb, ccb], writes=[ccb])
                                S.op("act", lambda e, psh=psh, cc=cc, w2=w2, gv=gv: e.activation(
                                    out=cc[:, T - 1:T], in_=psh[:, gv * 32 + 16:gv * 32 + 17], func=AF.Identity,
                                    scale=w2, bias=cc[:, T - 1:T]), reads=[pbh, PPb, ccb], writes=[ccb])
                                S.op("dve", lambda e, pso=pso, cc=cc, w0=w0: e.scalar_tensor_tensor(
                                    out=cc[:, 1:T], in0=pso[:, 0:T - 1], scalar=w0, in1=cc[:, 1:T], op0=ALU.mult, op1=ALU.add),
                                    reads=[pbo, PPb, ccb], writes=[ccb])
                                S.op("dve", lambda e, pso=pso, cc=cc, w2=w2: e.scalar_tensor_tensor(
                                    out=cc[:, 0:T - 1], in0=pso[:, 1:T], scalar=w2, in1=cc[:, 0:T - 1], op0=ALU.mult, op1=ALU.add),
                                    reads=[pbo, PPb, ccb], writes=[ccb])

                            def fin(q=q, mt=mt, ab=ab, abb=abb):
                                S.op("act", lambda e: e.activation(out=cg[q][:], in_=cg[q][:], func=AF.Gelu_apprx_tanh),
                                     reads=[cgb[q]], writes=[cgb[q]])
                                S.op("pool", lambda e: e.tensor_tensor(out=ab[:, mt, :], in0=cg[q][:], in1=cv[q][:],
                                                                       op=ALU.mult),
                                     reads=[cgb[q], cvb[q]], writes=[abb[mt]])

                            if pist["pend"] is not None:
                                pist["pend"]()
                            pist["pend"] = fin
                        pdone.add(pist["pi"] - 1)
                def down_tile(i):
                    t0 = off + i * T
                    ab = act_bf[i % 2]
                    abb = actb[i % 2]
                    def xr_load2(j):
                        q_ = j % 2
                        r0_ = t0 + j * 128
                        S.dma("pool", f"xr{q_}", lambda e: e.dma_start(out=xr[q_][:], in_=x1_d[r0_:r0_ + 128, :]),
                              writes=[xrb[q_]])

                    xr_load2(0)
                    xr_load2(1)
                    for j in range(4):
                        q = j % 2
                        r0 = t0 + j * 128
                        for hf in range(2):
                            ps, pb = pp.get()

                            def mmdn(e, ps=ps, j=j, hf=hf, ab=ab):
                                ins = None
                                for mt in range(NMT):
                                    ins = e.matmul(ps[:, :], lhsT=ab[:, mt, j * 128:(j + 1) * 128],
                                                   rhs=FD[:, mt, hf * 512:(hf + 1) * 512], start=(mt == 0), stop=(mt == NMT - 1))
                                return ins

                            S.op("pe", mmdn, reads=abb + [FDb], writes=[pb])
                            S.op("dve", lambda e, ps=ps, q=q, hf=hf: e.tensor_tensor(
                                out=xr[q][:, hf * 512:(hf + 1) * 512], in0=ps[:, :], in1=xr[q][:, hf * 512:(hf + 1) * 512],
                                op=ALU.add), reads=[pb, xrb[q]], writes=[xrb[q]])
                        if last:
                            S.op("act", lambda e, q=q: e.activation(out=fjunk[:, :], in_=xr[q][:], func=AF.Square, accum_out=fss[:]),
                                 reads=[xrb[q]], writes=[fjunkb, fssb])
                            S.op("dve", lambda e: e.tensor_scalar(out=fss[:], in0=fss[:], scalar1=1.0 / D, scalar2=EPS,
                                                                  op0=ALU.mult, op1=ALU.add), reads=[fssb], writes=[fssb])
                            S.op("pool", lambda e: e.tensor_tensor(out=fss[:], in0=fss[:], in1=fr.negh[:, :], op=ALU.pow),
                                 reads=[fssb, fr.neghb], writes=[fssb])
                            S.op("dve", lambda e, q=q: e.scalar_tensor_tensor(out=xr[q][:], in0=xr[q][:], scalar=fss[:, 0:1], in1=gfb[:],
                                                                           op0=ALU.mult, op1=ALU.mult),
                                 reads=[xrb[q], fssb, gfbb], writes=[xrb[q]])
                        S.dma("pool", f"xst{q}", lambda e, q=q, r0=r0: e.dma_start(out=xdst[r0:r0 + 128, :], in_=xr[q][:]),
                              reads=[xrb[q]])
                        if j + 2 < 4:
                            xr_load2(j + 2)

                R0 = 2
                for i in range(NT):
                    for r in range(R0 if i > 0 else 0, NPC):
                        up_piece(i, r)
                    feq.flush(i + 1)
                    if i + 1 < NT:
                        for r in range(R0):
                            up_piece(i + 1, r)
                    else:
                        pist["pend"]()
                        pist["pend"] = None
                    down_tile(i)
                S.emit()

        stages = cfg.get("stages", ("p0", "p1", "p2"))
        for l in range(L):
            for seq in SEQS:
                if "p0" in stages:
                    pass0(l, seq)
                if "p1" in stages:
                    pass1(l, seq)
                if "p2" in stages:
                    pass2(l, seq)
        if dbg:
            for nm, ap in dbg_out.items():
                src = {"x1": x1_d, "x2": x2_d}[nm]
                S.dma("pool", "dbg", lambda e, ap=ap, src=src: e.dma_start(out=ap, in_=src))
            S.emit()
    return nc


def _bf(a):
    return np.ascontiguousarray(a.astype(ml_dtypes.bfloat16))


def dft_tables(Stot, Ssub):
    out = np.zeros((2, Stot, Stot), dtype=ml_dtypes.bfloat16)
    k = np.arange(Ssub, dtype=np.int64)
    ang = 2.0 * np.pi * np.arange(Ssub, dtype=np.float64) / Ssub
    ct = (np.cos(ang) / np.sqrt(Ssub)).astype(np.float32)
    st = (-np.sin(ang) / np.sqrt(Ssub)).astype(np.float32)
    idx = (np.outer(k, k) % Ssub)
    cb = ct[idx].astype(ml_dtypes.bfloat16)
    sb_ = st[idx].astype(ml_dtypes.bfloat16)
    for b in range(Stot // Ssub):
        out[0, b * Ssub:(b + 1) * Ssub, b * Ssub:(b + 1) * Ssub] = cb
        out[1, b * Ssub:(b + 1) * Ssub, b * Ssub:(b + 1) * Ssub] = sb_
    return out


def invcnt_table(seq_lens):
    cols = []
    for S_ in seq_lens:
        t = np.arange(S_)
        rows = []
        for w in POOLW:
            lo, hi = -(w // 2), w // 2 - 1
            start = np.clip(t + lo, 0, S_)
            end = np.clip(t + hi + 1, 0, S_)
            rows.append(1.0 / (end - start).astype(np.float32))
        cols.append(np.stack(rows, 0))
    return np.ascontiguousarray(np.concatenate(cols, axis=1).astype(np.float32))


def host_weights(inp, L):
    m = {}
    for k in ("w_in", "w_gate", "a_proj", "b_proj", "d_proj", "w_out", "f_up", "f_down"):
        m[k] = np.ascontiguousarray(inp[k], dtype=np.float32)
    cp = np.asarray(inp["c_proj"], dtype=np.float32)
    cpk = np.zeros((L, 2, 256, D), np.float32)
    for g in range(4):
        for c in range(49):
            idx = g * 49 + c
            cpk[:, 0, idx] = cp[:, g * 96 + c]
            if 1 <= c <= 47:
                cpk[:, 1, idx] = cp[:, g * 96 + 96 - c]
    m["cpk"] = np.ascontiguousarray(cpk.reshape(L, 512, D))
    m["wst"] = np.ascontiguousarray(np.transpose(inp["b_ws"], (0, 3, 1, 2)).reshape(L, 128, 512), dtype=np.float32)
    m["dwg"] = np.ascontiguousarray(np.transpose(inp["d_wg"], (0, 2, 1, 3)).reshape(L, 96, 384), dtype=np.float32)
    m["bsr"] = np.ascontiguousarray(inp["b_bs"].reshape(L, 1, 512), dtype=np.float32)
    pp = np.zeros((L, 128, NPP), np.float32)
    cw = np.transpose(inp["a_conv_w"], (0, 2, 1)).reshape(L, 3, 128, 31)
    pp[:, :, PP_CW:PP_CW + 93] = np.transpose(cw, (0, 2, 1, 3)).reshape(L, 128, 93)
    pp[:, :, PP_CB:PP_CB + 3] = np.transpose(inp["a_conv_b"].reshape(L, 3, 128), (0, 2, 1))
    pp[:, :, PP_LG:PP_LG + 3] = np.transpose(inp["a_ln_g"].reshape(L, 3, 128), (0, 2, 1))
    pp[:, :, PP_LB:PP_LB + 3] = np.transpose(inp["a_ln_b"].reshape(L, 3, 128), (0, 2, 1))
    pp[:, :, PP_BG:PP_BG + 32] = np.transpose(inp["b_gate"].reshape(L, 4, 8, 128), (0, 3, 1, 2)).reshape(L, 128, 32)
    pp[:, 0:96, PP_DS:PP_DS + 4] = np.transpose(inp["d_scale"].reshape(L, 4, 96), (0, 2, 1))
    fw = np.transpose(inp["f_conv_w"], (0, 2, 1)).reshape(L, 44, 128, 3)
    pp[:, :, PP_FW:PP_FW + 132] = np.transpose(fw, (0, 2, 1, 3)).reshape(L, 128, 132)
    pp[:, :, PP_FB:PP_FB + 44] = np.transpose(inp["f_conv_b"].reshape(L, 44, 128), (0, 2, 1))
    m["pp"] = pp
    m["g1"] = np.ascontiguousarray(inp["norm1_g"].reshape(L, 1, D), dtype=np.float32)
    m["g2"] = np.ascontiguousarray(inp["norm2_g"].reshape(L, 1, D), dtype=np.float32)
    m["gf"] = np.ascontiguousarray(inp["final_g"].reshape(1, D), dtype=np.float32)
    m["blg"] = np.ascontiguousarray(inp["b_ln_g"].reshape(L, 1, 384), dtype=np.float32)
    m["blb"] = np.ascontiguousarray(inp["b_ln_b"].reshape(L, 1, 384), dtype=np.float32)
    c = np.arange(96)
    ang = 2.0 * np.pi * np.outer(c, c) / 96.0
    cs96 = np.concatenate([np.cos(ang)[:, 0:49], np.sin(ang)[:, 0:49]], axis=1) / np.sqrt(96.0)
    m["cs96"] = _bf(cs96.astype(np.float32))
    m["ident"] = _bf(np.eye(128, dtype=np.float32))
    m["ones"] = np.full((128, 128), 1.0 / 384.0, np.float32)
    return m


_CACHE = {}


def kernel(**inputs):
    L = 2
    xp = np.asarray(inputs["x_prompt"], dtype=np.float32)
    xs = np.asarray(inputs["x_sample"], dtype=np.float32)
    SA, SB = 8192, 2048
    NTOK = SA + 2 * SB
    seqs = [(0, SA, "A", 0), (SA, SB, "B", SA // T), (SA + SB, SB, "B", SA // T + SB // T)]
    NTT = NTOK // T
    cfg = {"seqs": seqs, "ntok": NTOK, "L": L, "ntiles": NTT}
    if "nc" not in _CACHE:
        _CACHE["nc"] = build(cfg)
    nc = _CACHE["nc"]
    base = host_weights(inputs, L)
    dftA_full = dft_tables(SA, SA)
    dftA_blk = dft_tables(SA, SB)
    dftB = dft_tables(SB, SB)
    in_maps = []
    assign = []
    for c in range(8):
        if c < 4:
            xa = xs[c]
            pids = [2 * c, 2 * c + 1]
            hm = np.ones((32, NTT), np.float32)
            invc = invcnt_table([SA, SB, SB])
            dA = dftA_full
            apid = None
        else:
            p0 = 8 + 6 * (c - 4)
            apid = [p0, p0 + 1, p0 + 2, p0 + 3]
            xa = xp[apid].reshape(SA, D)
            pids = [p0 + 4, p0 + 5]
            hm = np.ones((32, NTT), np.float32)
            for ti in range(SA // T):
                t0 = ti * T
                if t0 % SB == 0:
                    hm[0:16, ti] = 0.0
                if (t0 + T) % SB == 0:
                    hm[16:32, ti] = 0.0
            invc = invcnt_table([SB] * 6)
            dA = dftA_blk
        xcat = np.ascontiguousarray(np.concatenate([xa, xp[pids[0]], xp[pids[1]]], axis=0))
        m = dict(base)
        m["x"] = xcat
        m["hmask"] = hm
        m["invc"] = invc
        m["dftA"] = dA
        m["dftB"] = dftB
        in_maps.append(m)
        assign.append((c, apid, pids))
    res = run_bass_kernel_spmd(nc, in_maps, core_ids=list(range(8)))
    yp = np.empty_like(xp)
    ys = np.empty_like(xs)
    for (c, apid, pids), r in zip(assign, res.results):
        yv = np.asarray(r["y"], dtype=np.float32)
        if apid is None:
            ys[c] = yv[0:SA]
        else:
            yp[apid] = yv[0:SA].reshape(4, SB, D)
        yp[pids[0]] = yv[SA:SA + SB]
        yp[pids[1]] = yv[SA + SB:SA + 2 * SB]
    return (yp, ys)
```

```python
import numpy as np
import ml_dtypes
from contextlib import ExitStack
import concourse.bass as bass
import concourse.mybir as mybir
from concourse.bass_utils import run_bass_kernel_spmd

F32 = mybir.dt.float32
BF = mybir.dt.bfloat16
AF = mybir.ActivationFunctionType
ALU = mybir.AluOpType

D = 1024
DIN = 2304
DFF = 2816
NMT = DFF // 128
T = 512
HL = 16
EPS = 1e-6
OFF_A, OFF_B, OFF_C, OFF_D = 0, 768, 1536, 1920
POOLW = (2, 4, 8, 16)
ENG = ("pe", "act", "dve", "pool", "sp")

PP_CW = 0
PP_CB = PP_CW + 93
PP_LG = PP_CB + 3
PP_LB = PP_LG + 3
PP_BG = PP_LB + 3
PP_DS = PP_BG + 32
PP_FW = PP_DS + 4
PP_FB = PP_FW + 132
NPP = PP_FB + 44


class Buf:
    __slots__ = ("name", "w", "r")

    def __init__(self, name):
        self.name = name
        self.w = None
        self.r = {}


class Sched:
    def __init__(self, nc, es):
        self.nc = nc
        self.h = {}
        for e in ENG:
            self.h[e] = es.enter_context(nc.semaphore("s_" + e))
        self.cnt = {e: 0 for e in ENG}
        self.dcnt = {}
        self.known = {e: {} for e in ENG}
        self.ops = {e: [] for e in ENG}
        self.es = es

    def dsem(self, name):
        if name not in self.h:
            self.h[name] = self.es.enter_context(self.nc.semaphore("d_" + name))
            self.dcnt[name] = 0
        return name

    def _collect(self, eng, reads, writes):
        waits = {}

        def need(dep, kind):
            if dep is None:
                return
            key, val = dep
            if key == eng:
                if eng == "pe" or eng == "sp":
                    return
                if kind == "war":
                    return
            if val > waits.get(key, 0):
                waits[key] = val

        for b in reads:
            need(b.w, "raw")
        for b in writes:
            need(b.w, "waw")
            for r in b.r.values():
                need(r, "war")
        kn = self.known[eng]
        final = []
        for k, v in waits.items():
            if kn.get(k, 0) < v:
                kn[k] = v
                final.append((k, v))
        return final

    def op(self, eng, fn, reads=(), writes=()):
        waits = self._collect(eng, reads, writes)
        self.cnt[eng] += 1
        me = (eng, self.cnt[eng])
        for b in reads:
            b.r[eng] = me
        for b in writes:
            b.w = me
            b.r = {}
        self.ops[eng].append((waits, fn, (eng, 1)))

    def dma(self, q, sem, fn, reads=(), writes=()):
        self.dsem(sem)
        waits = self._collect(q, reads, writes)
        self.dcnt[sem] += 16
        me = (sem, self.dcnt[sem])
        for b in reads:
            b.r[sem] = me
        for b in writes:
            b.w = me
            b.r = {}
        self.ops[q].append((waits, fn, (sem, 16)))

    def dma_group(self, q, sem, fns, reads=(), writes=()):
        self.dsem(sem)
        waits = self._collect(q, reads, writes)
        for idx, fn in enumerate(fns):
            self.dcnt[sem] += 16
            self.ops[q].append((waits if idx == 0 else [], fn, (sem, 16)))
        me = (sem, self.dcnt[sem])
        for b in reads:
            b.r[sem] = me
        for b in writes:
            b.w = me
            b.r = {}

    def emit(self):
        nc = self.nc
        fin = [(k, v) for k, v in self.dcnt.items() if self.known["sp"].get(k, 0) < v]
        for k, v in fin:
            self.known["sp"][k] = v
        self.ops["sp"].append((fin, None, None))
        with nc.Block() as block:
            decos = {"pe": block.tensor, "act": block.scalar, "dve": block.vector,
                     "pool": block.gpsimd, "sp": block.sync}
            for eng in ENG:
                ops = self.ops[eng]
                if not ops:
                    continue

                def body(e, ops=ops):
                    for waits, fn, inc in ops:
                        for k, v in waits:
                            e.wait_ge(self.h[k], v)
                        if fn is not None:
                            inst = fn(e)
                            inst.then_inc(self.h[inc[0]], inc[1])

                decos[eng](body)
        self.ops = {e: [] for e in ENG}
        for e in ENG:
            for e2 in ENG:
                self.known[e][e2] = self.cnt[e2]


class Ps:
    def __init__(self, es, nc, n, tag):
        self.t = [es.enter_context(nc.psum_tensor(f"ps_{tag}{i}", [128, 512], F32)) for i in range(n)]
        self.b = [Buf(f"ps{i}") for i in range(n)]
        self.i = 0

    def get(self):
        i = self.i
        self.i = (i + 1) % len(self.t)
        return self.t[i], self.b[i]


class FeQ:
    def __init__(self, tasks):
        self.q = tasks
        self.pos = [0] * len(tasks)

    def step(self, i, n=1):
        if i >= len(self.q):
            return
        for _ in range(n):
            if self.pos[i] < len(self.q[i]):
                self.q[i][self.pos[i]]()
                self.pos[i] += 1

    def flush(self, i):
        self.step(i, 99)


def build(cfg):
    SEQS = cfg["seqs"]
    NTOK = cfg["ntok"]
    L = cfg["L"]
    NTT = cfg["ntiles"]
    dbg = cfg.get("dbg")
    nc = bass.Bass("TRN2", target_bir_lowering=False)

    def din(name, shape, dt=F32):
        return nc.dram_tensor(name, list(shape), dt, kind="ExternalInput").ap()

    def dscr(name, shape, dt):
        return nc.dram_tensor(name, list(shape), dt, kind="Internal").ap()

    x_in = din("x", [NTOK, D])
    y_out = nc.dram_tensor("y", [NTOK, D], F32, kind="ExternalOutput").ap()
    W = {}
    wshapes = {
        "w_in": [L, D, DIN], "w_gate": [L, 4, D, D], "a_proj": [L, 384, D], "b_proj": [L, 384, D],
        "cpk": [L, 512, D], "d_proj": [L, 384, D], "w_out": [L, D, D], "f_up": [L, D, 2 * DFF],
        "f_down": [L, DFF, D], "wst": [L, 128, 4 * 128], "dwg": [L, 96, 4 * 96], "bsr": [L, 1, 512],
    }
    WB = {}
    for k, shp in wshapes.items():
        W[k] = din(k, shp)
        WB[k] = dscr(k + "_bf", shp, BF)
    pp_in = din("pp", [L, 128, NPP])
    g1_in = din("g1", [L, 1, D])
    g2_in = din("g2", [L, 1, D])
    gf_in = din("gf", [1, D])
    blg_in = din("blg", [L, 1, 384])
    blb_in = din("blb", [L, 1, 384])
    cs96_in = din("cs96", [96, 98], BF)
    ident_in = din("ident", [128, 128], BF)
    ones_in = din("ones", [128, 128])
    hmask_in = din("hmask", [32, NTT])
    invc_in = din("invc", [4, NTOK])
    dft_in = {}
    for (off, S_, dn, tb) in SEQS:
        if dn not in dft_in:
            dft_in[dn] = din("dft" + dn, [2, S_, S_], BF)
    x1_d = dscr("x1_scr", [NTOK, D], F32)
    x2_d = dscr("x2_scr", [NTOK, D], F32)
    F_d = dscr("F_scr", [4, 128, NTOK], BF)
    dbg_out = {}
    if dbg:
        for nm, shp in dbg.items():
            dbg_out[nm] = nc.dram_tensor("dbg_" + nm, list(shp), F32, kind="ExternalOutput").ap()

    with ExitStack() as ges:
        S = Sched(nc, ges)

        wcastb = Buf("wcast")
        for k, shp in wshapes.items():
            src = W[k]
            dst = WB[k]
            if len(shp) == 4:
                src = src.rearrange("l i r c -> (l i r) c")
                dst = dst.rearrange("l i r c -> (l i r) c")
            else:
                src = src.rearrange("l r c -> (l r) c")
                dst = dst.rearrange("l r c -> (l r) c")
            R = src.shape[0]
            r0 = 0
            semn = "wc0" if k == "w_in" else "wc"
            while r0 < R:
                rr = min(128, R - r0)
                S.dma("pool", semn, lambda e, s=src[r0:r0 + rr, :], d=dst[r0:r0 + rr, :]:
                      e.dma_start(out=d, in_=s), writes=[wcastb] if k == "w_in" and r0 + rr >= R else [])
                r0 += rr

        class Front:
            def __init__(self, es, tag, g_src, pp):
                sbt = lambda n, s, d: es.enter_context(nc.sbuf_tensor(f"{tag}_{n}", s, d))
                self.tag = tag
                self.xs = [sbt(f"xs{i}", [128, D], F32) for i in range(2)]
                self.xsb = [Buf("xs") for _ in range(2)]
                self.ss = [sbt(f"ss{i}", [128, 1], F32) for i in range(2)]
                self.rs = [sbt(f"rs{i}", [128, 1], F32) for i in range(2)]
                self.ssb = [Buf("ss") for _ in range(2)]
                self.hb = [sbt(f"hb{i}", [128, D], BF) for i in range(2)]
                self.hbb = [Buf("hb") for _ in range(2)]
                self.pp = pp
                self.negh = sbt("negh", [128, 1], F32)
                self.neghb = Buf("negh")
                S.op("pool", lambda e: e.memset(self.negh[:], -0.5), writes=[self.neghb])
                self.junk = sbt("junk", [128, D], BF)
                self.junkb = Buf("junk")
                self.gbc = sbt("gbc", [128, D], F32)
                self.gbcb = Buf("gbc")
                self.ident = sbt("ident", [128, 128], BF)
                self.identb = Buf("ident")
                self.hm = sbt("hm", [32, NTT], F32)
                self.hmb = Buf("hm")
                self.i = 0
                S.dma_group("sp", "par", [
                    lambda e: e.dma_start(out=self.gbc[:], in_=g_src.broadcast_to([128, D])),
                    lambda e: e.dma_start(out=self.ident[:], in_=ident_in[:, :]),
                    lambda e: e.dma_start(out=self.hm[:], in_=hmask_in[:, :])],
                    writes=[self.gbcb, self.identb, self.hmb])

            def new(self, srcs, dst, dstb, npart=128, zero=False, mask_col=None):
                s = self.i % 2
                self.i += 1
                return dict(s=s, srcs=srcs, dst=dst, dstb=dstb, P=npart, zero=zero, mask=mask_col)

            def load(self, c):
                s, P = c["s"], c["P"]
                xs, xsb = self.xs[s], self.xsb[s]
                if c["zero"]:
                    S.op("pool", lambda e: e.memset(xs[0:P, :], 0.0), writes=[xsb])
                if c["srcs"]:
                    S.dma_group("sp", f"fxs{s}",
                                [(lambda e, r0=r0, nr=nr, ap=ap: e.dma_start(out=xs[r0:r0 + nr, :], in_=ap))
                                 for (r0, nr, ap) in c["srcs"]], writes=[xsb])

            def norm_a(self, c):
                s, P = c["s"], c["P"]
                xs, xsb, ss, rs, ssb, hb, hbb = (self.xs[s], self.xsb[s], self.ss[s], self.rs[s], self.ssb[s],
                                                self.hb[s], self.hbb[s])
                S.op("act", lambda e: e.activation(out=self.junk[0:P, :], in_=xs[0:P, :], func=AF.Square,
                                                   accum_out=ss[0:P, :]),
                     reads=[xsb], writes=[self.junkb, ssb])
                S.op("dve", lambda e: e.tensor_scalar(out=rs[0:P, :], in0=ss[0:P, :], scalar1=1.0 / D, scalar2=EPS,
                                                      op0=ALU.mult, op1=ALU.add), reads=[ssb], writes=[ssb])

            def norm_b(self, c):
                s, P = c["s"], c["P"]
                xs, xsb, ss, rs, ssb, hb, hbb = (self.xs[s], self.xsb[s], self.ss[s], self.rs[s], self.ssb[s],
                                                self.hb[s], self.hbb[s])
                S.op("pool", lambda e: e.tensor_tensor(out=rs[0:P, :], in0=rs[0:P, :], in1=self.negh[0:P, :], op=ALU.pow),
                     reads=[ssb, self.neghb], writes=[ssb])
                if c["mask"] is not None:
                    mc = c["mask"]
                    S.op("dve", lambda e: e.tensor_tensor(out=rs[0:P, :], in0=rs[0:P, :],
                                                          in1=self.hm[0:P, mc:mc + 1], op=ALU.mult),
                         reads=[ssb, self.hmb], writes=[ssb])
                S.op("dve", lambda e: e.scalar_tensor_tensor(out=hb[0:P, :], in0=xs[0:P, :], scalar=rs[0:P, 0:1],
                                                             in1=self.gbc[0:P, :], op0=ALU.mult, op1=ALU.mult),
                     reads=[xsb, ssb, self.gbcb], writes=[hbb])

            def trans(self, c):
                s, P = c["s"], c["P"]
                hb, hbb = self.hb[s], self.hbb[s]
                dst, dstb = c["dst"], c["dstb"]
                ps, tpb = self.pp.get()
                tp = ps[:, :].bitcast(BF).rearrange("p (k t) -> p k t", k=8)

                def tr(e):
                    ins = None
                    for k in range(8):
                        ins = e.transpose(out=tp[:, k, 0:P], in_=hb[0:P, k * 128:(k + 1) * 128],
                                          identity=self.ident[0:P, 0:P])
                    return ins

                S.op("pe", tr, reads=[hbb, self.identb], writes=[tpb])
                S.op("act", lambda e: e.copy(out=dst, in_=tp[:, :, 0:P]), reads=[tpb], writes=[dstb])

            def tasks(self, chunks):
                n = len(chunks)

                def first():
                    for c in chunks[0:2]:
                        self.load(c)
                    self.norm_a(chunks[0])

                out = [first]
                for st in range(n + 1):
                    def t(st=st):
                        if 1 <= st <= n:
                            self.trans(chunks[st - 1])
                        if st < n:
                            self.norm_b(chunks[st])
                        if st + 2 < n:
                            self.load(chunks[st + 2])
                        if st + 1 < n:
                            self.norm_a(chunks[st + 1])
                    out.append(t)
                return out

        def load_piece(ring, ringb, slot, parts, sem):
            S.dma_group("sp", sem, [(lambda e, d_=d_, s_=s_: e.dma_start(out=d_, in_=s_)) for d_, s_ in parts],
                        writes=[ringb[slot]])

        def wview(name, l, rows0, nrows, c0, c1, i=None):
            w = WB[name]
            if i is not None:
                w2 = w[l, i]
            else:
                w2 = w[l]
            return w2[rows0:rows0 + nrows, c0:c1]

        uid = [0]

        def pass0(l, seq):
            uid[0] += 1
            U = f"u{uid[0]}"
            off, SL, dn, tb = seq
            NT = SL // T
            NCH = SL // 128
            xsrc = x_in if l == 0 else x2_d
            with ExitStack() as es:
                sbt = lambda n, s, d: es.enter_context(nc.sbuf_tensor(f"{U}p0_{n}", s, d))
                pp = Ps(es, nc, 8, U + "p0")
                fr = Front(es, U + "p0", g1_in[l], pp)
                hT = [sbt(f"hT{i}", [128, 8, T], BF) for i in range(2)]
                hTb = [Buf("hT") for _ in range(2)]
                WC = sbt("WC", [128, 8, 384], BF)
                WCb = Buf("WC")
                CS = sbt("CS", [96, 98], BF)
                CSb = Buf("CS")
                zc = sbt("zc", [96, 4, T], BF)
                zcb = [Buf("zc") for _ in range(4)]
                ZCS = sbt("ZCS", [128, NCH, 392], BF)
                ZCSb = [[Buf("zcs")] for _ in range(NCH)]
                Bsb = sbt("Bsb", [128, 2, T], F32)
                Bsbb = [Buf("Bsb") for _ in range(2)]
                ring = [sbt(f"ring{i}", [128, 8, 2, T], BF) for i in range(3)]
                ringb = [Buf("ring") for _ in range(3)]
                Fsb = [sbt(f"Fsb{i}", [128, 4, T], BF) for i in range(2)]
                Fsbb = [[Buf("Fsb") for _ in range(4)] for _ in range(2)]
                S.dma("sp", "par", lambda e: e.dma_start(
                    out=WC[:], in_=WB["w_in"][l][:, OFF_C:OFF_D].rearrange("(k p) c -> p k c", p=128)),
                    reads=[wcastb], writes=[WCb])
                S.dma("sp", "par", lambda e: e.dma_start(out=CS[:], in_=cs96_in[:, :]), writes=[CSb])
                def fe_tasks0(i):
                    s = i % 2
                    t0 = off + i * T
                    return fr.tasks([fr.new([(0, 128, xsrc[t0 + j * 128:t0 + (j + 1) * 128, :])],
                                            hT[s][:, :, j * 128:(j + 1) * 128], hTb[s]) for j in range(4)])

                feq = FeQ([fe_tasks0(i) for i in range(NT)])
                feq.flush(0)
                for i in range(NT):
                    s = i % 2
                    t0 = off + i * T
                    for g in range(4):
                        feq.step(i + 1)
                        ps, pb = pp.get()

                        def mm(e, ps=ps, g=g, s=s):
                            ins = None
                            for k in range(8):
                                ins = e.matmul(ps[0:96, :], lhsT=WC[:, k, g * 96:(g + 1) * 96], rhs=hT[s][:, k, :],
                                               start=(k == 0), stop=(k == 7))
                            return ins

                        S.op("pe", mm, reads=[hTb[s], WCb], writes=[pb])
                        S.op("act", lambda e, ps=ps, g=g: e.copy(out=zc[:, g, :], in_=ps[0:96, :]),
                             reads=[pb], writes=[zcb[g]])
                    for j in range(4):
                        feq.step(i + 1)
                        ch = i * 4 + j
                        psA, pbA = pp.get()

                        def mm2(e, psA=psA, j=j):
                            ins = None
                            for g in range(4):
                                ins = e.matmul(psA[:, g * 98:(g + 1) * 98],
                                               lhsT=zc[:, g, j * 128:(j + 1) * 128], rhs=CS[:, :],
                                               start=True, stop=True)
                            return ins

                        S.op("pe", mm2, reads=zcb + [CSb], writes=[pbA])
                        if j % 2 == 0:
                            S.op("dve", lambda e, psA=psA, ch=ch: e.tensor_copy(
                                out=ZCS[:, ch, :].rearrange("p (cs g c) -> p g cs c", cs=2, g=4),
                                in_=psA[:, 0:392].rearrange("p (g cs c) -> p g cs c", g=4, cs=2)),
                                 reads=[pbA], writes=[ZCSb[ch][0]])
                        else:
                            S.op("act", lambda e, psA=psA, ch=ch: e.copy(
                                out=ZCS[:, ch, :].rearrange("p (cs g c) -> p g cs c", cs=2, g=4),
                                in_=psA[:, 0:392].rearrange("p (g cs c) -> p g cs c", g=4, cs=2)),
                                 reads=[pbA], writes=[ZCSb[ch][0]])
                dft = dft_in[dn]
                NPC = NCH // 8
                pieces = [(i, pc) for i in range(NT) for pc in range(NPC)]

                def issue(idx):
                    i, pc = pieces[idx]
                    slot = idx % 3
                    parts = []
                    for cs in range(2):
                        parts.append((ring[slot][:, :, cs, :],
                                      dft[cs, pc * 1024:(pc + 1) * 1024, i * T:(i + 1) * T].rearrange(
                                          "(ch q) p -> q ch p", q=128)))
                    load_piece(ring, ringb, slot, parts, f"rg{slot}")

                done = set()
                nxt = {"n": 0}

                def pref():
                    while nxt["n"] < len(pieces) and (nxt["n"] < 3 or (nxt["n"] - 3) in done):
                        issue(nxt["n"])
                        nxt["n"] += 1

                Fps = None
                for idx, (i, pc) in enumerate(pieces):
                    pref()
                    slot = idx % 3
                    if pc == 0:
                        Fps = [pp.get() for _ in range(4)]

                    def mm3(e, Fps=Fps, pc=pc, slot=slot):
                        ins = None
                        for c8 in range(8):
                            ch = pc * 8 + c8
                            for cs in range(2):
                                for g in range(2):
                                    msz = 128 if g == 0 else 68
                                    ins = e.matmul(Fps[cs * 2 + g][0][0:msz, :],
                                                   lhsT=ZCS[:, ch, cs * 196 + g * 128:cs * 196 + g * 128 + msz],
                                                   rhs=ring[slot][:, c8, cs, :],
                                                   start=(ch == 0), stop=(ch == NCH - 1))
                        return ins

                    S.op("pe", mm3, reads=[ringb[slot]] + [b_ for cb_ in ZCSb[pc * 8:(pc + 1) * 8] for b_ in cb_],
                         writes=[p[1] for p in Fps])
                    done.add(idx)
                    if pc == NPC - 1:
                        fs = i % 2
                        for g in range(2):
                            msz = 128 if g == 0 else 68
                            S.op("act", lambda e, g=g, Fps=Fps, msz=msz: e.copy(out=Bsb[0:msz, g, :], in_=Fps[2 + g][0][0:msz, :]),
                                 reads=[Fps[2 + g][1]], writes=[Bsbb[g]])
                            S.op("dve", lambda e, g=g, Fps=Fps, fs=fs, msz=msz: e.tensor_tensor(
                                out=Fsb[fs][0:msz, g, :], in0=Fps[g][0][0:msz, :], in1=Bsb[0:msz, g, :], op=ALU.add),
                                reads=[Fps[g][1], Bsbb[g]], writes=[Fsbb[fs][g]])
                            S.op("dve", lambda e, g=g, Fps=Fps, fs=fs, msz=msz: e.tensor_tensor(
                                out=Fsb[fs][0:msz, 2 + g, :], in0=Fps[g][0][0:msz, :], in1=Bsb[0:msz, g, :], op=ALU.subtract),
                                reads=[Fps[g][1], Bsbb[g]], writes=[Fsbb[fs][2 + g]])
                        t0 = off + i * T
                        S.dma("pool", f"fst{fs}", lambda e, fs=fs, t0=t0: e.dma_start(
                            out=F_d[:, :, t0:t0 + T].rearrange("g c t -> c g t"), in_=Fsb[fs][:, :, :]),
                            reads=Fsbb[fs])
                S.emit()

        def pass1(l, seq):
            uid[0] += 1
            U = f"u{uid[0]}"
            off, SL, dn, tb = seq
            NT = SL // T
            xsrc = x_in if l == 0 else x2_d
            with ExitStack() as es:
                sbt = lambda n, s, d: es.enter_context(nc.sbuf_tensor(f"{U}p1_{n}", s, d))
                pp = Ps(es, nc, 8, U + "p1")
                fr = Front(es, U + "p1", g1_in[l], pp)
                HW_ = 32 + T
                hT = [sbt(f"hT{i}", [128, 8, HW_], BF) for i in range(2)]
                hTb = [Buf("hT") for _ in range(2)]
                hThb = [Buf("hTh") for _ in range(2)]
                NSL = 5
                ring = [sbt(f"ring{i}", [128, 4096], BF) for i in range(NSL)]
                ringb = [Buf("ring") for _ in range(NSL)]
                PPt = sbt("pp", [128, NPP], F32)
                PPb = Buf("pp")
                ones = sbt("ones", [128, 128], F32)
                onesb = Buf("ones")
                WsT = sbt("WsT", [128, 4, 128], BF)
                dwg = sbt("dwg", [96, 4, 96], BF)
                bsr = sbt("bsr", [1, 512], BF)
                one1 = sbt("one1", [1, 96], BF)
                smallb = Buf("small")
                Gb = sbt("Gb", [128, 384], F32)
                Bb = sbt("Bb", [128, 384], F32)
                sg = [sbt(f"sg{i}", [128, T], F32) for i in range(2)]
                sgb = [Buf("sg") for _ in range(2)]
                sgh = sbt("sgh", [128, 32], F32)
                sghb = Buf("sgh")
                u = sbt("u", [128, 3, T + 32], BF)
                mg = sbt("mg", [128, 8, T], F32)
                mgb = [Buf("mg") for _ in range(8)]
                dg = sbt("dg", [128, 48, 128], BF)
                dgb = Buf("dg")
                ub_ = [Buf("u") for _ in range(3)]
                acc = sbt("acc", [128, 3, T], F32)
                accb = [Buf("acc") for _ in range(3)]
                lnm2_t = sbt("lnm2", [128, T], F32)
                lnrs_t = sbt("lnrs", [128, T], F32)
                lnm2 = lnm2_t[:, :]
                lnrs = lnrs_t[:, :]
                lnm2b = Buf("lnm2")
                lnrsb = Buf("lnrs")
                lnt = [sbt(f"lnt{i}", [128, T], F32) for i in range(2)]
                lntb = [Buf("lnt") for _ in range(2)]
                cbf = sbt("cbf", [128, 3, T], BF)
                cbfb = [Buf("cbf") for _ in range(3)]
                vg = [sbt(f"vg{i}", [128, 384], F32) for i in range(2)]
                vgb = [Buf("vg") for _ in range(2)]
                bst = [sbt(f"bst{i}", [128, 6], F32) for i in range(2)]
                bmv = [sbt(f"bmv{i}", [128, 2], F32) for i in range(2)]
                bstb = [Buf("bst") for _ in range(2)]
                vn = sbt("vn", [128, 384], F32)
                vnb_ = Buf("vn")
                vnb = sbt("vnb", [128, 4, 384], BF)
                vnbb = [Buf("vnb") for _ in range(4)]
                ug = sbt("ug", [96, 4, T], BF)
                ugb = [Buf("ug") for _ in range(4)]
                usv = sbt("usv", [96, 4, T], BF)
                usvb = [Buf("usv") for _ in range(4)]
                fsb = sbt("fsb", [128, 4, T], BF)
                fsbb = Buf("fsb")
                zd2 = [sbt(f"zd{i}", [96, T + 32], F32) for i in range(2)]
                zdb2 = [Buf("zd") for _ in range(2)]
                sa = sbt("sa", [96, T + 32], F32)
                sbb_ = sbt("sb", [96, T + 32], F32)
                sab = Buf("sa")
                sbb = Buf("sb")
                invc = [sbt(f"invc{i}", [96, T], F32) for i in range(4)]
                invcb = [Buf("invc") for _ in range(4)]
                diff = sbt("diff", [96, 4, T], BF)
                diffb = [Buf("diff") for _ in range(4)]
                yd = sbt("yd", [96, 4, T], BF)
                ydb = [Buf("yd") for _ in range(4)]
                mb = sbt("mb", [128, 8, T], BF)
                mbb = [Buf("mb") for _ in range(8)]
                tmp = [sbt(f"tmp{i}", [128, T], F32) for i in range(2)]
                tmpb = [Buf("tmp") for _ in range(2)]
                gs = [sbt(f"gs{i}", [128, T], F32) for i in range(2)]
                gsb = [Buf("gs") for _ in range(2)]
                xr = [sbt(f"xr{i}", [128, D], F32) for i in range(2)]
                xrb = [Buf("xr") for _ in range(2)]
                sqs = [lnt[0], lnt[1], sg[0]]
                sqb = [lntb[0], lntb[1], sgb[0]]
                cnt = {"tmp": 0, "gs": 0, "xr": 0, "sg": 0, "lnt": 0}

                par = [(PPt[:], pp_in[l]), (ones[:], ones_in[:, :]),
                       (WsT[:], WB["wst"][l].rearrange("q (h p) -> q h p", h=4)),
                       (dwg[:], WB["dwg"][l].rearrange("c (g d) -> c g d", g=4)),
                       (bsr[:], WB["bsr"][l]),
                       (Gb[:], blg_in[l].broadcast_to([128, 384])),
                       (Bb[:], blb_in[l].broadcast_to([128, 384]))]
                one1b = Buf("one1b")
                S.dma_group("sp", "par", [(lambda e, d_=d_, s_=s_: e.dma_start(out=d_, in_=s_)) for d_, s_ in par],
                            writes=[smallb, PPb, onesb])
                S.op("pool", lambda e: e.memset(one1[:], 1.0), writes=[one1b])
                for m_ in range(3):
                    for k_ in range(16):
                        S.op("dve", lambda e, m_=m_, k_=k_: e.tensor_scalar(
                            out=dg[:, m_ * 16 + k_, :], in0=fr.ident[:, :],
                            scalar1=PPt[:, PP_CW + m_ * 31 + k_:PP_CW + m_ * 31 + k_ + 1], scalar2=None,
                            op0=ALU.mult), reads=[fr.identb, PPb] + ([dgb] if (m_ + k_) > 0 else []), writes=[dgb])

                def piece_list():
                    wi = WB["w_in"][l]

                    def cols(c0, n):
                        return [(lambda slot, c0=c0, n=n: ring[slot][:, 0:8 * n].rearrange("p (k c) -> p k c", k=8),
                                 wi[:, c0:c0 + n].rearrange("(k p) c -> p k c", p=128))]

                    pcs = []
                    pcs.append(("WBv", cols(OFF_B + 384, 384)))
                    pcs.append(("WD", cols(OFF_D, 384)))
                    pcs.append(("WAa", cols(OFF_A, 384)))
                    pcs.append(("WAg", cols(OFF_A + 384, 384)))
                    pcs.append(("WBu", cols(OFF_B, 384)))
                    for br, pj, kk, kp in ((2, "cpk", 4, 128), (3, "d_proj", 4, 96), (1, "b_proj", 4, 96),
                                           (0, "a_proj", 3, 128)):
                        for hf in range(2):
                            pcs.append((f"Wg{br}{hf}", [(
                                lambda slot: ring[slot][:, 0:4096].rearrange("p (k c) -> p k c", k=8),
                                WB["w_gate"][l, br][:, hf * 512:(hf + 1) * 512].rearrange("(k p) c -> p k c", p=128))]))
                        pcs.append((f"Pj{br}", [(
                            lambda slot, kk=kk, kp=kp: ring[slot][0:kp, 0:kk * 1024].rearrange("p (k c) -> p k c", k=kk),
                            WB[pj][l].rearrange("(k p) c -> p k c", p=kp))]))
                    for hf in range(2):
                        pcs.append((f"Wo{hf}", [(
                            lambda slot: ring[slot][:, 0:4096].rearrange("p (k c) -> p k c", k=8),
                            WB["w_out"][l][:, hf * 512:(hf + 1) * 512].rearrange("(k p) c -> p k c", p=128))]))
                    return pcs

                allp = []
                for i in range(NT):
                    for nm, parts in piece_list():
                        allp.append((i, nm, parts))
                pstate = {"next": 0}
                pslot = {}
                pdone = set()
                pidx_of = {}
                for idx_, (i_, nm_, _) in enumerate(allp):
                    pidx_of[(i_, nm_)] = idx_

                def prefetch():
                    while pstate["next"] < len(allp) and (pstate["next"] < NSL or (pstate["next"] - NSL) in pdone):
                        idx = pstate["next"]
                        slot = idx % NSL
                        i_, nm, parts = allp[idx]
                        load_piece(ring, ringb, slot, [(dfn(slot), s_) for dfn, s_ in parts], f"rg{slot}")
                        pslot[(i_, nm)] = slot
                        pstate["next"] += 1

                def use(i, nm):
                    prefetch()
                    return pslot[(i, nm)]

                def rel(i, *nms):
                    for nm in nms:
                        pdone.add(pidx_of[(i, nm)])
                    prefetch()

                def fe_tasks1(i, src_d):
                    s = i % 2
                    t0 = off + i * T
                    tile_id = tb + i
                    srcs = []
                    if i > 0:
                        srcs.append((0, 16, src_d[t0 - 16:t0, :]))
                    if i < NT - 1:
                        srcs.append((16, 16, src_d[t0 + T:t0 + T + 16, :]))
                    chunks = [fr.new(srcs, hT[s][:, :, 0:32], hThb[s], npart=32, zero=(len(srcs) < 2), mask_col=tile_id)]
                    for j in range(4):
                        chunks.append(fr.new([(0, 128, src_d[t0 + j * 128:t0 + (j + 1) * 128, :])],
                                             hT[s][:, :, 32 + j * 128:32 + (j + 1) * 128], hTb[s]))
                    return fr.tasks(chunks)

                def load_F(i):
                    if i >= NT:
                        return
                    t0_ = off + i * T
                    S.dma("pool", "fl", lambda e: e.dma_start(
                        out=fsb[:, :, :], in_=F_d[:, :, t0_:t0_ + T].rearrange("g c t -> c g t")), writes=[fsbb])

                def xr_load1(j, t0_):
                    q_ = j % 2
                    r0_ = t0_ + j * 128
                    S.dma("pool", f"xr{q_}", lambda e: e.dma_start(out=xr[q_][:], in_=xsrc[r0_:r0_ + 128, :]),
                          writes=[xrb[q_]])

                feq = FeQ([fe_tasks1(i, xsrc) for i in range(NT)])
                feq.flush(0)
                load_F(0)
                for i in range(NT):
                    s = i % 2
                    t0 = off + i * T
                    tile_id = tb + i
                    hc = hT[s]

                    for g_ in range(4):
                        S.dma("pool", f"ic{g_}", lambda e, t0=t0, g_=g_: e.dma_start(
                            out=invc[g_][:, :], in_=invc_in[g_:g_ + 1, t0:t0 + T].broadcast_to([96, T])),
                            writes=[invcb[g_]])
                    slot = use(i, "WBv")
                    wv = ring[slot][:, 0:8 * 384].rearrange("p (k c) -> p k c", k=8)

                    def bv_a(j, slot=slot, wv=wv, hc=hc, s=s):
                        ps, pb = pp.get()
                        q = j % 2

                        def mmv(e):
                            ins = None
                            for k in range(8):
                                ins = e.matmul(ps[:, 0:384], lhsT=hc[:, k, 32 + j * 128:32 + (j + 1) * 128],
                                               rhs=wv[:, k, :], start=(k == 0), stop=(k == 7))
                            return ins

                        S.op("pe", mmv, reads=[hTb[s], ringb[slot]], writes=[pb])
                        S.op("act", lambda e: e.activation(out=vg[q][:], in_=ps[:, 0:384], func=AF.Gelu_apprx_tanh),
                             reads=[pb], writes=[vgb[q]])
                        S.op("dve", lambda e: e.bn_stats(out=bst[q][:], in_=vg[q][:]), reads=[vgb[q]], writes=[bstb[q]])
                        S.op("dve", lambda e: e.bn_aggr(out=bmv[q][:], in_=bst[q][:]), reads=[bstb[q]], writes=[bstb[q]])
                        S.op("dve", lambda e: e.tensor_scalar(out=bmv[q][:, 1:2], in0=bmv[q][:, 1:2], scalar1=EPS, scalar2=None,
                                                              op0=ALU.add), reads=[bstb[q]], writes=[bstb[q]])

                    def bv_b(j):
                        q = j % 2
                        S.op("pool", lambda e: e.tensor_tensor(out=bmv[q][:, 1:2], in0=bmv[q][:, 1:2], in1=fr.negh[:, :], op=ALU.pow),
                             reads=[bstb[q], fr.neghb], writes=[bstb[q]])
                        S.op("dve", lambda e: e.tensor_scalar(out=vn[:], in0=vg[q][:], scalar1=bmv[q][:, 0:1], scalar2=bmv[q][:, 1:2],
                                                              op0=ALU.subtract, op1=ALU.mult),
                             reads=[vgb[q], bstb[q]], writes=[vnb_])
                        S.op("pool", lambda e: e.tensor_tensor(out=vn[:], in0=vn[:], in1=Gb[:], op=ALU.mult),
                             reads=[vnb_, smallb], writes=[vnb_])
                        S.op("pool", lambda e: e.tensor_tensor(out=vnb[:, j, :], in0=vn[:], in1=Bb[:], op=ALU.add),
                             reads=[vnb_, smallb], writes=[vnbb[j]])

                    bv_a(0)
                    bv_a(1)
                    bv_b(0)
                    bv_a(2)
                    bv_b(1)
                    bv_a(3)
                    bv_b(2)
                    bv_b(3)

                    feq.step(i + 1)
                    rel(i, "WBv")
                    slot = use(i, "WD")
                    slotD = slot
                    wd = ring[slot][:, 0:8 * 384].rearrange("p (k c) -> p k c", k=8)
                    slotA = use(i, "WAa")
                    slotG = use(i, "WAg")
                    wa = ring[slotA][:, 0:8 * 384].rearrange("p (k c) -> p k c", k=8)
                    wg_ = ring[slotG][:, 0:8 * 384].rearrange("p (k c) -> p k c", k=8)

                    def d_group(g, wd=wd, hc=hc, s=s, t0=t0, slotD=slotD):
                        zd, zdb = zd2[g % 2], zdb2[g % 2]
                        ps, pb = pp.get()
                        psh, pbh = pp.get()

                        def mmd(e, ps=ps, psh=psh, g=g, wd=wd, hc=hc):
                            ins = None
                            for k in range(8):
                                ins = e.matmul(ps[0:96, :], lhsT=wd[:, k, g * 96:(g + 1) * 96], rhs=hc[:, k, 32:32 + T],
                                               start=(k == 0), stop=(k == 7))
                            for k in range(8):
                                ins = e.matmul(psh[0:96, 0:32], lhsT=wd[:, k, g * 96:(g + 1) * 96], rhs=hc[:, k, 0:32],
                                               start=(k == 0), stop=(k == 7))
                            return ins

                        S.op("pe", mmd, reads=[hTb[s], hThb[s], ringb[slotD]], writes=[pb, pbh])
                        S.op("act", lambda e, ps=ps, zd=zd: e.copy(out=zd[:, 16:16 + T], in_=ps[0:96, :]), reads=[pb], writes=[zdb])
                        S.op("act", lambda e, psh=psh, zd=zd: e.copy(out=zd[:, 0:16], in_=psh[0:96, 0:16]), reads=[pbh, zdb], writes=[zdb])
                        S.op("act", lambda e, psh=psh, zd=zd: e.copy(out=zd[:, 16 + T:32 + T], in_=psh[0:96, 16:32]),
                             reads=[pbh, zdb], writes=[zdb])
                        E = T + 32
                        S.op("dve", lambda e, zd=zd: e.tensor_tensor(out=sa[:, 1:E], in0=zd[:, 0:E - 1], in1=zd[:, 1:E], op=ALU.add),
                             reads=[zdb], writes=[sab])
                        cur, curb, oth, othb = sa, sab, sbb_, sbb
                        lo, hi = 1, E
                        sh = 1
                        for lev in range(g):
                            nlo, nhi = lo + sh, hi - sh
                            S.op("dve", lambda e, cur=cur, oth=oth, nlo=nlo, nhi=nhi, sh=sh: e.tensor_tensor(
                                out=oth[:, nlo:nhi], in0=cur[:, nlo - sh:nhi - sh], in1=cur[:, nlo + sh:nhi + sh], op=ALU.add),
                                reads=[curb], writes=[othb])
                            cur, curb, oth, othb = oth, othb, cur, curb
                            lo, hi = nlo, nhi
                            sh *= 2
                        S.op("dve", lambda e, cur=cur, oth=oth, g=g: e.tensor_tensor(
                            out=oth[:, 16:16 + T], in0=cur[:, 16:16 + T], in1=invc[g][:, :], op=ALU.mult),
                            reads=[curb, invcb[g]], writes=[othb])
                        S.op("dve", lambda e, oth=oth, g=g, zd=zd: e.tensor_tensor(
                            out=diff[:, g, :], in0=oth[:, 16:16 + T], in1=zd[:, 16:16 + T], op=ALU.subtract),
                            reads=[othb, zdb], writes=[diffb[g]])


                    def a_tile(m, wa=wa, wg_=wg_, hc=hc, s=s, slotA=slotA, slotG=slotG):
                        psa, pba = pp.get()
                        psg, pbg = pp.get()
                        psh, pbh = pp.get()

                        def mma(e, psa=psa, psg=psg, psh=psh, m=m, wa=wa, wg_=wg_, hc=hc):
                            ins = None
                            for k in range(8):
                                ins = e.matmul(psg[:, :], lhsT=wg_[:, k, m * 128:(m + 1) * 128], rhs=hc[:, k, 32:32 + T],
                                               start=(k == 0), stop=(k == 7))
                            for k in range(8):
                                ins = e.matmul(psa[:, :], lhsT=wa[:, k, m * 128:(m + 1) * 128], rhs=hc[:, k, 32:32 + T],
                                               start=(k == 0), stop=(k == 7))
                            for k in range(8):
                                ins = e.matmul(psh[:, 32:64], lhsT=wg_[:, k, m * 128:(m + 1) * 128], rhs=hc[:, k, 0:32],
                                               start=(k == 0), stop=(k == 7))
                            for k in range(8):
                                ins = e.matmul(psh[:, 0:32], lhsT=wa[:, k, m * 128:(m + 1) * 128], rhs=hc[:, k, 0:32],
                                               start=(k == 0), stop=(k == 7))
                            return ins

                        S.op("pe", mma, reads=[hTb[s], hThb[s], ringb[slotA], ringb[slotG]], writes=[pba, pbg, pbh])
                        q = cnt["sg"] % 2
                        cnt["sg"] += 1
                        S.op("act", lambda e, psg=psg, q=q: e.activation(out=sg[q][:], in_=psg[:, :], func=AF.Sigmoid),
                             reads=[pbg], writes=[sgb[q]])
                        S.op("act", lambda e, psh=psh: e.activation(out=sgh[:], in_=psh[:, 32:64], func=AF.Sigmoid),
                             reads=[pbh], writes=[sghb])
                        S.op("dve", lambda e, psa=psa, q=q, m=m: e.tensor_tensor(out=u[:, m, 16:16 + T], in0=psa[:, :], in1=sg[q][:],
                                                                               op=ALU.mult),
                             reads=[pba, sgb[q]], writes=[ub_[m]])
                        S.op("dve", lambda e, psh=psh, m=m: e.tensor_tensor(out=u[:, m, 0:16], in0=psh[:, 0:16], in1=sgh[:, 0:16],
                                                                          op=ALU.mult), reads=[pbh, sghb, ub_[m]], writes=[ub_[m]])
                        S.op("dve", lambda e, psh=psh, m=m: e.tensor_tensor(out=u[:, m, 16 + T:32 + T], in0=psh[:, 16:32],
                                                                          in1=sgh[:, 16:32], op=ALU.mult),
                             reads=[pbh, sghb, ub_[m]], writes=[ub_[m]])

                    d_group(0)
                    a_tile(0)
                    d_group(1)
                    feq.step(i + 1)
                    a_tile(1)
                    d_group(2)
                    a_tile(2)
                    d_group(3)
                    rel(i, "WD")
                    feq.step(i + 1)
                    rel(i, "WAa", "WAg")
                    for m in range(3):
                        psc, pbc = pp.get()

                        def mmc(e, psc=psc, m=m):
                            ins = None
                            for kk in range(16):
                                ins = e.matmul(psc[:, :], lhsT=dg[:, m * 16 + kk, :], rhs=u[:, m, 1 + kk:1 + kk + T],
                                               start=(kk == 0), stop=(kk == 15))
                            return ins

                        S.op("pe", mmc, reads=[ub_[m], dgb], writes=[pbc])
                        S.op("act", lambda e, psc=psc, m=m: e.activation(out=acc[:, m, :], in_=psc[:, :], func=AF.Identity,
                                                                       bias=PPt[:, PP_CB + m:PP_CB + m + 1]),
                             reads=[pbc, PPb], writes=[accb[m]])
                    drip = []
                    for kk in range(16, 31):
                        for m in range(3):
                            wcol = PPt[:, PP_CW + m * 31 + kk:PP_CW + m * 31 + kk + 1]
                            drip.append(lambda m=m, wcol=wcol, kk=kk: S.op("dve", lambda e: e.scalar_tensor_tensor(
                                out=acc[:, m, :], in0=u[:, m, 1 + kk:1 + kk + T], scalar=wcol, in1=acc[:, m, :],
                                op0=ALU.mult, op1=ALU.add), reads=[ub_[m], PPb, accb[m]], writes=[accb[m]]))

                    def ln_squares():
                        for m in range(3):
                            S.op("act", lambda e, m=m: e.activation(out=sqs[m][:], in_=acc[:, m, :], func=AF.Square),
                                 reads=[accb[m]], writes=[sqb[m]])

                    def ln_block():
                        ps1, pb1 = pp.get()
                        ps2, pb2 = pp.get()

                        def mmst(e, ps1=ps1, ps2=ps2):
                            ins = None
                            for m in range(3):
                                ins = e.matmul(ps1[:, :], lhsT=ones[:, :], rhs=acc[:, m, :], start=(m == 0), stop=(m == 2))
                            for m in range(3):
                                ins = e.matmul(ps2[:, :], lhsT=ones[:, :], rhs=sqs[m][:], start=(m == 0), stop=(m == 2))
                            return ins

                        S.op("pe", mmst, reads=accb + sqb + [onesb], writes=[pb1, pb2])
                        S.op("act", lambda e, ps1=ps1: e.activation(out=lnm2, in_=ps1[:, :], func=AF.Square),
                             reads=[pb1], writes=[lnm2b])
                        S.op("act", lambda e, ps1=ps1: e.copy(out=tmp[0][:], in_=ps1[:, :]), reads=[pb1], writes=[tmpb[0]])
                        S.op("dve", lambda e, ps2=ps2: e.scalar_tensor_tensor(out=lnrs, in0=ps2[:, :], scalar=EPS, in1=lnm2,
                                                                           op0=ALU.add, op1=ALU.subtract),
                             reads=[pb2, lnm2b], writes=[lnrsb])
                        S.op("act", lambda e: e.activation(out=lnrs, in_=lnrs, func=AF.Sqrt), reads=[lnrsb], writes=[lnrsb])
                        S.op("dve", lambda e: e.reciprocal(out=lnrs, in_=lnrs), reads=[lnrsb], writes=[lnrsb])
                        for m in range(3):
                            q = cnt["lnt"] % 2
                            cnt["lnt"] += 1
                            S.op("dve", lambda e, m=m, q=q: e.tensor_tensor(out=lnt[q][:], in0=acc[:, m, :], in1=tmp[0][:],
                                                                         op=ALU.subtract),
                                 reads=[accb[m], tmpb[0]], writes=[lntb[q]])
                            S.op("dve", lambda e, q=q: e.tensor_tensor(out=lnt[q][:], in0=lnt[q][:], in1=lnrs, op=ALU.mult),
                                 reads=[lntb[q], lnrsb], writes=[lntb[q]])
                            S.op("act", lambda e, m=m, q=q: e.activation(out=cbf[:, m, :], in_=lnt[q][:], func=AF.Silu,
                                                                       scale=PPt[:, PP_LG + m:PP_LG + m + 1],
                                                                       bias=PPt[:, PP_LB + m:PP_LB + m + 1]),
                                 reads=[lntb[q], PPb], writes=[cbfb[m]])


                    slot = use(i, "WBu")
                    wu = ring[slot][:, 0:8 * 384].rearrange("p (k c) -> p k c", k=8)
                    for hd in range(4):
                        ps, pb = pp.get()

                        def mmu(e, ps=ps, hd=hd, wu=wu, hc=hc):
                            ins = None
                            for k in range(8):
                                ins = e.matmul(ps[0:96, :], lhsT=wu[:, k, hd * 96:(hd + 1) * 96], rhs=hc[:, k, 32:32 + T],
                                               start=(k == 0), stop=(k == 7))
                            return ins

                        S.op("pe", mmu, reads=[hTb[s], ringb[slot]], writes=[pb])
                        S.op("act", lambda e, ps=ps, hd=hd: e.activation(out=ug[:, hd, :], in_=ps[0:96, :],
                                                                       func=AF.Gelu_apprx_tanh),
                             reads=[pb], writes=[ugb[hd]])
                        for _ in range(3):
                            if drip:
                                drip.pop(0)()
                    feq.step(i + 1)
                    rel(i, "WBu")
                    for hd in range(4):
                        ps, pb = pp.get()

                        def mms(e, ps=ps, hd=hd):
                            ins = None
                            for j in range(4):
                                ins = e.matmul(ps[0:96, j * 128:(j + 1) * 128], lhsT=vnb[:, j, hd * 96:(hd + 1) * 96],
                                               rhs=WsT[:, hd, :], start=True, stop=False)
                                ins = e.matmul(ps[0:96, j * 128:(j + 1) * 128], lhsT=one1[0:1, :],
                                               rhs=bsr[0:1, hd * 128:(hd + 1) * 128], start=False, stop=True)
                            return ins

                        S.op("pe", mms, reads=vnbb + [smallb, one1b], writes=[pb])
                        S.op("dve", lambda e, ps=ps, hd=hd: e.tensor_tensor(out=usv[:, hd, :], in0=ps[0:96, :], in1=ug[:, hd, :],
                                                                          op=ALU.mult),
                             reads=[pb, ugb[hd]], writes=[usvb[hd]])
                        for _ in range(3):
                            if drip:
                                drip.pop(0)()
                    for g in range(4):
                        ps, pb = pp.get()
                        S.op("pe", lambda e, ps=ps, g=g: e.matmul(ps[0:96, :], lhsT=dwg[:, g, :], rhs=diff[:, g, :],
                                                                   start=True, stop=True),
                             reads=[diffb[g], smallb], writes=[pb])
                        S.op("act", lambda e, ps=ps, g=g: e.activation(out=yd[:, g, :], in_=ps[0:96, :], func=AF.Copy,
                                                                     scale=PPt[0:96, PP_DS + g:PP_DS + g + 1]),
                             reads=[pb, PPb], writes=[ydb[g]])

                    first = True
                    for br, nm, kk, kp, srcs_ in ((2, "c", 4, 128, None), (3, "d", 4, 96, None), (1, "b", 4, 96, None),
                                                  (0, "a", 3, 128, None)):
                        feq.step(i + 1)
                        if br == 0:
                            xr_load1(0, t0)
                            xr_load1(1, t0)
                        sl0 = use(i, f"Wg{br}0")
                        sl1 = use(i, f"Wg{br}1")
                        slp = use(i, f"Pj{br}")
                        wgl = [ring[sl0][:, 0:4096].rearrange("p (k c) -> p k c", k=8),
                               ring[sl1][:, 0:4096].rearrange("p (k c) -> p k c", k=8)]
                        wpj = ring[slp][0:kp, 0:kk * 1024].rearrange("p (k c) -> p k c", k=kk)
                        ksz = [kp] * kk
                        if br == 2:
                            ksz = [128, 68, 128, 68]
                            rhs_of = lambda k_: fsb[0:(128 if k_ % 2 == 0 else 68), k_, :]
                            rbufs = [fsbb]
                        elif br == 3:
                            rhs_of = lambda k_: yd[:, k_, :]
                            rbufs = ydb
                        elif br == 1:
                            rhs_of = lambda k_: usv[:, k_, :]
                            rbufs = usvb
                        else:
                            rhs_of = lambda k_: cbf[:, k_, :]
                            rbufs = cbfb
                        for dt in range(8):
                            psg, pbg = pp.get()
                            psp, pbp = pp.get()
                            wgh = wgl[dt // 4]
                            slg = sl0 if dt < 4 else sl1

                            def mmg(e, psg=psg, psp=psp, dt=dt, wgh=wgh, wpj=wpj, rhs_of=rhs_of, kk=kk, hc=hc, ksz=ksz):
                                ins = None
                                for k in range(8):
                                    ins = e.matmul(psg[:, :], lhsT=wgh[:, k, (dt % 4) * 128:(dt % 4 + 1) * 128],
                                                   rhs=hc[:, k, 32:32 + T], start=(k == 0), stop=(k == 7))
                                for k in range(kk):
                                    ins = e.matmul(psp[:, :], lhsT=wpj[0:ksz[k], k, dt * 128:(dt + 1) * 128], rhs=rhs_of(k),
                                                   start=(k == 0), stop=(k == kk - 1))
                                return ins

                            S.op("pe", mmg, reads=[hTb[s], ringb[slg], ringb[slp]] + list(rbufs), writes=[pbg, pbp])
                            q = cnt["gs"] % 2
                            cnt["gs"] += 1
                            S.op("act", lambda e, psg=psg, q=q, br=br, dt=dt: e.activation(
                                out=gs[q][:], in_=psg[:, :], func=AF.Sigmoid,
                                bias=PPt[:, PP_BG + br * 8 + dt:PP_BG + br * 8 + dt + 1]),
                                reads=[pbg, PPb], writes=[gsb[q]])
                            if first:
                                S.op("dve", lambda e, psp=psp, q=q, dt=dt: e.tensor_tensor(out=mg[:, dt, :], in0=psp[:, :], in1=gs[q][:],
                                                                                          op=ALU.mult),
                                     reads=[pbp, gsb[q]], writes=[mgb[dt]])
                            else:
                                q2 = cnt["tmp"] % 2
                                cnt["tmp"] += 1
                                S.op("dve", lambda e, psp=psp, q=q, q2=q2: e.tensor_tensor(out=tmp[q2][:], in0=psp[:, :], in1=gs[q][:],
                                                                                        op=ALU.mult),
                                     reads=[pbp, gsb[q]], writes=[tmpb[q2]])
                                if br != 0:
                                    S.op("pool", lambda e, q2=q2, dt=dt: e.tensor_tensor(out=mg[:, dt, :], in0=mg[:, dt, :], in1=tmp[q2][:],
                                                                                        op=ALU.add),
                                         reads=[tmpb[q2], mgb[dt]], writes=[mgb[dt]])
                                else:
                                    S.op("pool", lambda e, q2=q2, dt=dt: e.tensor_tensor(out=mb[:, dt, :], in0=mg[:, dt, :], in1=tmp[q2][:],
                                                                                        op=ALU.add),
                                         reads=[tmpb[q2], mgb[dt]], writes=[mbb[dt]])
                            if br in (2, 3):
                                for _ in range(2 if br == 2 else 1):
                                    if drip:
                                        drip.pop(0)()
                            if br == 1 and dt == 1:
                                ln_block()
                        if br == 2:
                            load_F(i + 1)
                        rel(i, f"Wg{br}0", f"Wg{br}1", f"Pj{br}")
                        if br == 3:
                            while drip:
                                drip.pop(0)()
                            ln_squares()
                        first = False

                    slo = [use(i, "Wo0"), use(i, "Wo1")]
                    wol = [ring[slo[0]][:, 0:4096].rearrange("p (k c) -> p k c", k=8),
                           ring[slo[1]][:, 0:4096].rearrange("p (k c) -> p k c", k=8)]
                    for j in range(4):
                        q = j % 2
                        r0 = t0 + j * 128
                        for hf in range(2):
                            ps, pb = pp.get()

                            def mmo(e, ps=ps, j=j, hf=hf, wol=wol):
                                ins = None
                                for k in range(8):
                                    ins = e.matmul(ps[:, :], lhsT=mb[:, k, j * 128:(j + 1) * 128], rhs=wol[hf][:, k, :],
                                                   start=(k == 0), stop=(k == 7))
                                return ins

                            S.op("pe", mmo, reads=mbb + [ringb[slo[hf]]], writes=[pb])
                            S.op("dve", lambda e, ps=ps, q=q, hf=hf: e.tensor_tensor(
                                out=xr[q][:, hf * 512:(hf + 1) * 512], in0=ps[:, :], in1=xr[q][:, hf * 512:(hf + 1) * 512],
                                op=ALU.add), reads=[pb, xrb[q]], writes=[xrb[q]])
                        S.dma("pool", f"xst{q}", lambda e, q=q, r0=r0: e.dma_start(out=x1_d[r0:r0 + 128, :], in_=xr[q][:]),
                              reads=[xrb[q]])
                        if j + 2 < 4:
                            xr_load1(j + 2, t0)
                    rel(i, "Wo0", "Wo1")
                    feq.flush(i + 1)
                S.emit()

        def pass2(l, seq):
            uid[0] += 1
            U = f"u{uid[0]}"
            off, SL, dn, tb = seq
            NT = SL // T
            last = (l == L - 1)
            xdst = y_out if last else x2_d
            with ExitStack() as es:
                sbt = lambda n, s, d: es.enter_context(nc.sbuf_tensor(f"{U}p2_{n}", s, d))
                pp = Ps(es, nc, 8, U + "p2")
                fr = Front(es, U + "p2", g2_in[l], pp)
                HW_ = 32 + T
                hT = [sbt(f"hT{i}", [128, 8, HW_], BF) for i in range(2)]
                hTb = [Buf("hT") for _ in range(2)]
                hThb = [Buf("hTh") for _ in range(2)]
                NSL = 4
                ring = [sbt(f"ring{i}", [128, 8, 2, 256], BF) for i in range(NSL)]
                ringb = [Buf("ring") for _ in range(NSL)]
                PPt = sbt("pp", [128, NPP], F32)
                PPb = Buf("pp")
                FD = sbt("FD", [128, NMT, D], BF)
                FDb = Buf("FD")
                gfb = sbt("gfb", [128, D], F32)
                gfbb = Buf("gfb")
                act_bf = [sbt(f"actbf{i}", [128, NMT, T], BF) for i in range(2)]
                actb = [[Buf("act") for _ in range(NMT)] for _ in range(2)]
                cg = [sbt(f"cg{i}", [128, T], F32) for i in range(3)]
                cv = [sbt(f"cv{i}", [128, T], F32) for i in range(3)]
                cgb = [Buf("cg") for _ in range(3)]
                cvb = [Buf("cv") for _ in range(3)]
                xr = [sbt(f"xr{i}", [128, D], F32) for i in range(2)]
                xrb = [Buf("xr") for _ in range(2)]
                fss = sbt("fss", [128, 1], F32)
                fssb = Buf("fss")
                fjunk = fr.junk
                fjunkb = fr.junkb
                cnt = {"xr": 0, "ub": 0}
                S.dma("sp", "par", lambda e: e.dma_start(out=PPt[:], in_=pp_in[l]), writes=[PPb])
                S.dma("sp", "par", lambda e: e.dma_start(out=gfb[:], in_=gf_in.broadcast_to([128, D])), writes=[gfbb])
                for c3 in range(0, NMT, 6):
                    n3 = min(6, NMT - c3)
                    S.dma("sp", "par", lambda e, c3=c3, n3=n3: e.dma_start(
                        out=FD[:, c3:c3 + n3, :],
                        in_=WB["f_down"][l][c3 * 128:(c3 + n3) * 128, :].rearrange("(k p) c -> p k c", p=128)),
                        writes=[FDb])
                FDb.w = ("par", S.dcnt["par"])
                PPb.w = FDb.w
                gfbb.w = FDb.w
                NPC = NMT // 2
                allp = [(i, r) for i in range(NT) for r in range(NPC)]
                pstate = {"next": 0}
                wu_ = WB["f_up"][l]

                pdone = set()

                def prefetch():
                    while pstate["next"] < len(allp) and (pstate["next"] < NSL or (pstate["next"] - NSL) in pdone):
                        idx = pstate["next"]
                        slot = idx % NSL
                        i_, r = allp[idx]
                        parts = []
                        for gv in range(2):
                            c0 = gv * DFF + r * 256
                            parts.append((ring[slot][:, :, gv, :], wu_[:, c0:c0 + 256].rearrange("(k p) c -> p k c", p=128)))
                        load_piece(ring, ringb, slot, parts, f"rg{slot}")
                        pstate["next"] += 1

                def fe_tasks2(i):
                    s = i % 2
                    t0 = off + i * T
                    tile_id = tb + i
                    srcs = []
                    if i > 0:
                        srcs.append((0, 16, x1_d[t0 - 16:t0, :]))
                    if i < NT - 1:
                        srcs.append((16, 16, x1_d[t0 + T:t0 + T + 16, :]))
                    chunks = [fr.new(srcs, hT[s][:, :, 0:32], hThb[s], npart=32, zero=(len(srcs) < 2), mask_col=tile_id)]
                    for j in range(4):
                        chunks.append(fr.new([(0, 128, x1_d[t0 + j * 128:t0 + (j + 1) * 128, :])],
                                             hT[s][:, :, 32 + j * 128:32 + (j + 1) * 128], hTb[s]))
                    return fr.tasks(chunks)

                feq = FeQ([fe_tasks2(i) for i in range(NT)])
                feq.flush(0)
                pist = {"pi": 0, "pend": None}

                def up_piece(i, r):
                    if True:
                        s = i % 2
                        hc = hT[s]
                        ab = act_bf[i % 2]
                        abb = actb[i % 2]
                        prefetch()
                        slot = pist["pi"] % NSL
                        pist["pi"] += 1
                        feq.step(i + 1)
                        for mm_ in range(2):
                            mt = 2 * r + mm_
                            psg, pbg = pp.get()
                            psv, pbv = pp.get()
                            psh, pbh = pp.get()

                            def mmup(e, psg=psg, psv=psv, psh=psh, mm_=mm_, slot=slot, hc=hc):
                                ins = None
                                for gv, pso in ((0, psg), (1, psv)):
                                    for k in range(8):
                                        ins = e.matmul(pso[:, :], lhsT=ring[slot][:, k, gv, mm_ * 128:(mm_ + 1) * 128],
                                                       rhs=hc[:, k, 32:32 + T], start=(k == 0), stop=(k == 7))
                                for gv in range(2):
                                    for k in range(8):
                                        ins = e.matmul(psh[:, gv * 32:gv * 32 + 32],
                                                       lhsT=ring[slot][:, k, gv, mm_ * 128:(mm_ + 1) * 128],
                                                       rhs=hc[:, k, 0:32], start=(k == 0), stop=(k == 7))
                                return ins

                            S.op("pe", mmup, reads=[hTb[s], hThb[s], ringb[slot]], writes=[pbg, pbv, pbh])
                            q = cnt["ub"] % 3
                            cnt["ub"] += 1
                            for gv, pso, pbo, cc, ccb in ((0, psg, pbg, cg[q], cgb[q]), (1, psv, pbv, cv[q], cvb[q])):
                                ch_ = gv * NMT + mt
                                w0 = PPt[:, PP_FW + ch_ * 3 + 0:PP_FW + ch_ * 3 + 1]
                                w1 = PPt[:, PP_FW + ch_ * 3 + 1:PP_FW + ch_ * 3 + 2]
                                w2 = PPt[:, PP_FW + ch_ * 3 + 2:PP_FW + ch_ * 3 + 3]
                                fb = PPt[:, PP_FB + ch_:PP_FB + ch_ + 1]
                                S.op("act", lambda e, pso=pso, cc=cc, w1=w1, fb=fb: e.activation(
                                    out=cc[:], in_=pso[:, :], func=AF.Identity, scale=w1, bias=fb),
                                    reads=[pbo, PPb], writes=[ccb])
                                S.op("act", lambda e, psh=psh, cc=cc, w0=w0, gv=gv: e.activation(
                                    out=cc[:, 0:1], in_=psh[:, gv * 32 + 15:gv * 32 + 16], func=AF.Identity,
                                    scale=w0, bias=cc[:, 0:1]), reads=[pbh, PPb, ccb], writes=[ccb])
                                S.op("act", lambda e, psh=psh, cc=cc, w2=w2, gv=gv: e.activation(
                                    out=cc[:, T - 1:T], in_=psh[:, gv * 32 + 16:gv * 32 + 17], func=AF.Identity,
                                    scale=w2, bias=cc[:, T - 1:T]), reads=[pbh, PPb, ccb], writes=[ccb])
                                S.op("dve", lambda e, pso=pso, cc=cc, w0=w0: e.scalar_tensor_tensor(
                                    out=cc[:, 1:T], in0=pso[:, 0:T - 1], scalar=w0, in1=cc[:, 1:T], op0=ALU.mult, op1=ALU.add),
                                    reads=[pbo, PPb, ccb], writes=[ccb])
                                S.op("dve", lambda e, pso=pso, cc=cc, w2=w2: e.scalar_tensor_tensor(
                                    out=cc[:, 0:T - 1], in0=pso[:, 1:T], scalar=w2, in1=cc[:, 0:T - 1], op0=ALU.mult, op1=ALU.add),
                                    reads=[pbo, PPb, ccb], writes=[ccb])

                            def fin(q=q, mt=mt, ab=ab, abb=abb):
                                S.op("act", lambda e: e.activation(out=cg[q][:], in_=cg[q][:], func=AF.Gelu_apprx_tanh),
                                     reads=[cgb[q]], writes=[cgb[q]])
                                S.op("pool", lambda e: e.tensor_tensor(out=ab[:, mt, :], in0=cg[q][:], in1=cv[q][:],
                                                                       op=ALU.mult),
                                     reads=[cgb[q], cvb[q]], writes=[abb[mt]])

                            if pist["pend"] is not None:
                                pist["pend"]()
                            pist["pend"] = fin
                        pdone.add(pist["pi"] - 1)
                def down_tile(i):
                    t0 = off + i * T
                    ab = act_bf[i % 2]
                    abb = actb[i % 2]
                    def xr_load2(j):
                        q_ = j % 2
                        r0_ = t0 + j * 128
                        S.dma("pool", f"xr{q_}", lambda e: e.dma_start(out=xr[q_][:], in_=x1_d[r0_:r0_ + 128, :]),
                              writes=[xrb[q_]])

                    xr_load2(0)
                    xr_load2(1)
                    for j in range(4):
                        q = j % 2
                        r0 = t0 + j * 128
                        for hf in range(2):
                            ps, pb = pp.get()

                            def mmdn(e, ps=ps, j=j, hf=hf, ab=ab):
                                ins = None
                                for mt in range(NMT):
                                    ins = e.matmul(ps[:, :], lhsT=ab[:, mt, j * 128:(j + 1) * 128],
                                                   rhs=FD[:, mt, hf * 512:(hf + 1) * 512], start=(mt == 0), stop=(mt == NMT - 1))
                                return ins

                            S.op("pe", mmdn, reads=abb + [FDb], writes=[pb])
                            S.op("dve", lambda e, ps=ps, q=q, hf=hf: e.tensor_tensor(
                                out=xr[q][:, hf * 512:(hf + 1) * 512], in0=ps[:, :], in1=xr[q][:, hf * 512:(hf + 1) * 512],
                                op=ALU.add), reads=[pb, xrb[q]], writes=[xrb[q]])
                        if last:
                            S.op("act", lambda e, q=q: e.activation(out=fjunk[:, :], in_=xr[q][:], func=AF.Square, accum_out=fss[:]),
                                 reads=[xrb[q]], writes=[fjunkb, fssb])
                            S.op("dve", lambda e: e.tensor_scalar(out=fss[:], in0=fss[:], scalar1=1.0 / D, scalar2=EPS,
                                                                  op0=ALU.mult, op1=ALU.add), reads=[fssb], writes=[fssb])
                            S.op("pool", lambda e: e.tensor_tensor(out=fss[:], in0=fss[:], in1=fr.negh[:, :], op=ALU.pow),
                                 reads=[fssb, fr.neghb], writes=[fssb])
                            S.op("dve", lambda e, q=q: e.scalar_tensor_tensor(out=xr[q][:], in0=xr[q][:], scalar=fss[:, 0:1], in1=gfb[:],
                                                                           op0=ALU.mult, op1=ALU.mult),
                                 reads=[xrb[q], fssb, gfbb], writes=[xrb[q]])
                        S.dma("pool", f"xst{q}", lambda e, q=q, r0=r0: e.dma_start(out=xdst[r0:r0 + 128, :], in_=xr[q][:]),
                              reads=[xrb[q]])
                        if j + 2 < 4:
                            xr_load2(j + 2)

                R0 = 2
                for i in range(NT):
                    for r in range(R0 if i > 0 else 0, NPC):
                        up_piece(i, r)
                    feq.flush(i + 1)
                    if i + 1 < NT:
                        for r in range(R0):
                            up_piece(i + 1, r)
                    else:
                        pist["pend"]()
                        pist["pend"] = None
                    down_tile(i)
                S.emit()

        stages = cfg.get("stages", ("p0", "p1", "p2"))
        for l in range(L):
            for seq in SEQS:
                if "p0" in stages:
                    pass0(l, seq)
                if "p1" in stages:
                    pass1(l, seq)
                if "p2" in stages:
                    pass2(l, seq)
        if dbg:
            for nm, ap in dbg_out.items():
                src = {"x1": x1_d, "x2": x2_d}[nm]
                S.dma("pool", "dbg", lambda e, ap=ap, src=src: e.dma_start(out=ap, in_=src))
            S.emit()
    return nc


def _bf(a):
    return np.ascontiguousarray(a.astype(ml_dtypes.bfloat16))


def dft_tables(Stot, Ssub):
    out = np.zeros((2, Stot, Stot), dtype=ml_dtypes.bfloat16)
    k = np.arange(Ssub, dtype=np.int64)
    ang = 2.0 * np.pi * np.arange(Ssub, dtype=np.float64) / Ssub
    ct = (np.cos(ang) / np.sqrt(Ssub)).astype(np.float32)
    st = (-np.sin(ang) / np.sqrt(Ssub)).astype(np.float32)
    idx = (np.outer(k, k) % Ssub)
    cb = ct[idx].astype(ml_dtypes.bfloat16)
    sb_ = st[idx].astype(ml_dtypes.bfloat16)
    for b in range(Stot // Ssub):
        out[0, b * Ssub:(b + 1) * Ssub, b * Ssub:(b + 1) * Ssub] = cb
        out[1, b * Ssub:(b + 1) * Ssub, b * Ssub:(b + 1) * Ssub] = sb_
    return out


def invcnt_table(seq_lens):
    cols = []
    for S_ in seq_lens:
        t = np.arange(S_)
        rows = []
        for w in POOLW:
            lo, hi = -(w // 2), w // 2 - 1
            start = np.clip(t + lo, 0, S_)
            end = np.clip(t + hi + 1, 0, S_)
            rows.append(1.0 / (end - start).astype(np.float32))
        cols.append(np.stack(rows, 0))
    return np.ascontiguousarray(np.concatenate(cols, axis=1).astype(np.float32))


def host_weights(inp, L):
    m = {}
    for k in ("w_in", "w_gate", "a_proj", "b_proj", "d_proj", "w_out", "f_up", "f_down"):
        m[k] = np.ascontiguousarray(inp[k], dtype=np.float32)
    cp = np.asarray(inp["c_proj"], dtype=np.float32)
    cpk = np.zeros((L, 2, 256, D), np.float32)
    for g in range(4):
        for c in range(49):
            idx = g * 49 + c
            cpk[:, 0, idx] = cp[:, g * 96 + c]
            if 1 <= c <= 47:
                cpk[:, 1, idx] = cp[:, g * 96 + 96 - c]
    m["cpk"] = np.ascontiguousarray(cpk.reshape(L, 512, D))
    m["wst"] = np.ascontiguousarray(np.transpose(inp["b_ws"], (0, 3, 1, 2)).reshape(L, 128, 512), dtype=np.float32)
    m["dwg"] = np.ascontiguousarray(np.transpose(inp["d_wg"], (0, 2, 1, 3)).reshape(L, 96, 384), dtype=np.float32)
    m["bsr"] = np.ascontiguousarray(inp["b_bs"].reshape(L, 1, 512), dtype=np.float32)
    pp = np.zeros((L, 128, NPP), np.float32)
    cw = np.transpose(inp["a_conv_w"], (0, 2, 1)).reshape(L, 3, 128, 31)
    pp[:, :, PP_CW:PP_CW + 93] = np.transpose(cw, (0, 2, 1, 3)).reshape(L, 128, 93)
    pp[:, :, PP_CB:PP_CB + 3] = np.transpose(inp["a_conv_b"].reshape(L, 3, 128), (0, 2, 1))
    pp[:, :, PP_LG:PP_LG + 3] = np.transpose(inp["a_ln_g"].reshape(L, 3, 128), (0, 2, 1))
    pp[:, :, PP_LB:PP_LB + 3] = np.transpose(inp["a_ln_b"].reshape(L, 3, 128), (0, 2, 1))
    pp[:, :, PP_BG:PP_BG + 32] = np.transpose(inp["b_gate"].reshape(L, 4, 8, 128), (0, 3, 1, 2)).reshape(L, 128, 32)
    pp[:, 0:96, PP_DS:PP_DS + 4] = np.transpose(inp["d_scale"].reshape(L, 4, 96), (0, 2, 1))
    fw = np.transpose(inp["f_conv_w"], (0, 2, 1)).reshape(L, 44, 128, 3)
    pp[:, :, PP_FW:PP_FW + 132] = np.transpose(fw, (0, 2, 1, 3)).reshape(L, 128, 132)
    pp[:, :, PP_FB:PP_FB + 44] = np.transpose(inp["f_conv_b"].reshape(L, 44, 128), (0, 2, 1))
    m["pp"] = pp
    m["g1"] = np.ascontiguousarray(inp["norm1_g"].reshape(L, 1, D), dtype=np.float32)
    m["g2"] = np.ascontiguousarray(inp["norm2_g"].reshape(L, 1, D), dtype=np.float32)
    m["gf"] = np.ascontiguousarray(inp["final_g"].reshape(1, D), dtype=np.float32)
    m["blg"] = np.ascontiguousarray(inp["b_ln_g"].reshape(L, 1, 384), dtype=np.float32)
    m["blb"] = np.ascontiguousarray(inp["b_ln_b"].reshape(L, 1, 384), dtype=np.float32)
    c = np.arange(96)
    ang = 2.0 * np.pi * np.outer(c, c) / 96.0
    cs96 = np.concatenate([np.cos(ang)[:, 0:49], np.sin(ang)[:, 0:49]], axis=1) / np.sqrt(96.0)
    m["cs96"] = _bf(cs96.astype(np.float32))
    m["ident"] = _bf(np.eye(128, dtype=np.float32))
    m["ones"] = np.full((128, 128), 1.0 / 384.0, np.float32)
    return m


_CACHE = {}


def kernel(**inputs):
    L = 2
    xp = np.asarray(inputs["x_prompt"], dtype=np.float32)
    xs = np.asarray(inputs["x_sample"], dtype=np.float32)
    SA, SB = 8192, 2048
    NTOK = SA + 2 * SB
    seqs = [(0, SA, "A", 0), (SA, SB, "B", SA // T), (SA + SB, SB, "B", SA // T + SB // T)]
    NTT = NTOK // T
    cfg = {"seqs": seqs, "ntok": NTOK, "L": L, "ntiles": NTT}
    if "nc" not in _CACHE:
        _CACHE["nc"] = build(cfg)
    nc = _CACHE["nc"]
    base = host_weights(inputs, L)
    dftA_full = dft_tables(SA, SA)
    dftA_blk = dft_tables(SA, SB)
    dftB = dft_tables(SB, SB)
    in_maps = []
    assign = []
    for c in range(8):
        if c < 4:
            xa = xs[c]
            pids = [2 * c, 2 * c + 1]
            hm = np.ones((32, NTT), np.float32)
            invc = invcnt_table([SA, SB, SB])
            dA = dftA_full
            apid = None
        else:
            p0 = 8 + 6 * (c - 4)
            apid = [p0, p0 + 1, p0 + 2, p0 + 3]
            xa = xp[apid].reshape(SA, D)
            pids = [p0 + 4, p0 + 5]
            hm = np.ones((32, NTT), np.float32)
            for ti in range(SA // T):
                t0 = ti * T
                if t0 % SB == 0:
                    hm[0:16, ti] = 0.0
                if (t0 + T) % SB == 0:
                    hm[16:32, ti] = 0.0
            invc = invcnt_table([SB] * 6)
            dA = dftA_blk
        xcat = np.ascontiguousarray(np.concatenate([xa, xp[pids[0]], xp[pids[1]]], axis=0))
        m = dict(base)
        m["x"] = xcat
        m["hmask"] = hm
        m["invc"] = invc
        m["dftA"] = dA
        m["dftB"] = dftB
        in_maps.append(m)
        assign.append((c, apid, pids))
    res = run_bass_kernel_spmd(nc, in_maps, core_ids=list(range(8)))
    yp = np.empty_like(xp)
    ys = np.empty_like(xs)
    for (c, apid, pids), r in zip(assign, res.results):
        yv = np.asarray(r["y"], dtype=np.float32)
        if apid is None:
            ys[c] = yv[0:SA]
        else:
            yp[apid] = yv[0:SA].reshape(4, SB, D)
        yp[pids[0]] = yv[SA:SA + SB]
        yp[pids[1]] = yv[SA + SB:SA + 2 * SB]
    return (yp, ys)
```
